# Optimizing a Trainium2 kernel written in Bass

```python
import jax, jax.numpy as jnp
from jax import lax
import numpy as np

D_MODEL = 1024
BATCH = 2
SEQ = 16384
DEPTH = 4

D_CONV = 256
CONV_KERNEL = 31
NSA_HEADS = 8
NSA_KV_HEADS = 2
HEAD_DIM = 64
GQ = NSA_HEADS // NSA_KV_HEADS
D_NSA = NSA_HEADS * HEAD_DIM
RWKV_HEADS = 4
RWKV_HEAD_DIM = 64
D_RWKV = RWKV_HEADS * RWKV_HEAD_DIM
D_MIX = D_CONV + D_NSA + D_RWKV

CMP_BLOCK = 32
CMP_STRIDE = 16
SEL_BLOCK = 64
TOP_N = 16
WINDOW = 512
Q_BLOCK = 128
FORCE_BONUS = 1.0e4

DECAY_LORA = 32
AAA_LORA = 32
GATE_LORA = 64
RWKV_GN_EPS = 64e-5

D_FF = 2816
FFN_CONV = 3

RMS_EPS = 1e-6
LN_EPS = 1e-5
NEG_INF = -1e30
TINY = 1e-30

CONV_COLS = 2 * D_CONV
KV_COLS = NSA_KV_HEADS * HEAD_DIM
NSA_COLS = D_NSA + 6 * KV_COLS + 3 * NSA_HEADS
RWKV_COLS = 3 * D_RWKV + DECAY_LORA + AAA_LORA + GATE_LORA
IN_COLS = CONV_COLS + NSA_COLS + RWKV_COLS

kernel_name = "hybrid_conv_nsa_rwkv7_trunk"


def _offsets(sizes):
    out, acc = [], 0
    for s in sizes[:-1]:
        acc += s
        out.append(acc)
    return out


def rms_norm(x, g):
    xf = x.astype(jnp.float32)
    y = xf * lax.rsqrt(jnp.mean(xf * xf, axis=-1, keepdims=True) + RMS_EPS)
    return (y * g.astype(jnp.float32)).astype(x.dtype)


def layer_norm(x, g, b, eps):
    xf = x.astype(jnp.float32)
    mu = jnp.mean(xf, axis=-1, keepdims=True)
    var = jnp.mean(jnp.square(xf - mu), axis=-1, keepdims=True)
    y = (xf - mu) * lax.rsqrt(var + eps)
    return (y * g.astype(jnp.float32) + b.astype(jnp.float32)).astype(x.dtype)


def causal_dwconv(x, w, b):
    k = w.shape[0]
    y = lax.conv_general_dilated(
        x, w[:, None, :].astype(x.dtype), window_strides=(1,),
        padding=[(k - 1, 0)], dimension_numbers=('NWC', 'WIO', 'NWC'),
        feature_group_count=x.shape[-1])
    return y + b


def token_shift(p, mu):
    prev = jnp.pad(p, ((0, 0), (1, 0), (0, 0)))[:, :-1]
    return p + (prev - p) * mu


def masked_softmax(s, mask, axes):
    s = jnp.where(mask, s, NEG_INF)
    m = jnp.max(s, axis=axes, keepdims=True)
    e = jnp.where(mask, jnp.exp(s - m), 0.0)
    return e / jnp.maximum(jnp.sum(e, axis=axes, keepdims=True), TINY)


def conv_mixer(p, dw_w, dw_b, ln_g, ln_b):
    u, gate = jnp.split(p, 2, axis=-1)
    y = u * jax.nn.sigmoid(gate)
    y = causal_dwconv(y, dw_w, dw_b)
    y = layer_norm(y, ln_g, ln_b, LN_EPS)
    return jax.nn.silu(y)


def compress_blocks(kv, pe, w1, w2):
    b, t, g, d = kv.shape
    per = CMP_BLOCK // CMP_STRIDE
    nch = t // CMP_STRIDE
    nc = nch - per + 1
    ch = kv.reshape(b, nch, CMP_STRIDE, g, d)
    blocks = jnp.concatenate([ch[:, o:o + nc] for o in range(per)], axis=2)
    blocks = blocks + pe[:, None, :]
    flat = jnp.swapaxes(blocks, 2, 3).reshape(b, nc, g, CMP_BLOCK * d)
    return jax.nn.gelu(flat @ w1) @ w2


def nsa_mixer(p, pe_k, pe_v, w1_k, w2_k, w1_v, w2_v):
    b, t, _ = p.shape
    f32 = jnp.float32
    sizes = [D_NSA] + [KV_COLS] * 6 + [3 * NSA_HEADS]
    q, kc, vc, ks, vs, kw, vw, gt = jnp.split(p, _offsets(sizes), axis=-1)
    kv_shape = (b, t, NSA_KV_HEADS, HEAD_DIM)
    q = q.reshape(b, t, NSA_KV_HEADS, GQ, HEAD_DIM)
    k_cmp = compress_blocks(kc.reshape(kv_shape), pe_k, w1_k, w2_k)
    v_cmp = compress_blocks(vc.reshape(kv_shape), pe_v, w1_v, w2_v)
    nc = k_cmp.shape[1]
    ns = t // SEL_BLOCK
    n_top = min(TOP_N, ns)
    k_blk = jnp.transpose(ks.reshape(b, ns, SEL_BLOCK, NSA_KV_HEADS, HEAD_DIM), (0, 3, 1, 2, 4))
    v_blk = jnp.transpose(vs.reshape(b, ns, SEL_BLOCK, NSA_KV_HEADS, HEAD_DIM), (0, 3, 1, 2, 4))
    k_pad = jnp.pad(kw.reshape(kv_shape), ((0, 0), (WINDOW, 0), (0, 0), (0, 0)))
    v_pad = jnp.pad(vw.reshape(kv_shape), ((0, 0), (WINDOW, 0), (0, 0), (0, 0)))
    gates = jax.nn.sigmoid(gt.reshape(b, t, NSA_KV_HEADS, GQ, 3))
    cmp_end = jnp.arange(nc) * CMP_STRIDE + (CMP_BLOCK - 1)
    blk_ids = jnp.arange(ns)
    per = CMP_BLOCK // CMP_STRIDE
    ratio = SEL_BLOCK // CMP_STRIDE
    n_off = ratio + per - 1
    scale = HEAD_DIM ** -0.5
    gather = jax.vmap(jax.vmap(lambda blk, ix: blk[ix]))

    def one_block(c):
        t0 = c * Q_BLOCK
        tq = t0 + jnp.arange(Q_BLOCK)
        qc = lax.dynamic_slice_in_dim(q, t0, Q_BLOCK, axis=1)
        s = jnp.einsum('bqgrd,bngd->bgrqn', qc, k_cmp).astype(f32) * scale
        pc = masked_softmax(s, cmp_end[None, :] <= tq[:, None], (-1,))
        o_cmp = jnp.einsum('bgrqn,bngd->bqgrd', pc.astype(v_cmp.dtype), v_cmp)
        imp = jnp.pad(jnp.sum(pc, axis=2), ((0, 0), (0, 0), (0, 0), (per - 1, n_off)))
        sel = sum(imp[..., o:o + ratio * ns:ratio] for o in range(n_off))
        cur = tq // SEL_BLOCK
        future = blk_ids[None, :] > cur[:, None]
        forced = (blk_ids[None, :] == 0) | (blk_ids[None, :] == cur[:, None]) | (blk_ids[None, :] == cur[:, None] - 1)
        score = jnp.where(future, -1.0, jnp.where(forced, sel + FORCE_BONUS, sel))
        _, idx = lax.top_k(score, n_top)
        kg = gather(k_blk, idx)
        vg = gather(v_blk, idx)
        kpos = idx[..., None] * SEL_BLOCK + jnp.arange(SEL_BLOCK)
        msel = (kpos <= tq[:, None, None])[:, :, None]
        s = jnp.einsum('bqgrd,bgqnld->bgrqnl', qc, kg).astype(f32) * scale
        ps = masked_softmax(s, msel, (-2, -1))
        o_sel = jnp.einsum('bgrqnl,bgqnld->bqgrd', ps.astype(vg.dtype), vg)
        kwc = lax.dynamic_slice_in_dim(k_pad, t0, Q_BLOCK + WINDOW, axis=1)
        vwc = lax.dynamic_slice_in_dim(v_pad, t0, Q_BLOCK + WINDOW, axis=1)
        wpos = t0 - WINDOW + jnp.arange(Q_BLOCK + WINDOW)
        diff = tq[:, None] - wpos[None, :]
        mwin = (diff >= 0) & (diff < WINDOW) & (wpos[None, :] >= 0)
        s = jnp.einsum('bqgrd,bkgd->bgrqk', qc, kwc).astype(f32) * scale
        pw = masked_softmax(s, mwin, (-1,))
        o_win = jnp.einsum('bgrqk,bkgd->bqgrd', pw.astype(vwc.dtype), vwc)
        gc = lax.dynamic_slice_in_dim(gates, t0, Q_BLOCK, axis=1)
        return gc[..., 0:1] * o_cmp + gc[..., 1:2] * o_sel + gc[..., 2:3] * o_win

    out = lax.map(one_block, jnp.arange(t // Q_BLOCK))
    return jnp.moveaxis(out, 0, 1).reshape(b, t, D_NSA)


def rwkv7_mixer(p, mu, w0, w2, a0, a2, g2, k_k, k_a, r_k, ln_g, ln_b):
    b, t, _ = p.shape
    f32 = jnp.float32
    hn = (RWKV_HEADS, RWKV_HEAD_DIM)
    p = token_shift(p, mu)
    sizes = [D_RWKV, D_RWKV, D_RWKV, DECAY_LORA, AAA_LORA, GATE_LORA]
    r, k, v, xw, xa, xg = jnp.split(p, _offsets(sizes), axis=-1)
    wlog = -jax.nn.softplus(-(w0 + jnp.tanh(xw) @ w2).astype(f32)) - 0.5
    decay = jnp.exp(-jnp.exp(wlog))
    a = jax.nn.sigmoid(a0 + xa @ a2)
    g = jax.nn.sigmoid(xg) @ g2
    hs = (b, t) + hn
    r, k, v, a, decay = [z.reshape(hs).astype(f32) for z in (r, k, v, a, decay)]
    kk = k * k_k.reshape(hn)
    kk = kk / jnp.maximum(jnp.sqrt(jnp.sum(kk * kk, axis=-1, keepdims=True)), 1e-12)
    k = k * (1.0 + (a - 1.0) * k_a.reshape(hn))

    def step(S, inp):
        r_t, w_t, k_t, v_t, kk_t, a_t = inp
        sa = jnp.einsum('bhvk,bhk->bhv', S, -kk_t)
        S = S * w_t[:, :, None, :] + sa[..., None] * (kk_t * a_t)[:, :, None, :] + v_t[..., None] * k_t[:, :, None, :]
        return S, jnp.einsum('bhvk,bhk->bhv', S, r_t)

    xs = tuple(jnp.moveaxis(z, 1, 0) for z in (r, decay, k, v, kk, a))
    S0 = jnp.zeros((b, RWKV_HEADS, RWKV_HEAD_DIM, RWKV_HEAD_DIM), f32)
    _, y = lax.scan(step, S0, xs)
    y = jnp.moveaxis(y, 0, 1)
    y = layer_norm(y, ln_g.reshape(hn), ln_b.reshape(hn), RWKV_GN_EPS)
    y = y + jnp.sum(r * k * r_k, axis=-1, keepdims=True) * v
    return (y.reshape(b, t, D_RWKV) * g.astype(f32)).astype(p.dtype)


def conv_ffn(h, w_gate, w_up, conv_w, conv_b, w_down):
    a = causal_dwconv(h @ w_gate, conv_w, conv_b)
    return (jax.nn.silu(a) * (h @ w_up)) @ w_down


def setup_inputs(seed: int = 0) -> dict:
    key = jax.random.key(seed)
    ks = jax.random.split(key, 32)
    L = DEPTH

    def nrm(k, shape, scale):
        return jax.random.normal(k, shape, jnp.float32) * scale

    return {
        "x": nrm(ks[0], (BATCH, SEQ, D_MODEL), 1.0),
        "w_in": nrm(ks[1], (L, D_MODEL, IN_COLS), D_MODEL ** -0.5),
        "w_out": nrm(ks[2], (L, D_MIX, D_MODEL), D_MIX ** -0.5),
        "norm_mix": 1.0 + nrm(ks[3], (L, D_MODEL), 0.02),
        "norm_ffn": 1.0 + nrm(ks[4], (L, D_MODEL), 0.02),
        "norm_final": 1.0 + nrm(ks[5], (D_MODEL,), 0.02),
        "conv_dw_w": nrm(ks[6], (L, CONV_KERNEL, D_CONV), CONV_KERNEL ** -0.5),
        "conv_dw_b": nrm(ks[7], (L, D_CONV), 0.01),
        "conv_ln_g": 1.0 + nrm(ks[8], (L, D_CONV), 0.02),
        "conv_ln_b": nrm(ks[9], (L, D_CONV), 0.01),
        "cmp_pe_k": nrm(ks[10], (L, CMP_BLOCK, HEAD_DIM), 0.1),
        "cmp_pe_v": nrm(ks[11], (L, CMP_BLOCK, HEAD_DIM), 0.1),
        "cmp_w1_k": nrm(ks[12], (L, CMP_BLOCK * HEAD_DIM, HEAD_DIM), (CMP_BLOCK * HEAD_DIM) ** -0.5),
        "cmp_w2_k": nrm(ks[13], (L, HEAD_DIM, HEAD_DIM), HEAD_DIM ** -0.5),
        "cmp_w1_v": nrm(ks[14], (L, CMP_BLOCK * HEAD_DIM, HEAD_DIM), (CMP_BLOCK * HEAD_DIM) ** -0.5),
        "cmp_w2_v": nrm(ks[15], (L, HEAD_DIM, HEAD_DIM), HEAD_DIM ** -0.5),
        "rwkv_mu": jax.random.uniform(ks[16], (L, RWKV_COLS), jnp.float32),
        "rwkv_w0": jax.random.uniform(ks[17], (L, D_RWKV), jnp.float32, -4.0, 1.0),
        "rwkv_w2": nrm(ks[18], (L, DECAY_LORA, D_RWKV), 0.5 * DECAY_LORA ** -0.5),
        "rwkv_a0": nrm(ks[19], (L, D_RWKV), 0.1),
        "rwkv_a2": nrm(ks[20], (L, AAA_LORA, D_RWKV), AAA_LORA ** -0.5),
        "rwkv_g2": nrm(ks[21], (L, GATE_LORA, D_RWKV), GATE_LORA ** -0.5),
        "rwkv_k_k": 0.85 + nrm(ks[22], (L, D_RWKV), 0.02),
        "rwkv_k_a": 1.0 + nrm(ks[23], (L, D_RWKV), 0.02),
        "rwkv_r_k": nrm(ks[24], (L, RWKV_HEADS, RWKV_HEAD_DIM), 0.1),
        "rwkv_ln_g": 1.0 + nrm(ks[25], (L, D_RWKV), 0.02),
        "rwkv_ln_b": nrm(ks[26], (L, D_RWKV), 0.01),
        "ffn_w_gate": nrm(ks[27], (L, D_MODEL, D_FF), D_MODEL ** -0.5),
        "ffn_w_up": nrm(ks[28], (L, D_MODEL, D_FF), D_MODEL ** -0.5),
        "ffn_conv_w": nrm(ks[29], (L, FFN_CONV, D_FF), FFN_CONV ** -0.5),
        "ffn_conv_b": nrm(ks[30], (L, D_FF), 0.01),
        "ffn_w_down": nrm(ks[31], (L, D_FF, D_MODEL), D_FF ** -0.5),
    }


def reference(x, w_in, w_out, norm_mix, norm_ffn, norm_final, conv_dw_w, conv_dw_b,
              conv_ln_g, conv_ln_b, cmp_pe_k, cmp_pe_v, cmp_w1_k, cmp_w2_k, cmp_w1_v,
              cmp_w2_v, rwkv_mu, rwkv_w0, rwkv_w2, rwkv_a0, rwkv_a2, rwkv_g2, rwkv_k_k,
              rwkv_k_a, rwkv_r_k, rwkv_ln_g, rwkv_ln_b, ffn_w_gate, ffn_w_up, ffn_conv_w,
              ffn_conv_b, ffn_w_down):
    splits = _offsets([CONV_COLS, NSA_COLS, RWKV_COLS])
    for i in range(DEPTH):
        h = rms_norm(x, norm_mix[i])
        proj = h @ w_in[i]
        p_conv, p_nsa, p_rwkv = jnp.split(proj, splits, axis=-1)
        y_conv = conv_mixer(p_conv, conv_dw_w[i], conv_dw_b[i], conv_ln_g[i], conv_ln_b[i])
        y_nsa = nsa_mixer(p_nsa, cmp_pe_k[i], cmp_pe_v[i], cmp_w1_k[i], cmp_w2_k[i],
                          cmp_w1_v[i], cmp_w2_v[i])
        y_rwkv = rwkv7_mixer(p_rwkv, rwkv_mu[i], rwkv_w0[i], rwkv_w2[i], rwkv_a0[i],
                             rwkv_a2[i], rwkv_g2[i], rwkv_k_k[i], rwkv_k_a[i], rwkv_r_k[i],
                             rwkv_ln_g[i], rwkv_ln_b[i])
        mix = jnp.concatenate([y_conv, y_nsa, y_rwkv], axis=-1)
        x = x + mix @ w_out[i]
        h = rms_norm(x, norm_ffn[i])
        x = x + conv_ffn(h, ffn_w_gate[i], ffn_w_up[i], ffn_conv_w[i], ffn_conv_b[i], ffn_w_down[i])
    return rms_norm(x, norm_final)
```

```python
import numpy as np
from contextlib import ExitStack
import concourse.bass as bass
import concourse.mybir as mybir
from concourse.bass_utils import run_bass_kernel_spmd

F32 = mybir.dt.float32
BF16 = mybir.dt.bfloat16
AF = mybir.ActivationFunctionType
ALU = mybir.AluOpType
AX = mybir.AxisListType

D = 1024
SEQ = 16384
NB = 2
DEPTH = 4
DFF = 2816
NCORES = 8
TQ = SEQ // 4
RMS_EPS = 1e-6


class Buf:
    __slots__ = ("w", "r", "name")

    def __init__(self, name=""):
        self.w = None
        self.r = {}
        self.name = name


class KB:
    NDMA = 6

    def __init__(self):
        self.nc = bass.Bass("TRN2", target_bir_lowering=False)
        nc = self.nc
        self.E = {"pe": nc.tensor, "act": nc.scalar, "dve": nc.vector,
                  "pool": nc.gpsimd, "sp": nc.sync}
        self.sems = {}
        self.cnt = {}
        for e in ("pe", "act", "dve", "pool"):
            self.sems[e] = nc.alloc_semaphore("c_" + e)
            self.cnt[e] = 0
        self.dq = {}
        for q in ("sp", "pool", "act"):
            ring = []
            for i in range(self.NDMA):
                key = "d_%s_%d" % (q, i)
                self.sems[key] = nc.alloc_semaphore(key)
                self.cnt[key] = 0
                ring.append(key)
            self.dq[q] = [ring, 0]
        self.known = {e: {} for e in self.E}
        self.nbuf = 0
        self.out_tokens = []

    def sb(self, name, shape, dt):
        t = self.nc.alloc_sbuf_tensor(name, list(shape), dt)
        return t.ap(), Buf(name)

    def sbs(self, stack, name, shape, dt):
        t = stack.enter_context(self.nc.sbuf_tensor(name, list(shape), dt))
        return t.ap(), Buf(name)

    def release(self, bufs):
        for eng in self.E:
            self._waits(eng, [], bufs, skip_own=(eng == "pe"))

    def ps(self, name, shape, dt=F32):
        t = self.nc.alloc_psum_tensor(name, list(shape), dt)
        return t.ap(), Buf(name)

    def dram(self, name, shape, dt, kind):
        return self.nc.dram_tensor(name, list(shape), dt, kind=kind).ap(), Buf(name)

    def _waits(self, eng, reads, writes, skip_own=False):
        deps = {}
        for b in reads:
            if b.w is not None:
                k, v = b.w
                if deps.get(k, 0) < v:
                    deps[k] = v
        for b in writes:
            if b.w is not None:
                k, v = b.w
                if deps.get(k, 0) < v:
                    deps[k] = v
            for k, v in b.r.items():
                if deps.get(k, 0) < v:
                    deps[k] = v
        kn = self.known[eng]
        for k, v in deps.items():
            if skip_own and k == eng:
                continue
            if kn.get(k, 0) < v:
                self.E[eng].wait_ge(self.sems[k], v)
                kn[k] = v

    def _record(self, tok, reads, writes):
        k, v = tok
        for b in writes:
            b.w = tok
            b.r = {}
        for b in reads:
            if b.r.get(k, 0) < v:
                b.r[k] = v

    def op(self, eng, fn, reads=(), writes=()):
        self._waits(eng, reads, writes, skip_own=(eng == "pe"))
        inst = fn(self.E[eng])
        self.cnt[eng] += 1
        inst.then_inc(self.sems[eng], 1)
        tok = (eng, self.cnt[eng])
        self._record(tok, reads, writes)
        return tok

    def dma(self, q, out, in_, reads=(), writes=(), **kw):
        ring, i = self.dq[q]
        key = ring[i % self.NDMA]
        self.dq[q][1] = i + 1
        kn = self.known[q]
        if kn.get(key, 0) < self.cnt[key]:
            self.E[q].wait_ge(self.sems[key], self.cnt[key])
            kn[key] = self.cnt[key]
        self._waits(q, reads, writes)
        inst = self.E[q].dma_start(out=out, in_=in_, **kw)
        self.cnt[key] += 16
        inst.then_inc(self.sems[key], 16)
        tok = (key, self.cnt[key])
        self._record(tok, reads, writes)
        return tok

    def finish(self, out_bufs):
        deps = {}
        for b in out_bufs:
            if b.w is not None:
                k, v = b.w
                deps[k] = max(deps.get(k, 0), v)
        for q in self.dq:
            for key in self.dq[q][0]:
                if self.cnt[key] > 0:
                    deps[key] = max(deps.get(key, 0), self.cnt[key])
        for k, v in deps.items():
            self.nc.sync.wait_ge(self.sems[k], v)


def emit_rmsnorm(kb, xt, xb, gcol, ones, W, sq, sqb, ssum, ssb, pst, pstb, rstd, rsb, outs, outb,
                 out_dt_scale=None):
    nc = kb.nc
    for k in range(8):
        if k == 0:
            kb.op("act", lambda e: e.activation(out=ssum[:, :W], in_=xt[:, 0, :W], func=AF.Square),
                  reads=[xb], writes=[ssb])
        else:
            kb.op("act", lambda e, k=k: e.activation(out=sq[:, :W], in_=xt[:, k, :W], func=AF.Square),
                  reads=[xb], writes=[sqb])
            kb.op("dve", lambda e: e.tensor_tensor(out=ssum[:, :W], in0=ssum[:, :W], in1=sq[:, :W], op=ALU.add),
                  reads=[sqb, ssb], writes=[ssb])
    kb.op("pe", lambda e: e.matmul(pst[:, :W], lhsT=ones, rhs=ssum[:, :W], start=True, stop=True),
          reads=[ssb], writes=[pstb])
    kb.op("act", lambda e: e.activation(out=rstd[:, :W], in_=pst[:, :W], func=AF.Sqrt,
                                        bias=kb.eps_col, scale=1.0 / D),
          reads=[pstb], writes=[rsb])
    kb.op("dve", lambda e: e.reciprocal(out=rstd[:, :W], in_=rstd[:, :W]), reads=[rsb], writes=[rsb])
    for k in range(8):
        kb.op("dve", lambda e, k=k: e.scalar_tensor_tensor(out=outs[:, k, :W], in0=xt[:, k, :W],
                                                           scalar=gcol[:, k:k + 1], in1=rstd[:, :W],
                                                           op0=ALU.mult, op1=ALU.mult),
              reads=[xb, rsb], writes=[outb])


def build_stage_a():
    kb = KB()
    nc = kb.nc
    W = 512
    xT, xTb = kb.dram("xT", [D, TQ], F32, "ExternalInput")
    g, gb = kb.dram("g", [128, 8], F32, "ExternalInput")
    hT, hTb = kb.dram("hT", [D, TQ], F32, "ExternalOutput")
    gcol, gcb = kb.sb("gcol", [128, 8], F32)
    ones, onb = kb.sb("ones", [128, 128], F32)
    eps, epb = kb.sb("eps", [128, 1], F32)
    kb.eps_col = eps
    kb.op("dve", lambda e: e.memset(ones, 1.0), writes=[onb])
    kb.op("dve", lambda e: e.memset(eps, RMS_EPS), writes=[epb])
    kb.dma("sp", gcol, g, reads=[gb], writes=[gcb])
    xs = [kb.sb("x%d" % i, [128, 8, W], F32) for i in range(2)]
    os_ = [kb.sb("o%d" % i, [128, 8, W], F32) for i in range(2)]
    sq, sqb = kb.sb("sq", [128, W], F32)
    ssum, ssb = kb.sb("ssum", [128, W], F32)
    rstd, rsb = kb.sb("rstd", [128, W], F32)
    pst, pstb = kb.ps("pst", [128, W], F32)
    xTv = xT.rearrange("(k p) t -> p k t", p=128)
    hTv = hT.rearrange("(k p) t -> p k t", p=128)
    for c in range(TQ // W):
        xt, xb = xs[c % 2]
        ot, ob = os_[c % 2]
        kb.dma("sp", xt, xTv[:, :, c * W:(c + 1) * W], reads=[xTb], writes=[xb])
        emit_rmsnorm(kb, xt, xb, gcol, ones, W, sq, sqb, ssum, ssb, pst, pstb, rstd, rsb, ot, ob)
        kb.dma("sp", hTv[:, :, c * W:(c + 1) * W], ot, reads=[ob], writes=[hTb])
    kb.finish([hTb])
    return nc


def build_stage_c():
    kb = KB()
    nc = kb.nc
    W = 256
    NT = TQ
    NF = DFF // 128
    xT, xTb = kb.dram("xT", [D, NT], F32, "ExternalInput")
    mixT, mixTb = kb.dram("mixT", [D, NT], F32, "ExternalInput")
    xh, xhb = kb.dram("xh", [D, 2], F32, "ExternalInput")
    mixh, mixhb = kb.dram("mixh", [D, 2], F32, "ExternalInput")
    w_out, wob = kb.dram("w_out", [D, D], F32, "ExternalInput")
    w_gate, wgb_ = kb.dram("w_gate", [D, DFF], F32, "ExternalInput")
    w_up, wub_ = kb.dram("w_up", [D, DFF], F32, "ExternalInput")
    w_down, wdb_ = kb.dram("w_down", [DFF, D], F32, "ExternalInput")
    gf, gfb = kb.dram("g_ffn", [128, 8], F32, "ExternalInput")
    gn, gnb = kb.dram("g_next", [128, 8], F32, "ExternalInput")
    cw, cwb = kb.dram("conv_w", [128, NF, 3], F32, "ExternalInput")
    cb, cbb = kb.dram("conv_b", [128, NF], F32, "ExternalInput")
    xo, xob = kb.dram("xoT", [D, NT], F32, "ExternalOutput")
    ho, hob = kb.dram("hoT", [D, NT], F32, "ExternalOutput")

    wo_s, wo_b = kb.sb("wo_s", [128, 8, D], BF16)
    wg_s, wg_b = kb.sb("wg_s", [128, 8, DFF], BF16)
    wu_s, wu_b = kb.sb("wu_s", [128, 8, DFF], BF16)
    wd_s, wd_b = kb.sb("wd_s", [128, NF, D], BF16)
    wov = w_out.rearrange("(k p) n -> p k n", p=128)
    wgv = w_gate.rearrange("(k p) n -> p k n", p=128)
    wuv = w_up.rearrange("(k p) n -> p k n", p=128)
    wdv = w_down.rearrange("(k p) n -> p k n", p=128)
    for k in range(8):
        kb.dma("pool", wo_s[:, k, :], wov[:, k, :], reads=[wob], writes=[wo_b])
    for k in range(8):
        for hh in range(2):
            sl = slice(hh * (DFF // 2), (hh + 1) * (DFF // 2))
            kb.dma("pool", wg_s[:, k, sl], wgv[:, k, sl], reads=[wgb_], writes=[wg_b])
            kb.dma("pool", wu_s[:, k, sl], wuv[:, k, sl], reads=[wub_], writes=[wu_b])
    for f in range(NF):
        kb.dma("pool", wd_s[:, f, :], wdv[:, f, :], reads=[wdb_], writes=[wd_b])

    gfc, gfcb = kb.sb("gfc", [128, 8], F32)
    gnc, gncb = kb.sb("gnc", [128, 8], F32)
    cws, cwsb = kb.sb("cws", [128, NF, 3], F32)
    cbs, cbsb = kb.sb("cbs", [128, NF], F32)
    kb.dma("sp", gfc, gf, reads=[gfb], writes=[gfcb])
    kb.dma("sp", gnc, gn, reads=[gnb], writes=[gncb])
    kb.dma("sp", cws, cw, reads=[cwb], writes=[cwsb])
    kb.dma("sp", cbs, cb, reads=[cbb], writes=[cbsb])
    ones, onb = kb.sb("ones", [128, 128], F32)
    eps, epb = kb.sb("eps", [128, 1], F32)
    kb.eps_col = eps
    kb.op("dve", lambda e: e.memset(ones, 1.0), writes=[onb])
    kb.op("dve", lambda e: e.memset(eps, RMS_EPS), writes=[epb])

    xs = [kb.sb("x%d" % i, [128, 8, W], F32) for i in range(2)]
    ms = [kb.sb("m%d" % i, [128, 8, W], BF16) for i in range(2)]
    xmid, xmb = kb.sb("xmid", [128, 8, W], F32)
    h2, h2b = kb.sb("h2", [128, 8, W], BF16)
    actb, actbb = kb.sb("actb", [128, NF, W], BF16)
    xnew, xnb = xmid, xmb
    carry, carb = kb.sb("carry", [128, NF, 2], F32)
    aext = [kb.sb("aext%d" % i, [128, W + 2], F32) for i in range(2)]
    cs = [kb.sb("cs%d" % i, [128, W], F32) for i in range(2)]
    sg = [kb.sb("sg%d" % i, [128, W], F32) for i in range(2)]
    sq, sqb = kb.sb("sq", [128, W], F32)
    ssum, ssb = kb.sb("ssum", [128, W], F32)
    rstd, rsb = kb.sb("rstd", [128, W], F32)
    pa = [kb.ps("pa%d" % i, [128, 512], F32) for i in range(2)]
    pu = [kb.ps("pu%d" % i, [128, 512], F32) for i in range(2)]
    pacc = [kb.ps("pacc%d" % i, [128, 512], F32) for i in range(2)]
    pst, pstb = kb.ps("pst", [128, 512], F32)

    xTv = xT.rearrange("(k p) t -> p k t", p=128)
    mTv = mixT.rearrange("(k p) t -> p k t", p=128)
    xhv = xh.rearrange("(k p) t -> p k t", p=128)
    mhv = mixh.rearrange("(k p) t -> p k t", p=128)
    xov = xo.rearrange("(k p) t -> p k t", p=128)
    hov = ho.rearrange("(k p) t -> p k t", p=128)

    nchunks = NT // W
    acc_i = 0
    for c in range(-1, nchunks):
        halo = c < 0
        Wc = 2 if halo else W
        xt, xb = xs[c % 2]
        mt, mb = ms[c % 2]
        if halo:
            kb.dma("sp", xt[:, :, :Wc], xhv, reads=[xhb], writes=[xb])
            kb.dma("pool", mt[:, :, :Wc], mhv, reads=[mixhb], writes=[mb])
        else:
            kb.dma("sp", xt, xTv[:, :, c * W:(c + 1) * W], reads=[xTb], writes=[xb])
            kb.dma("pool", mt, mTv[:, :, c * W:(c + 1) * W], reads=[mixTb], writes=[mb])
        for m in range(8):
            p, pb = pacc[acc_i % 2]
            acc_i += 1
            for k in range(8):
                kb.op("pe", lambda e, k=k, m=m, p=p: e.matmul(p[:, :Wc], lhsT=wo_s[:, k, m * 128:(m + 1) * 128],
                                                           rhs=mt[:, k, :Wc], start=(k == 0), stop=(k == 7)),
                      reads=[wo_b, mb], writes=[pb])
            kb.op("dve", lambda e, m=m, p=p: e.tensor_tensor(out=xmid[:, m, :Wc], in0=p[:, :Wc], in1=xt[:, m, :Wc],
                                                          op=ALU.add),
                  reads=[pb, xb], writes=[xmb])
        emit_rmsnorm(kb, xmid, xmb, gfc, ones, Wc, sq, sqb, ssum, ssb, pst, pstb, rstd, rsb, h2, h2b)
        for f in range(NF):
            a, ab = pa[f % 2]
            u, ub = pu[f % 2]
            ae, aeb = aext[f % 2]
            ct, ctb = cs[f % 2]
            st, stb = sg[f % 2]
            for k in range(8):
                kb.op("pe", lambda e, k=k, f=f, a=a: e.matmul(a[:, :Wc], lhsT=wg_s[:, k, f * 128:(f + 1) * 128],
                                                           rhs=h2[:, k, :Wc], start=(k == 0), stop=(k == 7)),
                      reads=[wg_b, h2b], writes=[ab])
            if halo:
                kb.op("act", lambda e, f=f, a=a: e.activation(out=carry[:, f, :], in_=a[:, :2], func=AF.Copy),
                      reads=[ab], writes=[carb])
                continue
            for k in range(8):
                kb.op("pe", lambda e, k=k, f=f, u=u: e.matmul(u[:, :Wc], lhsT=wu_s[:, k, f * 128:(f + 1) * 128],
                                                           rhs=h2[:, k, :Wc], start=(k == 0), stop=(k == 7)),
                      reads=[wu_b, h2b], writes=[ub])
            kb.op("pool", lambda e, f=f, ae=ae: e.tensor_copy(out=ae[:, 0:2], in_=carry[:, f, :]),
                  reads=[carb], writes=[aeb])
            kb.op("act", lambda e, a=a, ae=ae: e.activation(out=ae[:, 2:2 + W], in_=a[:, :W], func=AF.Copy),
                  reads=[ab], writes=[aeb])
            kb.op("act", lambda e, f=f, a=a, ct=ct: e.activation(out=ct, in_=a[:, :W], func=AF.Identity,
                                                              scale=cws[:, f, 2:3], bias=cbs[:, f:f + 1]),
                  reads=[ab, cwsb, cbsb], writes=[ctb])
            kb.op("pool", lambda e, f=f, ae=ae: e.tensor_copy(out=carry[:, f, :], in_=ae[:, W:W + 2]),
                  reads=[aeb], writes=[carb])
            kb.op("dve", lambda e, f=f, ae=ae, ct=ct: e.scalar_tensor_tensor(out=ct, in0=ae[:, 1:1 + W],
                                                                          scalar=cws[:, f, 1:2], in1=ct,
                                                                          op0=ALU.mult, op1=ALU.add),
                  reads=[aeb, ctb, cwsb], writes=[ctb])
            kb.op("dve", lambda e, f=f, ae=ae, ct=ct: e.scalar_tensor_tensor(out=ct, in0=ae[:, 0:W],
                                                                          scalar=cws[:, f, 0:1], in1=ct,
                                                                          op0=ALU.mult, op1=ALU.add),
                  reads=[aeb, ctb, cwsb], writes=[ctb])
            kb.op("act", lambda e, ct=ct, st=st: e.activation(out=st, in_=ct, func=AF.Silu),
                  reads=[ctb], writes=[stb])
            kb.op("dve", lambda e, f=f, st=st, u=u: e.tensor_tensor(out=actb[:, f, :], in0=u[:, :W], in1=st,
                                                                 op=ALU.mult),
                  reads=[ub, stb], writes=[actbb])
        if halo:
            continue
        for m in range(8):
            p, pb = pacc[acc_i % 2]
            acc_i += 1
            for f in range(NF):
                kb.op("pe", lambda e, f=f, m=m, p=p: e.matmul(p[:, :W], lhsT=wd_s[:, f, m * 128:(m + 1) * 128],
                                                           rhs=actb[:, f, :], start=(f == 0), stop=(f == NF - 1)),
                      reads=[wd_b, actbb], writes=[pb])
            kb.op("dve", lambda e, m=m, p=p: e.tensor_tensor(out=xnew[:, m, :], in0=p[:, :W], in1=xmid[:, m, :],
                                                          op=ALU.add),
                  reads=[pb, xmb], writes=[xnb])
        kb.dma("sp", xov[:, :, c * W:(c + 1) * W], xnew, reads=[xnb], writes=[xob])
        hout, houtb = xt, xb
        emit_rmsnorm(kb, xnew, xnb, gnc, ones, W, sq, sqb, ssum, ssb, pst, pstb, rstd, rsb, hout, houtb)
        kb.dma("sp", hov[:, :, c * W:(c + 1) * W], hout, reads=[houtb], writes=[hob])
    kb.finish([xob, hob])
    return nc


def mm(kb, out, lhsT, rhs, start, stop, reads, writes):
    return kb.op("pe", lambda e: e.matmul(out, lhsT=lhsT, rhs=rhs, start=start, stop=stop),
                 reads=reads, writes=writes)


def actf(kb, out, in_, func, reads, writes, **kw):
    return kb.op("act", lambda e: e.activation(out=out, in_=in_, func=func, **kw), reads=reads, writes=writes)


def tt(kb, out, in0, in1, op, reads, writes, eng="dve"):
    return kb.op(eng, lambda e: e.tensor_tensor(out=out, in0=in0, in1=in1, op=op), reads=reads, writes=writes)


def ts(kb, out, in0, s1, op0, reads, writes, s2=None, op1=None, eng="dve"):
    if op1 is None:
        return kb.op(eng, lambda e: e.tensor_scalar(out=out, in0=in0, scalar1=s1, scalar2=None, op0=op0),
                     reads=reads, writes=writes)
    return kb.op(eng, lambda e: e.tensor_scalar(out=out, in0=in0, scalar1=s1, scalar2=s2, op0=op0, op1=op1),
                 reads=reads, writes=writes)


def stt(kb, out, in0, scalar, in1, op0, op1, reads, writes):
    return kb.op("dve", lambda e: e.scalar_tensor_tensor(out=out, in0=in0, scalar=scalar, in1=in1,
                                                        op0=op0, op1=op1), reads=reads, writes=writes)


def transp(kb, out, in_, ident, reads, writes):
    return kb.op("pe", lambda e: e.transpose(out, in_, ident), reads=reads, writes=writes)


CONV_K = 31
HALO = 32
LN_EPS = 1e-5


def build_stage_conv():
    kb = KB()
    nc = kb.nc
    W = 512
    NT = TQ
    hT, hTb = kb.dram("hT", [D, HALO + NT], F32, "ExternalInput")
    wc, wcb = kb.dram("w_conv", [D, 512], F32, "ExternalInput")
    dww, dwwb = kb.dram("dw_w", [128, 2, CONV_K], F32, "ExternalInput")
    dwb, dwbb = kb.dram("dw_b", [128, 2], F32, "ExternalInput")
    lng, lngb = kb.dram("ln_g", [128, 2], F32, "ExternalInput")
    lnb, lnbb = kb.dram("ln_b", [128, 2], F32, "ExternalInput")
    oT, oTb = kb.dram("convT", [256, NT], F32, "ExternalOutput")

    wcs, wcsb = kb.sb("wcs", [128, 8, 512], BF16)
    wcv = wc.rearrange("(k p) n -> p k n", p=128)
    for k in range(8):
        kb.dma("pool", wcs[:, k, :], wcv[:, k, :], reads=[wcb], writes=[wcsb])
    dws, dwsb = kb.sb("dws", [128, 2, CONV_K], F32)
    dbs, dbsb = kb.sb("dbs", [128, 2], F32)
    lgs, lgsb = kb.sb("lgs", [128, 2], F32)
    lbs, lbsb = kb.sb("lbs", [128, 2], F32)
    kb.dma("sp", dws, dww, reads=[dwwb], writes=[dwsb])
    kb.dma("sp", dbs, dwb, reads=[dwbb], writes=[dbsb])
    kb.dma("sp", lgs, lng, reads=[lngb], writes=[lgsb])
    kb.dma("sp", lbs, lnb, reads=[lnbb], writes=[lbsb])
    ones, onb = kb.sb("ones", [128, 128], F32)
    eps, epb = kb.sb("eps", [128, 1], F32)
    kb.op("dve", lambda e: e.memset(ones, 1.0), writes=[onb])
    kb.op("dve", lambda e: e.memset(eps, LN_EPS), writes=[epb])

    Y, Yb = kb.sb("Y", [128, 2, HALO + NT], F32)
    acc, accb = kb.sb("acc", [128, 2, NT], F32)
    hbs = [kb.sb("hb%d" % i, [128, 8, W], BF16) for i in range(2)]
    sgs = [kb.sb("sg%d" % i, [128, W], F32) for i in range(2)]
    pu = [kb.ps("pu%d" % i, [128, 512], F32) for i in range(2)]
    pg = [kb.ps("pg%d" % i, [128, 512], F32) for i in range(2)]
    hTv = hT.rearrange("(k p) t -> p k t", p=128)
    chunks = [(0, HALO)] + [(HALO + i * W, W) for i in range(NT // W)]
    it = 0
    for ci, (c0, Wc) in enumerate(chunks):
        hb, hbb = hbs[ci % 2]
        kb.dma("pool", hb[:, :, :Wc], hTv[:, :, c0:c0 + Wc], reads=[hTb], writes=[hbb])
        for ct in range(2):
            u, ub = pu[it % 2]
            g, gb = pg[it % 2]
            sg, sgb = sgs[it % 2]
            it += 1
            for k in range(8):
                mm(kb, u[:, :Wc], wcs[:, k, ct * 128:(ct + 1) * 128], hb[:, k, :Wc], k == 0, k == 7,
                   [wcsb, hbb], [ub])
            for k in range(8):
                mm(kb, g[:, :Wc], wcs[:, k, 256 + ct * 128:256 + (ct + 1) * 128], hb[:, k, :Wc], k == 0, k == 7,
                   [wcsb, hbb], [gb])
            actf(kb, sg[:, :Wc], g[:, :Wc], AF.Sigmoid, [gb], [sgb])
            tt(kb, Y[:, ct, c0:c0 + Wc], u[:, :Wc], sg[:, :Wc], ALU.mult, [ub, sgb], [Yb])
    HW = NT // 2
    accbs = [[Buf("acc%d%d" % (ct, h)) for h in range(2)] for ct in range(2)]
    for ct in range(2):
        for h in range(2):
            o0 = h * HW
            ab = accbs[ct][h]
            actf(kb, acc[:, ct, o0:o0 + HW], Y[:, ct, HALO + o0:HALO + o0 + HW], AF.Identity, [Yb, dwsb, dbsb], [ab],
                 scale=dws[:, ct, CONV_K - 1:CONV_K], bias=dbs[:, ct:ct + 1])
            for k in range(CONV_K - 1):
                s0 = HALO - (CONV_K - 1) + k + o0
                stt(kb, acc[:, ct, o0:o0 + HW], Y[:, ct, s0:s0 + HW], dws[:, ct, k:k + 1], acc[:, ct, o0:o0 + HW],
                    ALU.mult, ALU.add, [Yb, dwsb, ab], [ab])
    sq = [kb.sb("sq%d" % i, [128, W], F32) for i in range(2)]
    mean, meanb = kb.sb("mean", [128, W], F32)
    msq, msqb = kb.sb("msq", [128, W], F32)
    var, varb = kb.sb("var", [128, W], F32)
    z = [kb.sb("z%d" % i, [128, W], F32) for i in range(2)]
    ot = [kb.sb("ot%d" % i, [128, W], F32) for i in range(2)]
    p1, p1b = pu[0]
    p2, p2b = pu[1]
    oTv = oT.rearrange("(c p) t -> p c t", p=128)
    for n in range(NT // W):
        sl = slice(n * W, (n + 1) * W)
        ab = [accbs[0][n * W // HW], accbs[1][n * W // HW]]
        for ct in range(2):
            mm(kb, p1, ones, acc[:, ct, sl], ct == 0, ct == 1, [onb, ab[ct]], [p1b])
        for ct in range(2):
            s, sb_ = sq[ct]
            actf(kb, s, acc[:, ct, sl], AF.Square, [ab[ct]], [sb_])
            mm(kb, p2, ones, s, ct == 0, ct == 1, [onb, sb_], [p2b])
        ts(kb, mean, p1, 1.0 / 256, ALU.mult, [p1b], [meanb])
        tt(kb, msq, mean, mean, ALU.mult, [meanb], [msqb])
        stt(kb, var, p2, 1.0 / 256, msq, ALU.mult, ALU.subtract, [p2b, msqb], [varb])
        actf(kb, var, var, AF.Sqrt, [varb, epb], [varb], bias=eps, scale=1.0)
        kb.op("dve", lambda e: e.reciprocal(out=var, in_=var), reads=[varb], writes=[varb])
        for ct in range(2):
            zt, zb = z[ct]
            o, ob = ot[ct]
            tt(kb, zt, acc[:, ct, sl], mean, ALU.subtract, [ab[ct], meanb], [zb])
            tt(kb, zt, zt, var, ALU.mult, [zb, varb], [zb])
            actf(kb, o, zt, AF.Silu, [zb, lgsb, lbsb], [ob], scale=lgs[:, ct:ct + 1], bias=lbs[:, ct:ct + 1])
            kb.dma("sp", oTv[:, ct, sl], o, reads=[ob], writes=[oTb])
    kb.finish([oTb])
    return nc


def _cols(v, n):
    return np.ascontiguousarray(np.asarray(v, np.float32).reshape(n, 128).T)


def conv_inputs(P, l, hT):
    return {"hT": hT,
            "w_conv": np.ascontiguousarray(P["w_in"][l][:, 0:512]),
            "dw_w": np.ascontiguousarray(P["conv_dw_w"][l].reshape(CONV_K, 2, 128).transpose(2, 1, 0)),
            "dw_b": _cols(P["conv_dw_b"][l], 2), "ln_g": _cols(P["conv_ln_g"][l], 2),
            "ln_b": _cols(P["conv_ln_b"][l], 2)}


RW_COLS = 320
GN_EPS = 64e-5
CH = 128
RW_NTOK = SEQ


def build_stage_rwkv(ntok=None, dbg=0):
    ntok = ntok or RW_NTOK
    kb = KB()
    nc = kb.nc
    GW = 512
    hT, hTb = kb.dram("hT", [D, ntok], F32, "ExternalInput")
    wr, wrb = kb.dram("w_r", [D, RW_COLS], F32, "ExternalInput")
    mu, mub = kb.dram("mu_bc", [128, RW_COLS], F32, "ExternalInput")
    vecs, vecsb = kb.dram("vecs", [64, 8], F32, "ExternalInput")
    w2d, w2db = kb.dram("w2", [32, 64], F32, "ExternalInput")
    a2d, a2db = kb.dram("a2", [32, 64], F32, "ExternalInput")
    g2d, g2db = kb.dram("g2", [64, 64], F32, "ExternalInput")
    lgd, lgdb = kb.dram("lng_bc", [128, 64], F32, "ExternalInput")
    lbd, lbdb = kb.dram("lnb_bc", [128, 64], F32, "ExternalInput")
    yT, yTb = kb.dram("yT", [64, ntok], F32, "ExternalOutput")

    dif, difb = kb.sb("dif", [128, 128], F32)
    kb.op("pool", lambda e: e.iota(dif, pattern=[[1, 128]], base=0, channel_multiplier=-1, allow_small_or_imprecise_dtypes=True), writes=[difb])
    ident, identb = kb.sb("ident", [128, 128], F32)
    mU2, mU2b = kb.sb("mU2", [128, 256], F32)
    mL, mLb = kb.sb("mL", [128, 128], F32)
    ts(kb, ident, dif, 0.0, ALU.is_equal, [difb], [identb])
    ts(kb, mU2[:, 0:128], dif, 0.0, ALU.is_gt, [difb], [mU2b])
    ts(kb, mU2[:, 128:256], dif, 0.0, ALU.is_ge, [difb], [mU2b])
    ts(kb, mL, dif, 0.0, ALU.is_lt, [difb], [mLb])
    ones64, o64b = kb.sb("ones64", [64, 64], F32)
    kb.op("dve", lambda e: e.memset(ones64, 1.0), writes=[o64b])
    rmask, rmb = kb.sb("rmask", [64, GW], F32)
    kb.op("dve", lambda e: e.memset(rmask, 1.0), writes=[rmb])
    for i in range(GW // CH):
        kb.op("dve", lambda e, i=i: e.memset(rmask[:, i * CH:i * CH + 1], 0.0), writes=[rmb])
    geps, gepsb = kb.sb("geps", [128, 1], F32)
    kb.op("dve", lambda e: e.memset(geps, GN_EPS), writes=[gepsb])

    wf, wfb = kb.sb("wf", [128, 8, RW_COLS], F32)
    mus, musb = kb.sb("mus", [128, RW_COLS], F32)
    wA, wAb = kb.sb("wA", [128, 8, RW_COLS], BF16)
    wB, wBb = kb.sb("wB", [128, 8, RW_COLS], BF16)
    kb.dma("sp", wf, wr.rearrange("(k p) n -> p k n", p=128), reads=[wrb], writes=[wfb])
    kb.dma("sp", mus, mu, reads=[mub], writes=[musb])
    tmpw, tmpwb = kb.sb("tmpw", [128, RW_COLS], F32)
    for k in range(8):
        tt(kb, tmpw, wf[:, k, :], mus, ALU.mult, [wfb, musb], [tmpwb])
        kb.op("dve", lambda e, k=k: e.tensor_copy(out=wB[:, k, :], in_=tmpw), reads=[tmpwb], writes=[wBb])
        tt(kb, wA[:, k, :], wf[:, k, :], tmpw, ALU.subtract, [wfb, tmpwb], [wAb])
    vs_, vsb = kb.sb("vecs_s", [64, 8], F32)
    w2s, w2sb = kb.sb("w2s", [32, 64], F32)
    a2s, a2sb = kb.sb("a2s", [32, 64], F32)
    g2s, g2sb = kb.sb("g2s", [64, 64], F32)
    lgs, lgsb = kb.sb("lgs", [128, 64], F32)
    lbs, lbsb = kb.sb("lbs", [128, 64], F32)
    kb.dma("sp", vs_, vecs, reads=[vecsb], writes=[vsb])
    kb.dma("sp", w2s, w2d, reads=[w2db], writes=[w2sb])
    kb.dma("sp", a2s, a2d, reads=[a2db], writes=[a2sb])
    kb.dma("sp", g2s, g2d, reads=[g2db], writes=[g2sb])
    kb.dma("sp", lgs, lgd, reads=[lgdb], writes=[lgsb])
    kb.dma("sp", lbs, lbd, reads=[lbdb], writes=[lbsb])
    W0, A0, KK_, KA_, RK_ = (vs_[:, i:i + 1] for i in range(5))

    hbs = [kb.sb("hb%d" % i, [128, 8, GW + 1], BF16) for i in range(2)]

    def t64(name, p=64, w=GW):
        return kb.sb(name, [p, w], F32)
    r_s, r_sb = t64("r_s")
    k_s, k_sb = t64("k_s")
    th, thb = t64("th", 32)
    xa_s, xa_sb = t64("xa_s", 32)
    sgx, sgxb = t64("sgx")
    ld, ldb = t64("ld")
    a_, a_b = t64("a_")
    kkr, kkrb = t64("kkr")
    sqk, sqkb = t64("sqk")
    rn, rnb = t64("rn")
    kk, kkb = t64("kk")
    t1, t1b = t64("t1")
    kp, kpb = t64("kp")
    b_, b_b = t64("b_")
    L_, L_b = t64("L_")
    Lex, Lexb = t64("Lex")
    eL, eLb = t64("eL")
    eLex, eLexb = t64("eLex")
    enL, enLb = t64("enL")
    KR, KRb = kb.sb("KR", [64, GW // CH, 2, CH], F32)
    bt, btb = t64("bt")
    kt, ktb = t64("kt")
    rkp, rkpb = t64("rkp")
    yout, youtb = t64("yout")
    MAB, MABb = kb.sb("MAB", [128, 256], F32)
    MAK, MAKb = kb.sb("MAK", [128, 256], F32)
    Nn, Nnb = kb.sb("Nn", [128, 128], F32)
    MP = [kb.sb("MP%d" % i, [128, 256], F32) for i in range(5)]
    M64, M64b = kb.sb("M64", [128, 128], F32)
    BK, BKb = kb.sb("BK", [128, 128], F32)
    V_, V_b = kb.sb("V_", [128, 64], F32)
    Xs = [kb.sb("X%d" % i, [128, 64], F32) for i in range(2)]
    U_, U_b = kb.sb("U_", [128, 64], F32)
    Ss = [kb.sb("S%d" % i, [64, 64], F32) for i in range(2)]
    st6, st6b = kb.sb("st6", [128, 6], F32)
    mv, mvb = kb.sb("mv", [128, 2], F32)
    rs_, rs_b = kb.sb("rs_", [128, 1], F32)
    yn, ynb = kb.sb("yn", [128, 64], F32)
    bs_, bs_b = kb.sb("bs_", [128, 2], F32)
    yo, yob = kb.sb("yo", [128, 64], F32)

    pj = [kb.ps("pj%d" % i, [128, 512], F32) for i in range(2)]
    pA, pAb = kb.ps("pA", [128, 512], F32)
    pB, pBb = kb.ps("pB", [128, 512], F32)
    pW, pWb = kb.ps("pW", [128, 512], F32)
    pX, pXb = kb.ps("pX", [128, 512], F32)
    pY, pYb = kb.ps("pY", [128, 512], F32)
    pT, pTb = kb.ps("pT", [128, 512], F32)

    kb.op("dve", lambda e: e.memset(Ss[0][0], 0.0), writes=[Ss[0][1]])
    s_i = 0
    hTv = hT.rearrange("(k p) t -> p k t", p=128)
    pji = 0
    for g in range(ntok // GW):
        t0 = g * GW
        hb, hbb = hbs[g % 2]
        if g == 0:
            kb.op("pool", lambda e: e.memset(hb[:, :, 0:1], 0.0), writes=[hbb])
            kb.dma("pool", hb[:, :, 1:GW + 1], hTv[:, :, 0:GW], reads=[hTb], writes=[hbb])
        else:
            kb.dma("pool", hb, hTv[:, :, t0 - 1:t0 + GW], reads=[hTb], writes=[hbb])

        def proj(c0, m):
            nonlocal pji
            p, pb = pj[pji % 2]
            pji += 1
            for k in range(8):
                mm(kb, p[:m, :], wA[:, k, c0:c0 + m], hb[:, k, 1:GW + 1], k == 0, False, [wAb, hbb], [pb])
                mm(kb, p[:m, :], wB[:, k, c0:c0 + m], hb[:, k, 0:GW], False, k == 7, [wBb, hbb], [pb])
            return p, pb
        p, pb = proj(0, 64)
        actf(kb, r_s, p[:64, :], AF.Copy, [pb], [r_sb])
        p, pb = proj(64, 64)
        actf(kb, k_s, p[:64, :], AF.Copy, [pb], [k_sb])
        p, pb = proj(192, 32)
        actf(kb, th, p[:32, :], AF.Tanh, [pb], [thb])
        p, pb = proj(224, 32)
        actf(kb, xa_s, p[:32, :], AF.Copy, [pb], [xa_sb])
        p, pb = proj(256, 64)
        actf(kb, sgx, p[:64, :], AF.Sigmoid, [pb], [sgxb])
        mm(kb, pT[:64, :], w2s, th, True, True, [w2sb, thb], [pTb])
        actf(kb, ld, pT[:64, :], AF.Sigmoid, [pTb, vsb], [ldb], bias=W0)
        ts(kb, ld, ld, -0.6065306597126334, ALU.mult, [ldb], [ldb])
        mm(kb, pT[:64, :], a2s, xa_s, True, True, [a2sb, xa_sb], [pTb])
        actf(kb, a_, pT[:64, :], AF.Sigmoid, [pTb, vsb], [a_b], bias=A0)
        ts(kb, kkr, k_s, KK_, ALU.mult, [k_sb, vsb], [kkrb])
        actf(kb, sqk, kkr, AF.Square, [kkrb], [sqkb])
        mm(kb, pT[:64, :], ones64, sqk, True, True, [o64b, sqkb], [pTb])
        actf(kb, rn, pT[:64, :], AF.Sqrt, [pTb], [rnb])
        ts(kb, rn, rn, 1e-12, ALU.max, [rnb], [rnb])
        kb.op("dve", lambda e: e.reciprocal(out=rn, in_=rn), reads=[rnb], writes=[rnb])
        tt(kb, kk, kkr, rn, ALU.mult, [kkrb, rnb], [kkb])
        ts(kb, t1, a_, -1.0, ALU.add, [a_b, vsb], [t1b], s2=KA_, op1=ALU.mult)
        stt(kb, kp, t1, 1.0, k_s, ALU.add, ALU.mult, [t1b, k_sb], [kpb])
        tt(kb, b_, kk, a_, ALU.mult, [kkb, a_b], [b_b])
        kb.op("dve", lambda e: e.tensor_tensor_scan(out=L_, data0=rmask, data1=ld, initial=0.0,
                                                    op0=ALU.mult, op1=ALU.add),
              reads=[rmb, ldb], writes=[L_b])
        tt(kb, Lex, L_, ld, ALU.subtract, [L_b, ldb], [Lexb])
        actf(kb, eL, L_, AF.Exp, [L_b], [eLb])
        actf(kb, eLex, Lex, AF.Exp, [Lexb], [eLexb])
        actf(kb, enL, L_, AF.Exp, [L_b], [enLb], scale=-1.0)
        c4 = "p (c t) -> p c t"
        tt(kb, KR[:, :, 0, :], kk.rearrange(c4, t=CH), eLex.rearrange(c4, t=CH), ALU.mult, [kkb, eLexb], [KRb])
        tt(kb, KR[:, :, 1, :], r_s.rearrange(c4, t=CH), eL.rearrange(c4, t=CH), ALU.mult, [r_sb, eLb], [KRb])
        tt(kb, bt, b_, enL, ALU.mult, [b_b, enLb], [btb])
        tt(kb, kt, kp, enL, ALU.mult, [kpb, enLb], [ktb])
        stt(kb, rkp, r_s, RK_, kp, ALU.mult, ALU.mult, [r_sb, vsb, kpb], [rkpb])

        if dbg == 1:
            kb.dma("sp", yT[:, t0:t0 + GW], kt, reads=[ktb], writes=[yTb])
            continue
        for i in range(GW // CH):
            cs = slice(i * CH, (i + 1) * CH)
            KRi = KR[:, i, :, :].rearrange("p a t -> p (a t)")
            mm(kb, pA[:, 0:256], bt[:, cs], KRi, True, True, [btb, KRb], [pAb])
            mm(kb, pB[:, 0:256], kt[:, cs], KRi, True, True, [ktb, KRb], [pBb])
            mm(kb, pB[:, 256:384], KR[:, i, 0, :], bt[:, cs], True, True, [KRb, btb], [pBb])
            tt(kb, MAB, pA[:, 0:256], mU2, ALU.mult, [pAb, mU2b], [MABb])
            tt(kb, MAK, pB[:, 0:256], mU2, ALU.mult, [pBb, mU2b], [MAKb])
            tt(kb, Nn, pB[:, 256:384], mL, ALU.mult, [pBb, mLb], [Nnb])
            if dbg == 2:
                continue
            cM, cMb, cN, cNb = MAB[:, 0:128], MABb, Nn, Nnb
            for pi in range(3 if dbg < 10 else dbg - 10):
                mm(kb, pW[:, 0:128], cN, cM, True, True, [cNb, cMb], [pWb])
                mm(kb, pW[:, 128:256], cM, cN, True, True, [cNb, cMb], [pWb])
                mp, mpb = MP[pi]
                actf(kb, mp, pW[:, 0:256], AF.Copy, [pWb], [mpb])
                cM, cMb, cN, cNb = mp[:, 0:128], mpb, mp[:, 128:256], mpb
            mm(kb, pW[:, 0:128], cN, cM, True, True, [cNb, cMb], [pWb])
            actf(kb, M64, pW[:, 0:128], AF.Copy, [pWb], [M64b])
            if dbg == 3 or dbg >= 10:
                continue
            transp(kb, pT[:, 0:64], bt[:, cs], ident[0:64, 0:64], [btb, identb], [pTb])
            transp(kb, pT[:, 64:128], kt[:, cs], ident[0:64, 0:64], [ktb, identb], [pTb])
            kb.op("dve", lambda e: e.tensor_copy(out=BK, in_=pT[:, 0:128]), reads=[pTb], writes=[BKb])
            for k in range(8):
                mm(kb, pT[:, 128:192], hb[:, k, 1 + i * CH:1 + (i + 1) * CH], wA[:, k, 128:192], k == 0, False,
                   [hbb, wAb], [pTb])
                mm(kb, pT[:, 128:192], hb[:, k, i * CH:(i + 1) * CH], wB[:, k, 128:192], False, k == 7,
                   [hbb, wBb], [pTb])
            kb.op("dve", lambda e: e.tensor_copy(out=V_, in_=pT[:, 128:192]), reads=[pTb], writes=[V_b])
            if dbg == 4:
                continue
            S0, S0b = Ss[s_i % 2]
            S1, S1b = Ss[(s_i + 1) % 2]
            s_i += 1
            mm(kb, pX[:, 0:64], KR[:, i, 0, :], S0, True, False, [KRb, S0b], [pXb])
            mm(kb, pX[:, 0:64], MAK[:, 0:128], V_, False, True, [MAKb, V_b], [pXb])
            xi = 0
            X, Xb = Xs[xi]
            ts(kb, X, pX[:, 0:64], -1.0, ALU.mult, [pXb], [Xb])
            for (Mp_, Mpb_) in [(M64, M64b)] + [(MP[pi][0][:, 0:128], MP[pi][1]) for pi in (2, 1, 0)]:
                mm(kb, pX[:, 0:64], ident, X, True, False, [identb, Xb], [pXb])
                mm(kb, pX[:, 0:64], Mp_, X, False, True, [Mpb_, Xb], [pXb])
                xi += 1
                X, Xb = Xs[xi % 2]
                kb.op("dve", lambda e, X=X: e.tensor_copy(out=X, in_=pX[:, 0:64]), reads=[pXb], writes=[Xb])
            mm(kb, pX[:, 0:64], MAB[:, 0:128], X, True, True, [MABb, Xb], [pXb])
            tt(kb, U_, X, pX[:, 0:64], ALU.subtract, [Xb, pXb], [U_b])
            if dbg == 5:
                continue
            mm(kb, pY[:, 0:64], KR[:, i, 1, :], S0, True, False, [KRb, S0b], [pYb])
            mm(kb, pY[:, 0:64], MAB[:, 128:256], U_, False, False, [MABb, U_b], [pYb])
            mm(kb, pY[:, 0:64], MAK[:, 128:256], V_, False, True, [MAKb, V_b], [pYb])
            mm(kb, pY[:64, 64:128], ident[0:64, 0:64], S0, True, False, [identb, S0b], [pYb])
            mm(kb, pY[:64, 64:128], BK[:, 0:64], U_, False, False, [BKb, U_b], [pYb])
            mm(kb, pY[:64, 64:128], BK[:, 64:128], V_, False, True, [BKb, V_b], [pYb])
            ts(kb, S1, pY[:64, 64:128], eL[:, i * CH + CH - 1:i * CH + CH], ALU.mult, [pYb, eLb], [S1b])
            if dbg == 6:
                continue
            kb.op("dve", lambda e: e.bn_stats(out=st6, in_=pY[:, 0:64]), reads=[pYb], writes=[st6b])
            kb.op("dve", lambda e: e.bn_aggr(out=mv, in_=st6), reads=[st6b], writes=[mvb])
            actf(kb, rs_, mv[:, 1:2], AF.Sqrt, [mvb, gepsb], [rs_b], bias=geps, scale=1.0)
            kb.op("dve", lambda e: e.reciprocal(out=rs_, in_=rs_), reads=[rs_b], writes=[rs_b])
            ts(kb, yn, pY[:, 0:64], mv[:, 0:1], ALU.subtract, [pYb, mvb, rs_b], [ynb], s2=rs_, op1=ALU.mult)
            tt(kb, yn, yn, lgs, ALU.mult, [ynb, lgsb], [ynb])
            tt(kb, yn, yn, lbs, ALU.add, [ynb, lbsb], [ynb])
            mm(kb, pY[:, 128:130], rkp[:, cs], ones64[:, 0:2], True, True, [rkpb, o64b], [pYb])
            kb.op("dve", lambda e: e.tensor_copy(out=bs_, in_=pY[:, 128:130]), reads=[pYb], writes=[bs_b])
            stt(kb, yn, V_, bs_[:, 0:1], yn, ALU.mult, ALU.add, [V_b, bs_b, ynb], [ynb])
            mm(kb, pY[:, 192:256], sgx[:, cs], g2s, True, True, [sgxb, g2sb], [pYb])
            tt(kb, yo, yn, pY[:, 192:256], ALU.mult, [ynb, pYb], [yob])
            transp(kb, pT[:64, 256:384], yo, ident, [yob, identb], [pTb])
            actf(kb, yout[:, cs], pT[:64, 256:384], AF.Copy, [pTb], [youtb])
        kb.dma("sp", yT[:, t0:t0 + GW], yout, reads=[youtb], writes=[yTb])
    kb.finish([yTb])
    return nc


RW_OFF = 512 + 1304


def rwkv_inputs(P, l, h, hT):
    o = RW_OFF
    cols = np.concatenate([np.arange(o + h * 64, o + h * 64 + 64), np.arange(o + 256 + h * 64, o + 256 + h * 64 + 64),
                           np.arange(o + 512 + h * 64, o + 512 + h * 64 + 64), np.arange(o + 768, o + 896)])
    hs = slice(h * 64, (h + 1) * 64)
    vecs = np.zeros((64, 8), np.float32)
    vecs[:, 0] = P["rwkv_w0"][l][hs]
    vecs[:, 1] = P["rwkv_a0"][l][hs]
    vecs[:, 2] = P["rwkv_k_k"][l][hs]
    vecs[:, 3] = P["rwkv_k_a"][l][hs]
    vecs[:, 4] = P["rwkv_r_k"][l][h]
    return {"hT": hT,
            "w_r": np.ascontiguousarray(P["w_in"][l][:, cols]),
            "mu_bc": np.ascontiguousarray(np.broadcast_to(P["rwkv_mu"][l][cols - o][None, :], (128, RW_COLS))),
            "vecs": vecs,
            "w2": np.ascontiguousarray(P["rwkv_w2"][l][:, hs]), "a2": np.ascontiguousarray(P["rwkv_a2"][l][:, hs]),
            "g2": np.ascontiguousarray(P["rwkv_g2"][l][:, hs]),
            "lng_bc": np.ascontiguousarray(np.broadcast_to(P["rwkv_ln_g"][l][hs][None, :], (128, 64))),
            "lnb_bc": np.ascontiguousarray(np.broadcast_to(P["rwkv_ln_b"][l][hs][None, :], (128, 64)))}


NSLOT = 32
NKT = SEQ // 128
TINY = 1e-30


def build_stage_nsa(seq=None, dbg=0):
    seq = seq or SEQ
    TQn = seq // 4
    NS = TQn // 128
    NK = seq // 128
    NBLK = seq // 64
    NCH = seq // 16
    NNT = max(1, NCH // 128)
    NCP = NNT * 128
    BT = (NBLK + 127) // 128
    BR = min(128, NBLK)
    kb = KB()
    nc = kb.nc
    hT, hTb = kb.dram("hT", [D, seq], F32, "ExternalInput")
    hq, hqb_ = kb.dram("hTq", [D, TQn], F32, "ExternalInput")
    wq, wqb_ = kb.dram("w_q", [D, 512], F32, "ExternalInput")
    wkv, wkvb_ = kb.dram("w_kv", [D, 768], F32, "ExternalInput")
    wgt, wgtb_ = kb.dram("w_gt", [D, 24], F32, "ExternalInput")
    w1d, w1db = kb.dram("w1", [128, 2, 32, 64], F32, "ExternalInput")
    w2d, w2db = kb.dram("w2", [128, 2, 64], F32, "ExternalInput")
    ped, pedb = kb.dram("peT", [128, 2, 32], F32, "ExternalInput")
    tqd, tqdb = kb.dram("tq_bc", [128, TQn], F32, "ExternalInput")
    curd, curdb = kb.dram("curcol", [128, NS], F32, "ExternalInput")
    oT, oTb = kb.dram("nsaT", [512, TQn], F32, "ExternalOutput")

    identf, identfb = kb.sb("identf", [128, 128], F32)
    kpos, kposb = kb.sb("kpos", [128, NK], F32)
    cend, cendb = kb.sb("cend", [128, NNT], F32)
    jrow, jrowb = kb.sb("jrow", [128, NBLK], F32)
    E_, E_b = kb.sb("E_", [128, NNT, NBLK], BF16)
    NF = min(64, NK)
    F_, F_b = kb.sb("F_", [128, NF, 128], BF16)
    stA = ExitStack()
    dif, difb = kb.sbs(stA, "dif", [128, 128], F32)
    kb.op("pool", lambda e: e.iota(dif, pattern=[[1, 128]], base=0, channel_multiplier=-1,
                                   allow_small_or_imprecise_dtypes=True), writes=[difb])
    ts(kb, identf, dif, 0.0, ALU.is_equal, [difb], [identfb])
    kb.op("pool", lambda e: e.iota(kpos, pattern=[[128, NK]], base=0, channel_multiplier=1,
                                   allow_small_or_imprecise_dtypes=True), writes=[kposb])
    kb.op("pool", lambda e: e.iota(cend, pattern=[[2048, NNT]], base=31, channel_multiplier=16,
                                   allow_small_or_imprecise_dtypes=True), writes=[cendb])
    kb.op("pool", lambda e: e.iota(jrow, pattern=[[1, NBLK]], base=0, channel_multiplier=0,
                                   allow_small_or_imprecise_dtypes=True), writes=[jrowb])
    ev, evb = kb.sbs(stA, "ev", [128, NNT, NBLK], F32)
    kb.op("pool", lambda e: e.iota(ev, pattern=[[128, NNT], [-4, NBLK]], base=0, channel_multiplier=1,
                                   allow_small_or_imprecise_dtypes=True), writes=[evb])
    ev2, ev2b = kb.sbs(stA, "ev2", [128, NNT, NBLK], F32)
    ts(kb, ev2, ev, -1.0, ALU.is_ge, [evb], [ev2b])
    stt(kb, E_, ev, 3.0, ev2, ALU.is_le, ALU.mult, [evb, ev2b], [E_b])
    fv, fvb = kb.sbs(stA, "fv", [128, NF, 2, 64], F32)
    kb.op("pool", lambda e: e.iota(fv, pattern=[[-2, NF], [-1, 2], [0, 64]], base=0, channel_multiplier=1,
                                   allow_small_or_imprecise_dtypes=True), writes=[fvb])
    ts(kb, F_, fv.rearrange("p a b c -> p a (b c)"), 0.0, ALU.is_equal, [fvb], [F_b])
    kb.release([difb, evb, ev2b, fvb])
    stA.close()

    if dbg == 1:
        kb.finish([])
        return nc
    wq_s, wq_b = kb.sb("wq_s", [128, 8, 512], BF16)
    wgt_s, wgt_b = kb.sb("wgt_s", [128, 8, 24], BF16)
    wqv = wq.rearrange("(k p) n -> p k n", p=128)
    wkvv = wkv.rearrange("(k p) n -> p k n", p=128)
    for k in range(8):
        kb.dma("pool", wq_s[:, k, :], wqv[:, k, :], reads=[wqb_], writes=[wq_b])
    kb.dma("pool", wgt_s, wgt.rearrange("(k p) n -> p k n", p=128), reads=[wgtb_], writes=[wgt_b])
    tqsl = [kb.sb("tqs%d" % i, [128, 128], F32) for i in range(2)]
    curs, cursb = kb.sb("curs", [128, NS], F32)
    kb.dma("sp", curs, curd, reads=[curdb], writes=[cursb])
    kcm, kcmb = kb.sb("kcm", [128, NCP], BF16)
    vcm, vcmb = kb.sb("vcm", [128, NNT, 2, 65], BF16)
    kb.op("pool", lambda e: e.memset(vcm, 1.0), writes=[vcmb])
    kb.op("pool", lambda e: e.memset(kcm, 0.0), writes=[kcmb])
    hqt, hqtb = kb.sb("hq0", [128, 8, 128], BF16)
    QT, QTb = kb.sb("QT", [128, 2, 4, 128], BF16)
    kb.op("pool", lambda e: e.memset(QT, 0.0), writes=[QTb])
    gsb, gsbb = kb.sb("gsb", [128, 24], F32)
    Pt = [kb.sb("Pt%d" % i, [128, 4, 128], BF16) for i in range(2)]
    mk = [kb.sb("mk%d" % i, [128, 128], BF16) for i in range(2)]
    mk2 = [kb.sb("mk2%d" % i, [128, 128], F32) for i in range(2)]
    OT, OTb = kb.sb("OT", [128, 512], F32)
    oacc, oaccb = kb.sb("oacc", [128, 512], F32)
    rec, recb = kb.sb("rec", [128, 4], F32)
    coef, coefb = kb.sb("coef", [128, 4], F32)
    sel, selb = kb.sb("sel", [128, NBLK], F32)
    sc, scb = kb.sb("sc", [128, NBLK], F32)
    sc2, sc2b = kb.sb("sc2", [128, NBLK], F32)
    nf, nfb = kb.sb("nf", [128, NBLK], F32)
    frc, frcb = kb.sb("frc", [128, NBLK], F32)
    m8, m8b = kb.sb("m8", [128, 16], F32)
    bm, bmb = kb.sb("bm", [128, BT * 128], F32)
    nbT4, nbT4b = kb.sb("nbT4", [128, BT, 4, 128], BF16)
    kb.op("pool", lambda e: e.memset(nbT4, 0.0), writes=[nbT4b])
    kb.op("pool", lambda e: e.memset(bm, 0.0), writes=[bmb])
    pp = [kb.ps("pp%d" % i, [128, 512], F32) for i in range(8)]
    hTv = hT.rearrange("(k p) t -> p k t", p=128)
    GW = 512
    NG = seq // GW

    stB = ExitStack()
    wkv_s, wkv_b = kb.sbs(stB, "wkv_s", [128, 8, 256], BF16)
    for k in range(8):
        kb.dma("pool", wkv_s[:, k, :], wkvv[:, k, 0:256], reads=[wkvb_], writes=[wkv_b])
    hb, hbb = kb.sbs(stB, "hb", [128, 8, GW], BF16)
    w1s, w1sb = kb.sbs(stB, "w1s", [128, 2, 32, 64], BF16)
    kb.dma("pool", w1s[:, 0], w1d[:, 0], reads=[w1db], writes=[w1sb])
    kb.dma("pool", w1s[:, 1], w1d[:, 1], reads=[w1db], writes=[w1sb])
    w2s, w2sb = kb.sbs(stB, "w2s", [128, 2, 64], BF16)
    kb.dma("pool", w2s, w2d, reads=[w2db], writes=[w2sb])
    pes, pesb = kb.sbs(stB, "pes", [128, 2, 34], BF16)
    kb.op("pool", lambda e: e.memset(pes, 0.0), writes=[pesb])
    kb.dma("pool", pes[:, :, 0:32], ped, reads=[pedb], writes=[pesb])
    kcT, kcTb = kb.sbs(stB, "kcT", [128, 2, seq], BF16)
    gl, glb = kb.sbs(stB, "gl", [128, NCP], F32)
    gx, gxb = kb.sbs(stB, "gx", [128, NCP], F32)
    gbf, gbfb = kb.sbs(stB, "gbf", [128, NCP], BF16)
    bcol, bcolb = kb.sbs(stB, "bcol", [128, 2], F32)
    w2z, w2zb = kb.sbs(stB, "w2z", [128, 2, 64], BF16)
    if dbg == 11:
        kb.finish([])
        return nc
    for gi in range(NG):
        kb.dma("pool", hb, hTv[:, :, gi * GW:(gi + 1) * GW], reads=[hTb], writes=[hbb])
        for kv in range(2):
            p, pb = pp[(gi * 2 + kv) % 2]
            for k in range(8):
                mm(kb, p, wkv_s[:, k, kv * 128:(kv + 1) * 128], hb[:, k, :], k == 0, k == 7, [wkv_b, hbb], [pb])
            actf(kb, kcT[:, kv, gi * GW:(gi + 1) * GW], p, AF.Copy, [pb], [kcTb])
    if dbg == 12:
        kb.finish([])
        return nc
    kb.op("pool", lambda e: e.memset(gbf, 0.0), writes=[gbfb])
    NV = NCH - 1
    for kv in range(2):
        pbias, pbiasb = pp[2]
        for g in range(2):
            gs = slice(64 * g, 64 * g + 64)
            for l in range(32):
                mm(kb, pbias[gs, 0:2], w1s[gs, kv, l, :], pes[gs, kv, l:l + 2], l == 0, l == 31, [w1sb, pesb],
                   [pbiasb])
        kb.op("dve", lambda e: e.tensor_copy(out=bcol, in_=pbias[:, 0:2]), reads=[pbiasb], writes=[bcolb])
        if dbg == 13:
            kb.finish([])
            return nc
        n0 = 0
        ci = 0
        while n0 < NV:
            nn = min(512, NV - n0)
            pc, pcb = pp[3 + ci % 2]
            ci += 1
            for g in range(2):
                gs = slice(64 * g, 64 * g + 64)
                for l in range(32):
                    src = kcT[gs, kv, 16 * n0 + l:16 * n0 + l + 16 * (nn - 1) + 1:16]
                    mm(kb, pc[gs, 0:nn], w1s[gs, kv, l, :], src, l == 0, l == 31, [w1sb, kcTb], [pcb])
            actf(kb, gx[:, n0:n0 + nn], pc[:, 0:nn], AF.Identity, [pcb, bcolb], [gxb], bias=bcol[:, 0:1], scale=1.0)
            n0 += nn
        if dbg == 14:
            kb.finish([])
            return nc
        tt(kb, gl[:, 0:NV], gx[:, 0:NV], gx[:, 0:NV], ALU.mult, [gxb], [glb])
        ts(kb, gl[:, 0:NV], gl[:, 0:NV], 0.044715, ALU.mult, [glb], [glb], s2=1.0, op1=ALU.add)
        tt(kb, gl[:, 0:NV], gl[:, 0:NV], gx[:, 0:NV], ALU.mult, [glb, gxb], [glb])
        actf(kb, gl[:, 0:NV], gl[:, 0:NV], AF.Sigmoid, [glb], [glb], scale=1.5957691216057308)
        tt(kb, gbf[:, 0:NV], gl[:, 0:NV], gx[:, 0:NV], ALU.mult, [glb, gxb], [gbfb])
        if dbg == 15 or (dbg == 17 and kv == 1):
            kb.finish([])
            return nc
        if kv == 0:
            n0 = 0
            while n0 < NCP:
                nn = min(512, NCP - n0)
                pc, pcb = pp[5]
                for g in range(2):
                    gs = slice(64 * g, 64 * g + 64)
                    mm(kb, pc[gs, 0:nn], w2s[gs, 0, :], gbf[gs, n0:n0 + nn], True, True, [w2sb, gbfb], [pcb])
                actf(kb, kcm[:, n0:n0 + nn], pc[:, 0:nn], AF.Copy, [pcb], [kcmb])
                n0 += nn
            if dbg == 16:
                kb.finish([])
                return nc
        else:
            kb.op("dve", lambda e: e.memset(w2z, 0.0), writes=[w2zb])
            for g in range(2):
                gs = slice(64 * g, 64 * g + 64)
                kb.op("dve", lambda e, g=g, gs=gs: e.tensor_copy(out=w2z[gs, g, :], in_=w2s[gs, 1, :]),
                      reads=[w2sb], writes=[w2zb])
            for nt in range(NNT):
                pc, pcb = pp[5]
                for g in range(2):
                    mm(kb, pc[:, g * 64:(g + 1) * 64], gbf[:, nt * 128:(nt + 1) * 128], w2z[:, g, :], True, True,
                       [w2zb, gbfb], [pcb])
                actf(kb, vcm[:, nt, :, 0:64], pc[:, 0:128].rearrange("p (g d) -> p g d", g=2), AF.Copy,
                     [pcb], [vcmb])
    kb.release([wkv_b, hbb, w1sb, w2sb, pesb, kcTb, glb, gxb, gbfb, bcolb, w2zb])
    stB.close()

    if dbg == 2:
        kb.finish([])
        return nc
    ksT, ksTb = kb.sb("ksT", [128, seq], BF16)
    kwT, kwTb = kb.sb("kwT", [128, seq], BF16)
    vsA, vsAb = kb.sb("vsA", [128, NK, 2, 65], BF16)
    vwA, vwAb = kb.sb("vwA", [128, NK, 2, 65], BF16)
    kb.op("pool", lambda e: e.memset(vsA, 1.0), writes=[vsAb])
    kb.op("pool", lambda e: e.memset(vwA, 1.0), writes=[vwAb])
    stD = ExitStack()
    wk2, wk2b = kb.sbs(stD, "wk2", [128, 8, 512], BF16)
    for k in range(8):
        kb.dma("pool", wk2[:, k, :], wkvv[:, k, 256:768], reads=[wkvb_], writes=[wk2b])
    hb, hbb = kb.sbs(stD, "hb2", [128, 8, GW], BF16)
    for gi in range(NG):
        kb.dma("pool", hb, hTv[:, :, gi * GW:(gi + 1) * GW], reads=[hTb], writes=[hbb])
        for wi, dst, dstb in ((0, ksT, ksTb), (2, kwT, kwTb)):
            p, pb = pp[wi // 2]
            for k in range(8):
                mm(kb, p, wk2[:, k, wi * 128:(wi + 1) * 128], hb[:, k, :], k == 0, k == 7, [wk2b, hbb], [pb])
            actf(kb, dst[:, gi * GW:(gi + 1) * GW], p, AF.Copy, [pb], [dstb])
        for wi, dst, dstb in ((1, vsA, vsAb), (3, vwA, vwAb)):
            p, pb = pp[2 + wi // 2]
            for i4 in range(4):
                for k in range(8):
                    mm(kb, p[:, i4 * 128:(i4 + 1) * 128], hb[:, k, i4 * 128:(i4 + 1) * 128],
                       wk2[:, k, wi * 128:(wi + 1) * 128], k == 0, k == 7, [wk2b, hbb], [pb])
            kb.op("dve", lambda e, p=p, dst=dst, gi=gi: e.tensor_copy(
                out=dst[:, gi * 4:(gi + 1) * 4, :, 0:64],
                in_=p.rearrange("p (i g d) -> p i g d", i=4, g=2)), reads=[pb], writes=[dstb])
    kb.release([wk2b, hbb])
    stD.close()

    if dbg == 3:
        kb.finish([])
        return nc
    pS = [pp[0], pp[1]]
    pO, pOb = pp[2]
    pSel4 = [pp[3], pp[4], pp[5], pp[7]]
    pM, pMb = pp[5]
    pTk, pTkb = pp[6]
    pQ, pQb = pp[7]
    hqv = hq.rearrange("(k p) t -> p k t", p=128)
    oTv = oT.rearrange("(c p) t -> p c t", p=128)
    cnt = [0]

    def attend(g, tiles, kT, kTb_, vA, vAb_, br, first_branch, mask_fn=None, bias_fn=None, extra=None):
        gs = slice(64 * g, 64 * g + 64)
        nt_ = len(tiles)
        for idx, i in enumerate(tiles):
            c = cnt[0]
            cnt[0] += 1
            S, Sb = pS[c % 2]
            P, Pb = Pt[c % 2]
            hasb = bias_fn is not None
            mm(kb, S, kT[:, i * 128:(i + 1) * 128], QT[:, g, :, :].rearrange("p r q -> p (r q)"), True, not hasb,
               [kTb_, QTb], [Sb])
            if hasb:
                bias_fn(i, S, Sb)
            actf(kb, P.rearrange("p r q -> p (r q)"), S, AF.Exp, [Sb], [Pb])
            mres = mask_fn(i, c) if mask_fn is not None else None
            if mres is not None:
                m_ap, m_b = mres
                tt(kb, P, P, m_ap.rearrange("p (o q) -> p o q", o=1).to_broadcast([128, 4, 128]), ALU.mult,
                   [Pb, m_b], [Pb])
            mm(kb, pO[0:65, :], vA[:, i, g, :], P.rearrange("p r q -> p (r q)"), idx == 0, idx == nt_ - 1,
               [vAb_, Pb], [pOb])
            if extra is not None:
                extra(i, idx, P, Pb, nt_)
        actf(kb, OT[0:65, :], pO[0:65, :], AF.Copy, [pOb], [OTb])
        for r in range(4):
            transp(kb, pTk[:, r * 65:(r + 1) * 65], OT[0:65, r * 128:(r + 1) * 128], identf[0:65, 0:65],
                   [OTb, identfb], [pTkb])
        pv = pTk[:, 0:260].rearrange("p (r e) -> p r e", r=4)
        ts(kb, rec, pv[:, :, 64], TINY, ALU.max, [pTkb], [recb])
        kb.op("dve", lambda e: e.reciprocal(out=rec, in_=rec), reads=[recb], writes=[recb])
        gv = gsb.rearrange("p (g r t) -> p g r t", g=2, r=4)
        tt(kb, coef, rec, gv[:, g, :, br], ALU.mult, [recb, gsbb], [coefb])
        for r in range(4):
            dst = oacc[:, (g * 4 + r) * 64:(g * 4 + r + 1) * 64]
            if first_branch:
                ts(kb, dst, pv[:, r, 0:64], coef[:, r:r + 1], ALU.mult, [pTkb, coefb], [oaccb])
            else:
                stt(kb, dst, pv[:, r, 0:64], coef[:, r:r + 1], dst, ALU.mult, ALU.add, [pTkb, coefb, oaccb], [oaccb])

    for m in range(NS):
        kb.dma("pool", hqt, hqv[:, :, m * 128:(m + 1) * 128], reads=[hqb_], writes=[hqtb])
        for r in range(4):
            for k in range(8):
                mm(kb, pQ[:, r * 128:(r + 1) * 128], wq_s[:, k, r * 128:(r + 1) * 128], hqt[:, k, :], k == 0, k == 7,
                   [wq_b, hqtb], [pQb])
        for g in range(2):
            gs = slice(64 * g, 64 * g + 64)
            actf(kb, QT[gs, g, :, :].rearrange("p r q -> p (r q)"), pQ[gs, :], AF.Copy, [pQb], [QTb], scale=0.125)
        for k in range(8):
            mm(kb, pM[:, 0:24], hqt[:, k, :], wgt_s[:, k, :], k == 0, k == 7, [hqtb, wgt_b], [pMb])
        actf(kb, gsb, pM[:, 0:24], AF.Sigmoid, [pMb], [gsbb])
        tqs, tqsb = tqsl[m % 2]
        kb.dma("sp", tqs, tqd[:, m * 128:(m + 1) * 128], reads=[tqdb], writes=[tqsb])
        curc = curs[:, m:m + 1]
        ts(kb, nf, jrow, curc, ALU.is_le, [jrowb, cursb], [nfb])
        ts(kb, frc, jrow, curc, ALU.is_equal, [jrowb, cursb], [frcb])
        stt(kb, frc, jrow, 0.0, frc, ALU.is_equal, ALU.add, [jrowb, frcb], [frcb])
        ts(kb, sc2, jrow, curc, ALU.subtract, [jrowb, cursb], [sc2b], s2=-1.0, op1=ALU.is_equal)
        tt(kb, frc, frc, sc2, ALU.add, [frcb, sc2b], [frcb])
        ts(kb, frc, frc, 1.0, ALU.min, [frcb], [frcb])
        for g in range(2):
            def cmp_mask(i, c):
                mt, mtb = mk[c % 2]
                ts(kb, mt, tqs, cend[:, i:i + 1], ALU.is_ge, [tqsb, cendb], [mtb])
                return mt, mtb

            def cmp_extra(i, idx, P, Pb, n):
                for r in range(4):
                    ps_, psb = pSel4[r]
                    mm(kb, ps_[:, 0:NBLK], P[:, r, :], E_[:, i, :], idx == 0, idx == n - 1, [Pb, E_b], [psb])
            attend(g, list(range(NNT)), kcm, kcmb, vcm, vcmb, 0, True, mask_fn=cmp_mask, extra=cmp_extra)
            for r in range(4):
                ps_, psb = pSel4[r]
                src = ps_[:, 0:NBLK]
                if r == 0:
                    ts(kb, sel, src, rec[:, 0:1], ALU.mult, [psb, recb], [selb])
                else:
                    stt(kb, sel, src, rec[:, r:r + 1], sel, ALU.mult, ALU.add, [psb, recb, selb], [selb])
            stt(kb, sc, frc, 1.0e4, sel, ALU.mult, ALU.add, [frcb, selb], [scb])
            ts(kb, sc, sc, 1.0, ALU.add, [scb], [scb])
            tt(kb, sc, sc, nf, ALU.mult, [scb, nfb], [scb])
            ts(kb, sc, sc, -1.0, ALU.add, [scb], [scb])
            kb.op("dve", lambda e: e.max(out=m8[:, 0:8], in_=sc), reads=[scb], writes=[m8b])
            kb.op("dve", lambda e: e.match_replace(out=sc2, in_to_replace=m8[:, 0:8], in_values=sc, imm_value=-1e30),
                  reads=[scb, m8b], writes=[sc2b])
            kb.op("dve", lambda e: e.max(out=m8[:, 8:16], in_=sc2), reads=[sc2b], writes=[m8b])
            ts(kb, sc2, sc, m8[:, 15:16], ALU.is_ge, [scb, m8b], [sc2b])
            tt(kb, bm[:, 0:NBLK], sc2, nf, ALU.mult, [sc2b, nfb], [bmb])
            for t2 in range(BT):
                transp(kb, pM[0:BR, t2 * 128:(t2 + 1) * 128], bm[:, t2 * 128:t2 * 128 + BR], identf,
                       [bmb, identfb], [pMb])
            for t2 in range(BT):
                src = pM[0:BR, t2 * 128:(t2 + 1) * 128]
                ts(kb, nbT4[0:BR, t2, :, :], src.rearrange("p (o q) -> p o q", o=1).to_broadcast([BR, 4, 128]),
                   -1.0, ALU.add, [pMb], [nbT4b], s2=30000.0, op1=ALU.mult)

            def sel_bias(i, S, Sb):
                mm(kb, S, F_[:, i % 64, :], nbT4[:, i // 64, :, :].rearrange("p r q -> p (r q)"), False, True,
                   [F_b, nbT4b], [Sb])

            def sel_mask(i, c, m=m):
                if i < 4 * m:
                    return None
                mt, mtb = mk[c % 2]
                ts(kb, mt, tqs, kpos[:, i:i + 1], ALU.is_ge, [tqsb, kposb], [mtb])
                return mt, mtb
            attend(g, list(range(4 * m + 4)), ksT, ksTb, vsA, vsAb, 1, False, mask_fn=sel_mask, bias_fn=sel_bias)

            def win_mask(i, c):
                mt, mtb = mk[c % 2]
                m2, m2b = mk2[c % 2]
                ts(kb, m2, tqs, kpos[:, i:i + 1], ALU.subtract, [tqsb, kposb], [m2b])
                ts(kb, mt, m2, 0.0, ALU.is_ge, [m2b], [mtb])
                stt(kb, mt, m2, 512.0, mt, ALU.is_lt, ALU.mult, [m2b, mtb], [mtb])
                return mt, mtb
            attend(g, list(range(max(0, 4 * m - 4), 4 * m + 4)), kwT, kwTb, vwA, vwAb, 2, False, mask_fn=win_mask)
        for t4 in range(4):
            transp(kb, pQ[:, t4 * 128:(t4 + 1) * 128], oacc[:, t4 * 128:(t4 + 1) * 128], identf, [oaccb, identfb],
                   [pQb])
        actf(kb, OT, pQ, AF.Copy, [pQb], [OTb])
        kb.dma("sp", oTv[:, :, m * 128:(m + 1) * 128], OT.rearrange("p (c q) -> p c q", c=4), reads=[OTb],
               writes=[oTb])
    kb.finish([oTb])
    return nc


def nsa_inputs(P, l, j, hT_full, hT_q, seq=None):
    seq = seq or SEQ
    ns = seq // 4 // 128
    o = 512
    qcols = np.array([o + (g * 4 + r) * 64 + d for r in range(4) for g in range(2) for d in range(64)])
    w1 = np.stack([P["cmp_w1_k"][l].reshape(32, 64, 64), P["cmp_w1_v"][l].reshape(32, 64, 64)], 0)
    w1 = np.ascontiguousarray(np.tile(w1.transpose(2, 0, 1, 3), (2, 1, 1, 1)))
    w2 = np.stack([P["cmp_w2_k"][l], P["cmp_w2_v"][l]], 1)
    w2 = np.ascontiguousarray(np.tile(w2, (2, 1, 1)))
    pe = np.stack([P["cmp_pe_k"][l].T, P["cmp_pe_v"][l].T], 1)
    pe = np.ascontiguousarray(np.tile(pe, (2, 1, 1)))
    tq = np.concatenate([np.arange(128) + 128 * (4 * m + j) for m in range(ns)]).astype(np.float32)
    cur = np.stack([(np.arange(128) + 128 * (4 * m + j)) // 64 for m in range(ns)], 1).astype(np.float32)
    return {"hT": hT_full, "hTq": hT_q,
            "w_q": np.ascontiguousarray(P["w_in"][l][:, qcols]),
            "w_kv": np.ascontiguousarray(P["w_in"][l][:, o + 512:o + 512 + 768]),
            "w_gt": np.ascontiguousarray(P["w_in"][l][:, o + 1280:o + 1304]),
            "w1": w1.astype(np.float32), "w2": w2.astype(np.float32), "peT": pe.astype(np.float32),
            "tq_bc": np.ascontiguousarray(np.broadcast_to(tq[None, :], (128, seq // 4))),
            "curcol": np.ascontiguousarray(cur)}


_PROGS = {}


def _prog(name, fn):
    if name not in _PROGS:
        _PROGS[name] = fn()
    return _PROGS[name]


def _run(nc, maps):
    return run_bass_kernel_spmd(nc, maps, core_ids=list(range(NCORES))).results


def kernel(**P):
    P = {k: np.asarray(v) for k, v in P.items()}
    x = P["x"]
    xT = [np.ascontiguousarray(x[c // 4, (c % 4) * TQ:(c % 4 + 1) * TQ].T) for c in range(NCORES)]
    res = _run(_prog("a", build_stage_a), [{"xT": xT[c], "g": _cols(P["norm_mix"][0], 8)} for c in range(NCORES)])
    hT = [r["hT"] for r in res]
    out = None
    for l in range(DEPTH):
        hfull = [np.ascontiguousarray(np.concatenate(hT[b * 4:(b + 1) * 4], axis=1)) for b in range(NB)]
        maps = []
        for c in range(NCORES):
            b, j = c // 4, c % 4
            hh = np.zeros((D, HALO + TQ), np.float32)
            hh[:, HALO:] = hT[c]
            if j > 0:
                hh[:, :HALO] = hT[c - 1][:, TQ - HALO:]
            maps.append(conv_inputs(P, l, hh))
        rconv = _run(_prog("conv", build_stage_conv), maps)
        rrw = _run(_prog("rwkv", build_stage_rwkv),
                   [rwkv_inputs(P, l, c % 4, hfull[c // 4]) for c in range(NCORES)])
        maps = []
        for c in range(NCORES):
            b, j = c // 4, c % 4
            hq = hfull[b].reshape(D, NSLOT, 4, 128)[:, :, j, :].reshape(D, TQ)
            maps.append(nsa_inputs(P, l, j, hfull[b], np.ascontiguousarray(hq)))
        rnsa = _run(_prog("nsa", build_stage_nsa), maps)
        mixfull = []
        for b in range(NB):
            mf = np.empty((D, SEQ), np.float32)
            for j in range(4):
                c = b * 4 + j
                mf[0:256, j * TQ:(j + 1) * TQ] = rconv[c]["convT"]
                mf[256:768].reshape(512, NSLOT, 4, 128)[:, :, j, :] = rnsa[c]["nsaT"].reshape(512, NSLOT, 128)
                mf[768 + j * 64:768 + (j + 1) * 64, :] = rrw[c]["yT"]
            mixfull.append(mf)
        maps = []
        for c in range(NCORES):
            b, j = c // 4, c % 4
            mixT = np.ascontiguousarray(mixfull[b][:, j * TQ:(j + 1) * TQ])
            if j > 0:
                xh = np.ascontiguousarray(xT[c - 1][:, TQ - 2:])
                mh = np.ascontiguousarray(mixfull[b][:, j * TQ - 2:j * TQ])
            else:
                xh = np.zeros((D, 2), np.float32)
                mh = np.zeros((D, 2), np.float32)
            gnext = P["norm_mix"][l + 1] if l + 1 < DEPTH else P["norm_final"]
            maps.append({"xT": xT[c], "mixT": mixT, "xh": xh, "mixh": mh,
                         "w_out": np.ascontiguousarray(P["w_out"][l]), "w_gate": np.ascontiguousarray(P["ffn_w_gate"][l]),
                         "w_up": np.ascontiguousarray(P["ffn_w_up"][l]), "w_down": np.ascontiguousarray(P["ffn_w_down"][l]),
                         "g_ffn": _cols(P["norm_ffn"][l], 8), "g_next": _cols(gnext, 8),
                         "conv_w": np.ascontiguousarray(P["ffn_conv_w"][l].reshape(3, DFF // 128, 128).transpose(2, 1, 0)),
                         "conv_b": _cols(P["ffn_conv_b"][l], DFF // 128)})
        rc = _run(_prog("c", build_stage_c), maps)
        xT = [r["xoT"] for r in rc]
        hT = [r["hoT"] for r in rc]
    out = np.empty((NB, SEQ, D), np.float32)
    for c in range(NCORES):
        out[c // 4, (c % 4) * TQ:(c % 4 + 1) * TQ] = hT[c].T
    return out
```

```python
import numpy as np
from contextlib import ExitStack
import concourse.bass as bass
import concourse.mybir as mybir
from concourse.bass_utils import run_bass_kernel_spmd

F32 = mybir.dt.float32
BF16 = mybir.dt.bfloat16
AF = mybir.ActivationFunctionType
ALU = mybir.AluOpType
AX = mybir.AxisListType

D = 1024
SEQ = 16384
NB = 2
DEPTH = 4
DFF = 2816
NCORES = 8
TQ = SEQ // 4
RMS_EPS = 1e-6


class Buf:
    __slots__ = ("w", "r", "name")

    def __init__(self, name=""):
        self.w = None
        self.r = {}
        self.name = name


class KB:
    NDMA = 6

    def __init__(self):
        self.nc = bass.Bass("TRN2", target_bir_lowering=False)
        nc = self.nc
        self.E = {"pe": nc.tensor, "act": nc.scalar, "dve": nc.vector,
                  "pool": nc.gpsimd, "sp": nc.sync}
        self.sems = {}
        self.cnt = {}
        for e in ("pe", "act", "dve", "pool"):
            self.sems[e] = nc.alloc_semaphore("c_" + e)
            self.cnt[e] = 0
        self.dq = {}
        for q in ("sp", "pool", "act"):
            ring = []
            for i in range(self.NDMA):
                key = "d_%s_%d" % (q, i)
                self.sems[key] = nc.alloc_semaphore(key)
                self.cnt[key] = 0
                ring.append(key)
            self.dq[q] = [ring, 0]
        self.known = {e: {} for e in self.E}
        self.nbuf = 0
        self.out_tokens = []

    def sb(self, name, shape, dt):
        t = self.nc.alloc_sbuf_tensor(name, list(shape), dt)
        return t.ap(), Buf(name)

    def sbs(self, stack, name, shape, dt):
        t = stack.enter_context(self.nc.sbuf_tensor(name, list(shape), dt))
        return t.ap(), Buf(name)

    def release(self, bufs):
        for eng in self.E:
            self._waits(eng, [], bufs, skip_own=(eng == "pe"))

    def ps(self, name, shape, dt=F32):
        t = self.nc.alloc_psum_tensor(name, list(shape), dt)
        return t.ap(), Buf(name)

    def dram(self, name, shape, dt, kind):
        return self.nc.dram_tensor(name, list(shape), dt, kind=kind).ap(), Buf(name)

    def _waits(self, eng, reads, writes, skip_own=False):
        deps = {}
        for b in reads:
            if b.w is not None:
                k, v = b.w
                if deps.get(k, 0) < v:
                    deps[k] = v
        for b in writes:
            if b.w is not None:
                k, v = b.w
                if deps.get(k, 0) < v:
                    deps[k] = v
            for k, v in b.r.items():
                if deps.get(k, 0) < v:
                    deps[k] = v
        kn = self.known[eng]
        for k, v in deps.items():
            if skip_own and k == eng:
                continue
            if kn.get(k, 0) < v:
                self.E[eng].wait_ge(self.sems[k], v)
                kn[k] = v

    def _record(self, tok, reads, writes):
        k, v = tok
        for b in writes:
            b.w = tok
            b.r = {}
        for b in reads:
            if b.r.get(k, 0) < v:
                b.r[k] = v

    def op(self, eng, fn, reads=(), writes=()):
        self._waits(eng, reads, writes, skip_own=(eng == "pe"))
        inst = fn(self.E[eng])
        self.cnt[eng] += 1
        inst.then_inc(self.sems[eng], 1)
        tok = (eng, self.cnt[eng])
        self._record(tok, reads, writes)
        return tok

    def dma(self, q, out, in_, reads=(), writes=(), **kw):
        ring, i = self.dq[q]
        key = ring[i % self.NDMA]
        self.dq[q][1] = i + 1
        kn = self.known[q]
        if kn.get(key, 0) < self.cnt[key]:
            self.E[q].wait_ge(self.sems[key], self.cnt[key])
            kn[key] = self.cnt[key]
        self._waits(q, reads, writes)
        inst = self.E[q].dma_start(out=out, in_=in_, **kw)
        self.cnt[key] += 16
        inst.then_inc(self.sems[key], 16)
        tok = (key, self.cnt[key])
        self._record(tok, reads, writes)
        return tok

    def finish(self, out_bufs):
        deps = {}
        for b in out_bufs:
            if b.w is not None:
                k, v = b.w
                deps[k] = max(deps.get(k, 0), v)
        for q in self.dq:
            for key in self.dq[q][0]:
                if self.cnt[key] > 0:
                    deps[key] = max(deps.get(key, 0), self.cnt[key])
        for k, v in deps.items():
            self.nc.sync.wait_ge(self.sems[k], v)


def emit_rmsnorm(kb, xt, xb, gcol, ones, W, sq, sqb, ssum, ssb, pst, pstb, rstd, rsb, outs, outb,
                 out_dt_scale=None):
    nc = kb.nc
    for k in range(8):
        if k == 0:
            kb.op("act", lambda e: e.activation(out=ssum[:, :W], in_=xt[:, 0, :W], func=AF.Square),
                  reads=[xb], writes=[ssb])
        else:
            kb.op("act", lambda e, k=k: e.activation(out=sq[:, :W], in_=xt[:, k, :W], func=AF.Square),
                  reads=[xb], writes=[sqb])
            kb.op("dve", lambda e: e.tensor_tensor(out=ssum[:, :W], in0=ssum[:, :W], in1=sq[:, :W], op=ALU.add),
                  reads=[sqb, ssb], writes=[ssb])
    kb.op("pe", lambda e: e.matmul(pst[:, :W], lhsT=ones, rhs=ssum[:, :W], start=True, stop=True),
          reads=[ssb], writes=[pstb])
    kb.op("act", lambda e: e.activation(out=rstd[:, :W], in_=pst[:, :W], func=AF.Sqrt,
                                        bias=kb.eps_col, scale=1.0 / D),
          reads=[pstb], writes=[rsb])
    kb.op("dve", lambda e: e.reciprocal(out=rstd[:, :W], in_=rstd[:, :W]), reads=[rsb], writes=[rsb])
    for k in range(8):
        kb.op("dve", lambda e, k=k: e.scalar_tensor_tensor(out=outs[:, k, :W], in0=xt[:, k, :W],
                                                           scalar=gcol[:, k:k + 1], in1=rstd[:, :W],
                                                           op0=ALU.mult, op1=ALU.mult),
              reads=[xb, rsb], writes=[outb])


def build_stage_a():
    kb = KB()
    nc = kb.nc
    W = 512
    xT, xTb = kb.dram("xT", [D, TQ], F32, "ExternalInput")
    g, gb = kb.dram("g", [128, 8], F32, "ExternalInput")
    hT, hTb = kb.dram("hT", [D, TQ], F32, "ExternalOutput")
    gcol, gcb = kb.sb("gcol", [128, 8], F32)
    ones, onb = kb.sb("ones", [128, 128], F32)
    eps, epb = kb.sb("eps", [128, 1], F32)
    kb.eps_col = eps
    kb.op("dve", lambda e: e.memset(ones, 1.0), writes=[onb])
    kb.op("dve", lambda e: e.memset(eps, RMS_EPS), writes=[epb])
    kb.dma("sp", gcol, g, reads=[gb], writes=[gcb])
    xs = [kb.sb("x%d" % i, [128, 8, W], F32) for i in range(2)]
    os_ = [kb.sb("o%d" % i, [128, 8, W], F32) for i in range(2)]
    sq, sqb = kb.sb("sq", [128, W], F32)
    ssum, ssb = kb.sb("ssum", [128, W], F32)
    rstd, rsb = kb.sb("rstd", [128, W], F32)
    pst, pstb = kb.ps("pst", [128, W], F32)
    xTv = xT.rearrange("(k p) t -> p k t", p=128)
    hTv = hT.rearrange("(k p) t -> p k t", p=128)
    for c in range(TQ // W):
        xt, xb = xs[c % 2]
        ot, ob = os_[c % 2]
        kb.dma("sp", xt, xTv[:, :, c * W:(c + 1) * W], reads=[xTb], writes=[xb])
        emit_rmsnorm(kb, xt, xb, gcol, ones, W, sq, sqb, ssum, ssb, pst, pstb, rstd, rsb, ot, ob)
        kb.dma("sp", hTv[:, :, c * W:(c + 1) * W], ot, reads=[ob], writes=[hTb])
    kb.finish([hTb])
    return nc


def build_stage_c():
    kb = KB()
    nc = kb.nc
    W = 256
    NT = TQ
    NF = DFF // 128
    xT, xTb = kb.dram("xT", [D, NT], F32, "ExternalInput")
    mixT, mixTb = kb.dram("mixT", [D, NT], F32, "ExternalInput")
    xh, xhb = kb.dram("xh", [D, 2], F32, "ExternalInput")
    mixh, mixhb = kb.dram("mixh", [D, 2], F32, "ExternalInput")
    w_out, wob = kb.dram("w_out", [D, D], F32, "ExternalInput")
    w_gate, wgb_ = kb.dram("w_gate", [D, DFF], F32, "ExternalInput")
    w_up, wub_ = kb.dram("w_up", [D, DFF], F32, "ExternalInput")
    w_down, wdb_ = kb.dram("w_down", [DFF, D], F32, "ExternalInput")
    gf, gfb = kb.dram("g_ffn", [128, 8], F32, "ExternalInput")
    gn, gnb = kb.dram("g_next", [128, 8], F32, "ExternalInput")
    cw, cwb = kb.dram("conv_w", [128, NF, 3], F32, "ExternalInput")
    cb, cbb = kb.dram("conv_b", [128, NF], F32, "ExternalInput")
    xo, xob = kb.dram("xoT", [D, NT], F32, "ExternalOutput")
    ho, hob = kb.dram("hoT", [D, NT], F32, "ExternalOutput")

    wo_s, wo_b = kb.sb("wo_s", [128, 8, D], BF16)
    wg_s, wg_b = kb.sb("wg_s", [128, 8, DFF], BF16)
    wu_s, wu_b = kb.sb("wu_s", [128, 8, DFF], BF16)
    wd_s, wd_b = kb.sb("wd_s", [128, NF, D], BF16)
    wov = w_out.rearrange("(k p) n -> p k n", p=128)
    wgv = w_gate.rearrange("(k p) n -> p k n", p=128)
    wuv = w_up.rearrange("(k p) n -> p k n", p=128)
    wdv = w_down.rearrange("(k p) n -> p k n", p=128)
    for k in range(8):
        kb.dma("pool", wo_s[:, k, :], wov[:, k, :], reads=[wob], writes=[wo_b])
    for k in range(8):
        for hh in range(2):
            sl = slice(hh * (DFF // 2), (hh + 1) * (DFF // 2))
            kb.dma("pool", wg_s[:, k, sl], wgv[:, k, sl], reads=[wgb_], writes=[wg_b])
            kb.dma("pool", wu_s[:, k, sl], wuv[:, k, sl], reads=[wub_], writes=[wu_b])
    for f in range(NF):
        kb.dma("pool", wd_s[:, f, :], wdv[:, f, :], reads=[wdb_], writes=[wd_b])

    gfc, gfcb = kb.sb("gfc", [128, 8], F32)
    gnc, gncb = kb.sb("gnc", [128, 8], F32)
    cws, cwsb = kb.sb("cws", [128, NF, 3], F32)
    cbs, cbsb = kb.sb("cbs", [128, NF], F32)
    kb.dma("sp", gfc, gf, reads=[gfb], writes=[gfcb])
    kb.dma("sp", gnc, gn, reads=[gnb], writes=[gncb])
    kb.dma("sp", cws, cw, reads=[cwb], writes=[cwsb])
    kb.dma("sp", cbs, cb, reads=[cbb], writes=[cbsb])
    ones, onb = kb.sb("ones", [128, 128], F32)
    eps, epb = kb.sb("eps", [128, 1], F32)
    kb.eps_col = eps
    kb.op("dve", lambda e: e.memset(ones, 1.0), writes=[onb])
    kb.op("dve", lambda e: e.memset(eps, RMS_EPS), writes=[epb])

    xs = [kb.sb("x%d" % i, [128, 8, W], F32) for i in range(2)]
    ms = [kb.sb("m%d" % i, [128, 8, W], BF16) for i in range(2)]
    xmid, xmb = kb.sb("xmid", [128, 8, W], F32)
    h2, h2b = kb.sb("h2", [128, 8, W], BF16)
    actb, actbb = kb.sb("actb", [128, NF, W], BF16)
    xnew, xnb = xmid, xmb
    carry, carb = kb.sb("carry", [128, NF, 2], F32)
    aext = [kb.sb("aext%d" % i, [128, W + 2], F32) for i in range(2)]
    cs = [kb.sb("cs%d" % i, [128, W], F32) for i in range(2)]
    sg = [kb.sb("sg%d" % i, [128, W], F32) for i in range(2)]
    sq, sqb = kb.sb("sq", [128, W], F32)
    ssum, ssb = kb.sb("ssum", [128, W], F32)
    rstd, rsb = kb.sb("rstd", [128, W], F32)
    pa = [kb.ps("pa%d" % i, [128, 512], F32) for i in range(2)]
    pu = [kb.ps("pu%d" % i, [128, 512], F32) for i in range(2)]
    pacc = [kb.ps("pacc%d" % i, [128, 512], F32) for i in range(2)]
    pst, pstb = kb.ps("pst", [128, 512], F32)

    xTv = xT.rearrange("(k p) t -> p k t", p=128)
    mTv = mixT.rearrange("(k p) t -> p k t", p=128)
    xhv = xh.rearrange("(k p) t -> p k t", p=128)
    mhv = mixh.rearrange("(k p) t -> p k t", p=128)
    xov = xo.rearrange("(k p) t -> p k t", p=128)
    hov = ho.rearrange("(k p) t -> p k t", p=128)

    nchunks = NT // W
    acc_i = 0
    for c in range(-1, nchunks):
        halo = c < 0
        Wc = 2 if halo else W
        xt, xb = xs[c % 2]
        mt, mb = ms[c % 2]
        if halo:
            kb.dma("sp", xt[:, :, :Wc], xhv, reads=[xhb], writes=[xb])
            kb.dma("pool", mt[:, :, :Wc], mhv, reads=[mixhb], writes=[mb])
        else:
            kb.dma("sp", xt, xTv[:, :, c * W:(c + 1) * W], reads=[xTb], writes=[xb])
            kb.dma("pool", mt, mTv[:, :, c * W:(c + 1) * W], reads=[mixTb], writes=[mb])
        for m in range(8):
            p, pb = pacc[acc_i % 2]
            acc_i += 1
            for k in range(8):
                kb.op("pe", lambda e, k=k, m=m, p=p: e.matmul(p[:, :Wc], lhsT=wo_s[:, k, m * 128:(m + 1) * 128],
                                                           rhs=mt[:, k, :Wc], start=(k == 0), stop=(k == 7)),
                      reads=[wo_b, mb], writes=[pb])
            kb.op("dve", lambda e, m=m, p=p: e.tensor_tensor(out=xmid[:, m, :Wc], in0=p[:, :Wc], in1=xt[:, m, :Wc],
                                                          op=ALU.add),
                  reads=[pb, xb], writes=[xmb])
        emit_rmsnorm(kb, xmid, xmb, gfc, ones, Wc, sq, sqb, ssum, ssb, pst, pstb, rstd, rsb, h2, h2b)
        for f in range(NF):
            a, ab = pa[f % 2]
            u, ub = pu[f % 2]
            ae, aeb = aext[f % 2]
            ct, ctb = cs[f % 2]
            st, stb = sg[f % 2]
            for k in range(8):
                kb.op("pe", lambda e, k=k, f=f, a=a: e.matmul(a[:, :Wc], lhsT=wg_s[:, k, f * 128:(f + 1) * 128],
                                                           rhs=h2[:, k, :Wc], start=(k == 0), stop=(k == 7)),
                      reads=[wg_b, h2b], writes=[ab])
            if halo:
                kb.op("act", lambda e, f=f, a=a: e.activation(out=carry[:, f, :], in_=a[:, :2], func=AF.Copy),
                      reads=[ab], writes=[carb])
                continue
            for k in range(8):
                kb.op("pe", lambda e, k=k, f=f, u=u: e.matmul(u[:, :Wc], lhsT=wu_s[:, k, f * 128:(f + 1) * 128],
                                                           rhs=h2[:, k, :Wc], start=(k == 0), stop=(k == 7)),
                      reads=[wu_b, h2b], writes=[ub])
            kb.op("pool", lambda e, f=f, ae=ae: e.tensor_copy(out=ae[:, 0:2], in_=carry[:, f, :]),
                  reads=[carb], writes=[aeb])
            kb.op("act", lambda e, a=a, ae=ae: e.activation(out=ae[:, 2:2 + W], in_=a[:, :W], func=AF.Copy),
                  reads=[ab], writes=[aeb])
            kb.op("act", lambda e, f=f, a=a, ct=ct: e.activation(out=ct, in_=a[:, :W], func=AF.Identity,
                                                              scale=cws[:, f, 2:3], bias=cbs[:, f:f + 1]),
                  reads=[ab, cwsb, cbsb], writes=[ctb])
            kb.op("pool", lambda e, f=f, ae=ae: e.tensor_copy(out=carry[:, f, :], in_=ae[:, W:W + 2]),
                  reads=[aeb], writes=[carb])
            kb.op("dve", lambda e, f=f, ae=ae, ct=ct: e.scalar_tensor_tensor(out=ct, in0=ae[:, 1:1 + W],
                                                                          scalar=cws[:, f, 1:2], in1=ct,
                                                                          op0=ALU.mult, op1=ALU.add),
                  reads=[aeb, ctb, cwsb], writes=[ctb])
            kb.op("dve", lambda e, f=f, ae=ae, ct=ct: e.scalar_tensor_tensor(out=ct, in0=ae[:, 0:W],
                                                                          scalar=cws[:, f, 0:1], in1=ct,
                                                                          op0=ALU.mult, op1=ALU.add),
                  reads=[aeb, ctb, cwsb], writes=[ctb])
            kb.op("act", lambda e, ct=ct, st=st: e.activation(out=st, in_=ct, func=AF.Silu),
                  reads=[ctb], writes=[stb])
            kb.op("dve", lambda e, f=f, st=st, u=u: e.tensor_tensor(out=actb[:, f, :], in0=u[:, :W], in1=st,
                                                                 op=ALU.mult),
                  reads=[ub, stb], writes=[actbb])
        if halo:
            continue
        for m in range(8):
            p, pb = pacc[acc_i % 2]
            acc_i += 1
            for f in range(NF):
                kb.op("pe", lambda e, f=f, m=m, p=p: e.matmul(p[:, :W], lhsT=wd_s[:, f, m * 128:(m + 1) * 128],
                                                           rhs=actb[:, f, :], start=(f == 0), stop=(f == NF - 1)),
                      reads=[wd_b, actbb], writes=[pb])
            kb.op("dve", lambda e, m=m, p=p: e.tensor_tensor(out=xnew[:, m, :], in0=p[:, :W], in1=xmid[:, m, :],
                                                          op=ALU.add),
                  reads=[pb, xmb], writes=[xnb])
        kb.dma("sp", xov[:, :, c * W:(c + 1) * W], xnew, reads=[xnb], writes=[xob])
        hout, houtb = xt, xb
        emit_rmsnorm(kb, xnew, xnb, gnc, ones, W, sq, sqb, ssum, ssb, pst, pstb, rstd, rsb, hout, houtb)
        kb.dma("sp", hov[:, :, c * W:(c + 1) * W], hout, reads=[houtb], writes=[hob])
    kb.finish([xob, hob])
    return nc


def mm(kb, out, lhsT, rhs, start, stop, reads, writes):
    return kb.op("pe", lambda e: e.matmul(out, lhsT=lhsT, rhs=rhs, start=start, stop=stop),
                 reads=reads, writes=writes)


def actf(kb, out, in_, func, reads, writes, **kw):
    return kb.op("act", lambda e: e.activation(out=out, in_=in_, func=func, **kw), reads=reads, writes=writes)


def tt(kb, out, in0, in1, op, reads, writes, eng="dve"):
    return kb.op(eng, lambda e: e.tensor_tensor(out=out, in0=in0, in1=in1, op=op), reads=reads, writes=writes)


def ts(kb, out, in0, s1, op0, reads, writes, s2=None, op1=None, eng="dve"):
    if op1 is None:
        return kb.op(eng, lambda e: e.tensor_scalar(out=out, in0=in0, scalar1=s1, scalar2=None, op0=op0),
                     reads=reads, writes=writes)
    return kb.op(eng, lambda e: e.tensor_scalar(out=out, in0=in0, scalar1=s1, scalar2=s2, op0=op0, op1=op1),
                 reads=reads, writes=writes)


def stt(kb, out, in0, scalar, in1, op0, op1, reads, writes):
    return kb.op("dve", lambda e: e.scalar_tensor_tensor(out=out, in0=in0, scalar=scalar, in1=in1,
                                                        op0=op0, op1=op1), reads=reads, writes=writes)


def transp(kb, out, in_, ident, reads, writes):
    return kb.op("pe", lambda e: e.transpose(out, in_, ident), reads=reads, writes=writes)


CONV_K = 31
HALO = 32
LN_EPS = 1e-5


def build_stage_conv():
    kb = KB()
    nc = kb.nc
    W = 512
    NT = TQ
    hT, hTb = kb.dram("hT", [D, HALO + NT], F32, "ExternalInput")
    wc, wcb = kb.dram("w_conv", [D, 512], F32, "ExternalInput")
    dww, dwwb = kb.dram("dw_w", [128, 2, CONV_K], F32, "ExternalInput")
    dwb, dwbb = kb.dram("dw_b", [128, 2], F32, "ExternalInput")
    lng, lngb = kb.dram("ln_g", [128, 2], F32, "ExternalInput")
    lnb, lnbb = kb.dram("ln_b", [128, 2], F32, "ExternalInput")
    oT, oTb = kb.dram("convT", [256, NT], F32, "ExternalOutput")

    wcs, wcsb = kb.sb("wcs", [128, 8, 512], BF16)
    wcv = wc.rearrange("(k p) n -> p k n", p=128)
    for k in range(8):
        kb.dma("pool", wcs[:, k, :], wcv[:, k, :], reads=[wcb], writes=[wcsb])
    dws, dwsb = kb.sb("dws", [128, 2, CONV_K], F32)
    dbs, dbsb = kb.sb("dbs", [128, 2], F32)
    lgs, lgsb = kb.sb("lgs", [128, 2], F32)
    lbs, lbsb = kb.sb("lbs", [128, 2], F32)
    kb.dma("sp", dws, dww, reads=[dwwb], writes=[dwsb])
    kb.dma("sp", dbs, dwb, reads=[dwbb], writes=[dbsb])
    kb.dma("sp", lgs, lng, reads=[lngb], writes=[lgsb])
    kb.dma("sp", lbs, lnb, reads=[lnbb], writes=[lbsb])
    ones, onb = kb.sb("ones", [128, 128], F32)
    eps, epb = kb.sb("eps", [128, 1], F32)
    kb.op("dve", lambda e: e.memset(ones, 1.0), writes=[onb])
    kb.op("dve", lambda e: e.memset(eps, LN_EPS), writes=[epb])

    Y, Yb = kb.sb("Y", [128, 2, HALO + NT], F32)
    acc, accb = kb.sb("acc", [128, 2, NT], F32)
    hbs = [kb.sb("hb%d" % i, [128, 8, W], BF16) for i in range(2)]
    sgs = [kb.sb("sg%d" % i, [128, W], F32) for i in range(2)]
    pu = [kb.ps("pu%d" % i, [128, 512], F32) for i in range(2)]
    pg = [kb.ps("pg%d" % i, [128, 512], F32) for i in range(2)]
    hTv = hT.rearrange("(k p) t -> p k t", p=128)
    chunks = [(0, HALO)] + [(HALO + i * W, W) for i in range(NT // W)]
    it = 0
    for ci, (c0, Wc) in enumerate(chunks):
        hb, hbb = hbs[ci % 2]
        kb.dma("pool", hb[:, :, :Wc], hTv[:, :, c0:c0 + Wc], reads=[hTb], writes=[hbb])
        for ct in range(2):
            u, ub = pu[it % 2]
            g, gb = pg[it % 2]
            sg, sgb = sgs[it % 2]
            it += 1
            for k in range(8):
                mm(kb, u[:, :Wc], wcs[:, k, ct * 128:(ct + 1) * 128], hb[:, k, :Wc], k == 0, k == 7,
                   [wcsb, hbb], [ub])
            for k in range(8):
                mm(kb, g[:, :Wc], wcs[:, k, 256 + ct * 128:256 + (ct + 1) * 128], hb[:, k, :Wc], k == 0, k == 7,
                   [wcsb, hbb], [gb])
            actf(kb, sg[:, :Wc], g[:, :Wc], AF.Sigmoid, [gb], [sgb])
            tt(kb, Y[:, ct, c0:c0 + Wc], u[:, :Wc], sg[:, :Wc], ALU.mult, [ub, sgb], [Yb])
    HW = NT // 2
    accbs = [[Buf("acc%d%d" % (ct, h)) for h in range(2)] for ct in range(2)]
    for ct in range(2):
        for h in range(2):
            o0 = h * HW
            ab = accbs[ct][h]
            actf(kb, acc[:, ct, o0:o0 + HW], Y[:, ct, HALO + o0:HALO + o0 + HW], AF.Identity, [Yb, dwsb, dbsb], [ab],
                 scale=dws[:, ct, CONV_K - 1:CONV_K], bias=dbs[:, ct:ct + 1])
            for k in range(CONV_K - 1):
                s0 = HALO - (CONV_K - 1) + k + o0
                stt(kb, acc[:, ct, o0:o0 + HW], Y[:, ct, s0:s0 + HW], dws[:, ct, k:k + 1], acc[:, ct, o0:o0 + HW],
                    ALU.mult, ALU.add, [Yb, dwsb, ab], [ab])
    sq = [kb.sb("sq%d" % i, [128, W], F32) for i in range(2)]
    mean, meanb = kb.sb("mean", [128, W], F32)
    msq, msqb = kb.sb("msq", [128, W], F32)
    var, varb = kb.sb("var", [128, W], F32)
    z = [kb.sb("z%d" % i, [128, W], F32) for i in range(2)]
    ot = [kb.sb("ot%d" % i, [128, W], F32) for i in range(2)]
    p1, p1b = pu[0]
    p2, p2b = pu[1]
    oTv = oT.rearrange("(c p) t -> p c t", p=128)
    for n in range(NT // W):
        sl = slice(n * W, (n + 1) * W)
        ab = [accbs[0][n * W // HW], accbs[1][n * W // HW]]
        for ct in range(2):
            mm(kb, p1, ones, acc[:, ct, sl], ct == 0, ct == 1, [onb, ab[ct]], [p1b])
        for ct in range(2):
            s, sb_ = sq[ct]
            actf(kb, s, acc[:, ct, sl], AF.Square, [ab[ct]], [sb_])
            mm(kb, p2, ones, s, ct == 0, ct == 1, [onb, sb_], [p2b])
        ts(kb, mean, p1, 1.0 / 256, ALU.mult, [p1b], [meanb])
        tt(kb, msq, mean, mean, ALU.mult, [meanb], [msqb])
        stt(kb, var, p2, 1.0 / 256, msq, ALU.mult, ALU.subtract, [p2b, msqb], [varb])
        actf(kb, var, var, AF.Sqrt, [varb, epb], [varb], bias=eps, scale=1.0)
        kb.op("dve", lambda e: e.reciprocal(out=var, in_=var), reads=[varb], writes=[varb])
        for ct in range(2):
            zt, zb = z[ct]
            o, ob = ot[ct]
            tt(kb, zt, acc[:, ct, sl], mean, ALU.subtract, [ab[ct], meanb], [zb])
            tt(kb, zt, zt, var, ALU.mult, [zb, varb], [zb])
            actf(kb, o, zt, AF.Silu, [zb, lgsb, lbsb], [ob], scale=lgs[:, ct:ct + 1], bias=lbs[:, ct:ct + 1])
            kb.dma("sp", oTv[:, ct, sl], o, reads=[ob], writes=[oTb])
    kb.finish([oTb])
    return nc


def _cols(v, n):
    return np.ascontiguousarray(np.asarray(v, np.float32).reshape(n, 128).T)


def conv_inputs(P, l, hT):
    return {"hT": hT,
            "w_conv": np.ascontiguousarray(P["w_in"][l][:, 0:512]),
            "dw_w": np.ascontiguousarray(P["conv_dw_w"][l].reshape(CONV_K, 2, 128).transpose(2, 1, 0)),
            "dw_b": _cols(P["conv_dw_b"][l], 2), "ln_g": _cols(P["conv_ln_g"][l], 2),
            "ln_b": _cols(P["conv_ln_b"][l], 2)}


RW_COLS = 320
GN_EPS = 64e-5
CH = 128
RW_NTOK = SEQ


def build_stage_rwkv(ntok=None, dbg=0):
    ntok = ntok or RW_NTOK
    kb = KB()
    nc = kb.nc
    GW = 512
    hT, hTb = kb.dram("hT", [D, ntok], F32, "ExternalInput")
    wr, wrb = kb.dram("w_r", [D, RW_COLS], F32, "ExternalInput")
    mu, mub = kb.dram("mu_bc", [128, RW_COLS], F32, "ExternalInput")
    vecs, vecsb = kb.dram("vecs", [64, 8], F32, "ExternalInput")
    w2d, w2db = kb.dram("w2", [32, 64], F32, "ExternalInput")
    a2d, a2db = kb.dram("a2", [32, 64], F32, "ExternalInput")
    g2d, g2db = kb.dram("g2", [64, 64], F32, "ExternalInput")
    lgd, lgdb = kb.dram("lng_bc", [128, 64], F32, "ExternalInput")
    lbd, lbdb = kb.dram("lnb_bc", [128, 64], F32, "ExternalInput")
    yT, yTb = kb.dram("yT", [64, ntok], F32, "ExternalOutput")

    dif, difb = kb.sb("dif", [128, 128], F32)
    kb.op("pool", lambda e: e.iota(dif, pattern=[[1, 128]], base=0, channel_multiplier=-1, allow_small_or_imprecise_dtypes=True), writes=[difb])
    ident, identb = kb.sb("ident", [128, 128], F32)
    mU2, mU2b = kb.sb("mU2", [128, 256], F32)
    mL, mLb = kb.sb("mL", [128, 128], F32)
    ts(kb, ident, dif, 0.0, ALU.is_equal, [difb], [identb])
    ts(kb, mU2[:, 0:128], dif, 0.0, ALU.is_gt, [difb], [mU2b])
    ts(kb, mU2[:, 128:256], dif, 0.0, ALU.is_ge, [difb], [mU2b])
    ts(kb, mL, dif, 0.0, ALU.is_lt, [difb], [mLb])
    ones64, o64b = kb.sb("ones64", [64, 64], F32)
    kb.op("dve", lambda e: e.memset(ones64, 1.0), writes=[o64b])
    rmask, rmb = kb.sb("rmask", [64, GW], F32)
    kb.op("dve", lambda e: e.memset(rmask, 1.0), writes=[rmb])
    for i in range(GW // CH):
        kb.op("dve", lambda e, i=i: e.memset(rmask[:, i * CH:i * CH + 1], 0.0), writes=[rmb])
    geps, gepsb = kb.sb("geps", [128, 1], F32)
    kb.op("dve", lambda e: e.memset(geps, GN_EPS), writes=[gepsb])

    wf, wfb = kb.sb("wf", [128, 8, RW_COLS], F32)
    mus, musb = kb.sb("mus", [128, RW_COLS], F32)
    wA, wAb = kb.sb("wA", [128, 8, RW_COLS], BF16)
    wB, wBb = kb.sb("wB", [128, 8, RW_COLS], BF16)
    kb.dma("sp", wf, wr.rearrange("(k p) n -> p k n", p=128), reads=[wrb], writes=[wfb])
    kb.dma("sp", mus, mu, reads=[mub], writes=[musb])
    tmpw, tmpwb = kb.sb("tmpw", [128, RW_COLS], F32)
    for k in range(8):
        tt(kb, tmpw, wf[:, k, :], mus, ALU.mult, [wfb, musb], [tmpwb])
        kb.op("dve", lambda e, k=k: e.tensor_copy(out=wB[:, k, :], in_=tmpw), reads=[tmpwb], writes=[wBb])
        tt(kb, wA[:, k, :], wf[:, k, :], tmpw, ALU.subtract, [wfb, tmpwb], [wAb])
    vs_, vsb = kb.sb("vecs_s", [64, 8], F32)
    w2s, w2sb = kb.sb("w2s", [32, 64], F32)
    a2s, a2sb = kb.sb("a2s", [32, 64], F32)
    g2s, g2sb = kb.sb("g2s", [64, 64], F32)
    lgs, lgsb = kb.sb("lgs", [128, 64], F32)
    lbs, lbsb = kb.sb("lbs", [128, 64], F32)
    kb.dma("sp", vs_, vecs, reads=[vecsb], writes=[vsb])
    kb.dma("sp", w2s, w2d, reads=[w2db], writes=[w2sb])
    kb.dma("sp", a2s, a2d, reads=[a2db], writes=[a2sb])
    kb.dma("sp", g2s, g2d, reads=[g2db], writes=[g2sb])
    kb.dma("sp", lgs, lgd, reads=[lgdb], writes=[lgsb])
    kb.dma("sp", lbs, lbd, reads=[lbdb], writes=[lbsb])
    W0, A0, KK_, KA_, RK_ = (vs_[:, i:i + 1] for i in range(5))

    hbs = [kb.sb("hb%d" % i, [128, 8, GW + 1], BF16) for i in range(2)]

    def t64(name, p=64, w=GW):
        return kb.sb(name, [p, w], F32)
    r_s, r_sb = t64("r_s")
    k_s, k_sb = t64("k_s")
    th, thb = t64("th", 32)
    xa_s, xa_sb = t64("xa_s", 32)
    sgx, sgxb = t64("sgx")
    ld, ldb = t64("ld")
    a_, a_b = t64("a_")
    kkr, kkrb = t64("kkr")
    sqk, sqkb = t64("sqk")
    rn, rnb = t64("rn")
    kk, kkb = t64("kk")
    t1, t1b = t64("t1")
    kp, kpb = t64("kp")
    b_, b_b = t64("b_")
    L_, L_b = t64("L_")
    Lex, Lexb = t64("Lex")
    eL, eLb = t64("eL")
    eLex, eLexb = t64("eLex")
    enL, enLb = t64("enL")
    KR, KRb = kb.sb("KR", [64, GW // CH, 2, CH], F32)
    bt, btb = t64("bt")
    kt, ktb = t64("kt")
    rkp, rkpb = t64("rkp")
    yout, youtb = t64("yout")
    MABs = [kb.sb("MAB%d" % i, [128, 256], F32) for i in range(2)]
    MAKs = [kb.sb("MAK%d" % i, [128, 256], F32) for i in range(2)]
    Nns = [kb.sb("Nn%d" % i, [128, 128], F32) for i in range(2)]
    MPs = [[kb.sb("MP%d_%d" % (j, i), [128, 256], F32) for i in range(3)] for j in range(2)]
    M64s = [kb.sb("M16_%d" % i, [128, 128], F32) for i in range(2)]
    BKs = [kb.sb("BK%d" % i, [128, 128], F32) for i in range(2)]
    Vs = [kb.sb("V_%d" % i, [128, 64], F32) for i in range(2)]
    Xs = [kb.sb("X%d" % i, [128, 64], F32) for i in range(2)]
    U_, U_b = kb.sb("U_", [128, 64], F32)
    Ss = [kb.sb("S%d" % i, [64, 64], F32) for i in range(2)]
    st6, st6b = kb.sb("st6", [128, 6], F32)
    mv, mvb = kb.sb("mv", [128, 2], F32)
    rs_, rs_b = kb.sb("rs_", [128, 1], F32)
    yn, ynb = kb.sb("yn", [128, 64], F32)
    bs_, bs_b = kb.sb("bs_", [128, 2], F32)
    yo, yob = kb.sb("yo", [128, 64], F32)

    pj = [kb.ps("pj%d" % i, [128, 512], F32) for i in range(2)]
    pA, pAb = kb.ps("pA", [128, 512], F32)
    pB, pBb = kb.ps("pB", [128, 512], F32)
    pW, pWb = kb.ps("pW", [128, 512], F32)
    pX, pXb = kb.ps("pX", [128, 512], F32)
    pY, pYb = kb.ps("pY", [128, 512], F32)
    pT, pTb = kb.ps("pT", [128, 512], F32)

    kb.op("dve", lambda e: e.memset(Ss[0][0], 0.0), writes=[Ss[0][1]])
    s_i = 0
    hTv = hT.rearrange("(k p) t -> p k t", p=128)
    pji = 0
    for g in range(ntok // GW):
        t0 = g * GW
        hb, hbb = hbs[g % 2]
        if g == 0:
            kb.op("pool", lambda e: e.memset(hb[:, :, 0:1], 0.0), writes=[hbb])
            kb.dma("pool", hb[:, :, 1:GW + 1], hTv[:, :, 0:GW], reads=[hTb], writes=[hbb])
        else:
            kb.dma("pool", hb, hTv[:, :, t0 - 1:t0 + GW], reads=[hTb], writes=[hbb])

        def proj(c0, m):
            nonlocal pji
            p, pb = pj[pji % 2]
            pji += 1
            for k in range(8):
                mm(kb, p[:m, :], wA[:, k, c0:c0 + m], hb[:, k, 1:GW + 1], k == 0, False, [wAb, hbb], [pb])
                mm(kb, p[:m, :], wB[:, k, c0:c0 + m], hb[:, k, 0:GW], False, k == 7, [wBb, hbb], [pb])
            return p, pb
        p, pb = proj(0, 64)
        actf(kb, r_s, p[:64, :], AF.Copy, [pb], [r_sb])
        p, pb = proj(64, 64)
        actf(kb, k_s, p[:64, :], AF.Copy, [pb], [k_sb])
        p, pb = proj(192, 32)
        actf(kb, th, p[:32, :], AF.Tanh, [pb], [thb])
        p, pb = proj(224, 32)
        actf(kb, xa_s, p[:32, :], AF.Copy, [pb], [xa_sb])
        p, pb = proj(256, 64)
        actf(kb, sgx, p[:64, :], AF.Sigmoid, [pb], [sgxb])
        mm(kb, pT[:64, :], w2s, th, True, True, [w2sb, thb], [pTb])
        actf(kb, ld, pT[:64, :], AF.Sigmoid, [pTb, vsb], [ldb], bias=W0)
        ts(kb, ld, ld, -0.6065306597126334, ALU.mult, [ldb], [ldb])
        mm(kb, pT[:64, :], a2s, xa_s, True, True, [a2sb, xa_sb], [pTb])
        actf(kb, a_, pT[:64, :], AF.Sigmoid, [pTb, vsb], [a_b], bias=A0)
        ts(kb, kkr, k_s, KK_, ALU.mult, [k_sb, vsb], [kkrb])
        actf(kb, sqk, kkr, AF.Square, [kkrb], [sqkb])
        mm(kb, pT[:64, :], ones64, sqk, True, True, [o64b, sqkb], [pTb])
        actf(kb, rn, pT[:64, :], AF.Sqrt, [pTb], [rnb])
        ts(kb, rn, rn, 1e-12, ALU.max, [rnb], [rnb])
        kb.op("dve", lambda e: e.reciprocal(out=rn, in_=rn), reads=[rnb], writes=[rnb])
        tt(kb, kk, kkr, rn, ALU.mult, [kkrb, rnb], [kkb])
        ts(kb, t1, a_, -1.0, ALU.add, [a_b, vsb], [t1b], s2=KA_, op1=ALU.mult)
        stt(kb, kp, t1, 1.0, k_s, ALU.add, ALU.mult, [t1b, k_sb], [kpb])
        tt(kb, b_, kk, a_, ALU.mult, [kkb, a_b], [b_b])
        kb.op("dve", lambda e: e.tensor_tensor_scan(out=L_, data0=rmask, data1=ld, initial=0.0,
                                                    op0=ALU.mult, op1=ALU.add),
              reads=[rmb, ldb], writes=[L_b])
        tt(kb, Lex, L_, ld, ALU.subtract, [L_b, ldb], [Lexb])
        actf(kb, eL, L_, AF.Exp, [L_b], [eLb])
        actf(kb, eLex, Lex, AF.Exp, [Lexb], [eLexb])
        actf(kb, enL, L_, AF.Exp, [L_b], [enLb], scale=-1.0)
        c4 = "p (c t) -> p c t"
        tt(kb, KR[:, :, 0, :], kk.rearrange(c4, t=CH), eLex.rearrange(c4, t=CH), ALU.mult, [kkb, eLexb], [KRb])
        tt(kb, KR[:, :, 1, :], r_s.rearrange(c4, t=CH), eL.rearrange(c4, t=CH), ALU.mult, [r_sb, eLb], [KRb])
        tt(kb, bt, b_, enL, ALU.mult, [b_b, enLb], [btb])
        tt(kb, kt, kp, enL, ALU.mult, [kpb, enLb], [ktb])
        stt(kb, rkp, r_s, RK_, kp, ALU.mult, ALU.mult, [r_sb, vsb, kpb], [rkpb])

        if dbg == 1:
            kb.dma("sp", yT[:, t0:t0 + GW], kt, reads=[ktb], writes=[yTb])
            continue
        def pre(i, q):
            cs = slice(i * CH, (i + 1) * CH)
            MAB, MABb = MABs[q]
            MAK, MAKb = MAKs[q]
            Nn, Nnb = Nns[q]
            MPq = MPs[q]
            M64, M64b = M64s[q]
            BK, BKb = BKs[q]
            V_, V_b = Vs[q]
            KRi = KR[:, i, :, :].rearrange("p a t -> p (a t)")
            mm(kb, pA[:, 0:256], bt[:, cs], KRi, True, True, [btb, KRb], [pAb])
            mm(kb, pB[:, 0:256], kt[:, cs], KRi, True, True, [ktb, KRb], [pBb])
            mm(kb, pB[:, 256:384], KR[:, i, 0, :], bt[:, cs], True, True, [KRb, btb], [pBb])
            yield
            tt(kb, MAB, pA[:, 0:256], mU2, ALU.mult, [pAb, mU2b], [MABb])
            tt(kb, MAK, pB[:, 0:256], mU2, ALU.mult, [pBb, mU2b], [MAKb])
            tt(kb, Nn, pB[:, 256:384], mL, ALU.mult, [pBb, mLb], [Nnb])
            yield
            cM, cMb, cN, cNb = MAB[:, 0:128], MABb, Nn, Nnb
            for pi in range(3):
                mm(kb, pW[:, 0:128], cN, cM, True, True, [cNb, cMb], [pWb])
                mm(kb, pW[:, 128:256], cM, cN, True, True, [cNb, cMb], [pWb])
                yield
                mp, mpb = MPq[pi]
                actf(kb, mp, pW[:, 0:256], AF.Copy, [pWb], [mpb])
                cM, cMb, cN, cNb = mp[:, 0:128], mpb, mp[:, 128:256], mpb
                yield
            mm(kb, pW[:, 0:128], cN, cM, True, True, [cNb, cMb], [pWb])
            transp(kb, pT[:, 0:64], bt[:, cs], ident[0:64, 0:64], [btb, identb], [pTb])
            transp(kb, pT[:, 64:128], kt[:, cs], ident[0:64, 0:64], [ktb, identb], [pTb])
            yield
            actf(kb, M64, pW[:, 0:128], AF.Copy, [pWb], [M64b])
            kb.op("dve", lambda e: e.tensor_copy(out=BK, in_=pT[:, 0:128]), reads=[pTb], writes=[BKb])
            for k in range(8):
                mm(kb, pT[:, 128:192], hb[:, k, 1 + i * CH:1 + (i + 1) * CH], wA[:, k, 128:192], k == 0, False,
                   [hbb, wAb], [pTb])
                mm(kb, pT[:, 128:192], hb[:, k, i * CH:(i + 1) * CH], wB[:, k, 128:192], False, k == 7,
                   [hbb, wBb], [pTb])
            yield
            kb.op("dve", lambda e: e.tensor_copy(out=V_, in_=pT[:, 128:192]), reads=[pTb], writes=[V_b])
            yield

        def dep(i, q):
            nonlocal s_i
            cs = slice(i * CH, (i + 1) * CH)
            MAB, MABb = MABs[q]
            MAK, MAKb = MAKs[q]
            MPq = MPs[q]
            M64, M64b = M64s[q]
            BK, BKb = BKs[q]
            V_, V_b = Vs[q]
            S0, S0b = Ss[s_i % 2]
            S1, S1b = Ss[(s_i + 1) % 2]
            s_i += 1
            mm(kb, pX[:, 0:64], KR[:, i, 0, :], S0, True, False, [KRb, S0b], [pXb])
            mm(kb, pX[:, 0:64], MAK[:, 0:128], V_, False, True, [MAKb, V_b], [pXb])
            yield
            xi = 0
            X, Xb = Xs[xi]
            ts(kb, X, pX[:, 0:64], -1.0, ALU.mult, [pXb], [Xb])
            yield
            for (Mp_, Mpb_) in [(M64, M64b)] + [(MPq[pi][0][:, 0:128], MPq[pi][1]) for pi in (2, 1, 0)]:
                mm(kb, pX[:, 0:64], ident, X, True, False, [identb, Xb], [pXb])
                mm(kb, pX[:, 0:64], Mp_, X, False, True, [Mpb_, Xb], [pXb])
                yield
                xi += 1
                X, Xb = Xs[xi % 2]
                kb.op("dve", lambda e, X=X: e.tensor_copy(out=X, in_=pX[:, 0:64]), reads=[pXb], writes=[Xb])
                yield
            mm(kb, pX[:, 0:64], MAB[:, 0:128], X, True, True, [MABb, Xb], [pXb])
            yield
            tt(kb, U_, X, pX[:, 0:64], ALU.subtract, [Xb, pXb], [U_b])
            yield
            mm(kb, pY[:64, 64:128], ident[0:64, 0:64], S0, True, False, [identb, S0b], [pYb])
            mm(kb, pY[:64, 64:128], BK[:, 0:64], U_, False, False, [BKb, U_b], [pYb])
            mm(kb, pY[:64, 64:128], BK[:, 64:128], V_, False, True, [BKb, V_b], [pYb])
            mm(kb, pY[:, 0:64], KR[:, i, 1, :], S0, True, False, [KRb, S0b], [pYb])
            mm(kb, pY[:, 0:64], MAB[:, 128:256], U_, False, False, [MABb, U_b], [pYb])
            mm(kb, pY[:, 0:64], MAK[:, 128:256], V_, False, True, [MAKb, V_b], [pYb])
            yield
            ts(kb, S1, pY[:64, 64:128], eL[:, i * CH + CH - 1:i * CH + CH], ALU.mult, [pYb, eLb], [S1b])
            kb.op("dve", lambda e: e.bn_stats(out=st6, in_=pY[:, 0:64]), reads=[pYb], writes=[st6b])
            kb.op("dve", lambda e: e.bn_aggr(out=mv, in_=st6), reads=[st6b], writes=[mvb])
            p0, p0b = pj[0]
            p1, p1b = pj[1]
            mm(kb, p0[:, 0:2], rkp[:, cs], ones64[:, 0:2], True, True, [rkpb, o64b], [p0b])
            mm(kb, p0[:, 64:128], sgx[:, cs], g2s, True, True, [sgxb, g2sb], [p0b])
            yield
            actf(kb, rs_, mv[:, 1:2], AF.Sqrt, [mvb, gepsb], [rs_b], bias=geps, scale=1.0)
            kb.op("dve", lambda e: e.tensor_copy(out=bs_, in_=p0[:, 0:2]), reads=[p0b], writes=[bs_b])
            yield
            kb.op("dve", lambda e: e.reciprocal(out=rs_, in_=rs_), reads=[rs_b], writes=[rs_b])
            ts(kb, yn, pY[:, 0:64], mv[:, 0:1], ALU.subtract, [pYb, mvb, rs_b], [ynb], s2=rs_, op1=ALU.mult)
            tt(kb, yn, yn, lgs, ALU.mult, [ynb, lgsb], [ynb])
            tt(kb, yn, yn, lbs, ALU.add, [ynb, lbsb], [ynb])
            stt(kb, yn, V_, bs_[:, 0:1], yn, ALU.mult, ALU.add, [V_b, bs_b, ynb], [ynb])
            tt(kb, yo, yn, p0[:, 64:128], ALU.mult, [ynb, p0b], [yob])
            yield
            transp(kb, p1[:64, 0:128], yo, ident, [yob, identb], [p1b])
            yield
            actf(kb, yout[:, cs], p1[:64, 0:128], AF.Copy, [p1b], [youtb])
            yield

        def zip_run(gens):
            gens = [g_ for g_ in gens if g_ is not None]
            while gens:
                for g_ in list(gens):
                    try:
                        next(g_)
                    except StopIteration:
                        gens.remove(g_)

        nchunk = GW // CH
        pending = None
        for i in range(nchunk):
            q = (g * nchunk + i) % 2
            zip_run([pre(i, q), pending])
            pending = dep(i, q)
        zip_run([pending])
        kb.dma("sp", yT[:, t0:t0 + GW], yout, reads=[youtb], writes=[yTb])
    kb.finish([yTb])
    return nc


RW_OFF = 512 + 1304


def rwkv_inputs(P, l, h, hT):
    o = RW_OFF
    cols = np.concatenate([np.arange(o + h * 64, o + h * 64 + 64), np.arange(o + 256 + h * 64, o + 256 + h * 64 + 64),
                           np.arange(o + 512 + h * 64, o + 512 + h * 64 + 64), np.arange(o + 768, o + 896)])
    hs = slice(h * 64, (h + 1) * 64)
    vecs = np.zeros((64, 8), np.float32)
    vecs[:, 0] = P["rwkv_w0"][l][hs]
    vecs[:, 1] = P["rwkv_a0"][l][hs]
    vecs[:, 2] = P["rwkv_k_k"][l][hs]
    vecs[:, 3] = P["rwkv_k_a"][l][hs]
    vecs[:, 4] = P["rwkv_r_k"][l][h]
    return {"hT": hT,
            "w_r": np.ascontiguousarray(P["w_in"][l][:, cols]),
            "mu_bc": np.ascontiguousarray(np.broadcast_to(P["rwkv_mu"][l][cols - o][None, :], (128, RW_COLS))),
            "vecs": vecs,
            "w2": np.ascontiguousarray(P["rwkv_w2"][l][:, hs]), "a2": np.ascontiguousarray(P["rwkv_a2"][l][:, hs]),
            "g2": np.ascontiguousarray(P["rwkv_g2"][l][:, hs]),
            "lng_bc": np.ascontiguousarray(np.broadcast_to(P["rwkv_ln_g"][l][hs][None, :], (128, 64))),
            "lnb_bc": np.ascontiguousarray(np.broadcast_to(P["rwkv_ln_b"][l][hs][None, :], (128, 64)))}


NSLOT = 32
NKT = SEQ // 128
TINY = 1e-30


def build_stage_nsa(seq=None, dbg=0):
    seq = seq or SEQ
    TQn = seq // 4
    NS = TQn // 128
    NK = seq // 128
    NBLK = seq // 64
    NCH = seq // 16
    NNT = max(1, NCH // 128)
    NCP = NNT * 128
    BT = (NBLK + 127) // 128
    BR = min(128, NBLK)
    kb = KB()
    nc = kb.nc
    hT, hTb = kb.dram("hT", [D, seq], F32, "ExternalInput")
    hq, hqb_ = kb.dram("hTq", [D, TQn], F32, "ExternalInput")
    wq, wqb_ = kb.dram("w_q", [D, 512], F32, "ExternalInput")
    wkv, wkvb_ = kb.dram("w_kv", [D, 768], F32, "ExternalInput")
    wgt, wgtb_ = kb.dram("w_gt", [D, 24], F32, "ExternalInput")
    w1d, w1db = kb.dram("w1", [128, 2, 32, 64], F32, "ExternalInput")
    w2d, w2db = kb.dram("w2", [128, 2, 64], F32, "ExternalInput")
    ped, pedb = kb.dram("peT", [128, 2, 32], F32, "ExternalInput")
    tqd, tqdb = kb.dram("tq_bc", [128, TQn], F32, "ExternalInput")
    curd, curdb = kb.dram("curcol", [128, NS], F32, "ExternalInput")
    oT, oTb = kb.dram("nsaT", [512, TQn], F32, "ExternalOutput")

    identf, identfb = kb.sb("identf", [128, 128], F32)
    kpos, kposb = kb.sb("kpos", [128, NK], F32)
    cend, cendb = kb.sb("cend", [128, NNT], F32)
    jrow, jrowb = kb.sb("jrow", [128, NBLK], F32)
    E_, E_b = kb.sb("E_", [128, NNT, NBLK], BF16)
    NF = min(64, NK)
    F_, F_b = kb.sb("F_", [128, NF, 128], BF16)
    stA = ExitStack()
    dif, difb = kb.sbs(stA, "dif", [128, 128], F32)
    kb.op("pool", lambda e: e.iota(dif, pattern=[[1, 128]], base=0, channel_multiplier=-1,
                                   allow_small_or_imprecise_dtypes=True), writes=[difb])
    ts(kb, identf, dif, 0.0, ALU.is_equal, [difb], [identfb])
    kb.op("pool", lambda e: e.iota(kpos, pattern=[[128, NK]], base=0, channel_multiplier=1,
                                   allow_small_or_imprecise_dtypes=True), writes=[kposb])
    kb.op("pool", lambda e: e.iota(cend, pattern=[[2048, NNT]], base=31, channel_multiplier=16,
                                   allow_small_or_imprecise_dtypes=True), writes=[cendb])
    kb.op("pool", lambda e: e.iota(jrow, pattern=[[1, NBLK]], base=0, channel_multiplier=0,
                                   allow_small_or_imprecise_dtypes=True), writes=[jrowb])
    ev, evb = kb.sbs(stA, "ev", [128, NNT, NBLK], F32)
    kb.op("pool", lambda e: e.iota(ev, pattern=[[128, NNT], [-4, NBLK]], base=0, channel_multiplier=1,
                                   allow_small_or_imprecise_dtypes=True), writes=[evb])
    ev2, ev2b = kb.sbs(stA, "ev2", [128, NNT, NBLK], F32)
    ts(kb, ev2, ev, -1.0, ALU.is_ge, [evb], [ev2b])
    stt(kb, E_, ev, 3.0, ev2, ALU.is_le, ALU.mult, [evb, ev2b], [E_b])
    fv, fvb = kb.sbs(stA, "fv", [128, NF, 2, 64], F32)
    kb.op("pool", lambda e: e.iota(fv, pattern=[[-2, NF], [-1, 2], [0, 64]], base=0, channel_multiplier=1,
                                   allow_small_or_imprecise_dtypes=True), writes=[fvb])
    ts(kb, F_, fv.rearrange("p a b c -> p a (b c)"), 0.0, ALU.is_equal, [fvb], [F_b])
    kb.release([difb, evb, ev2b, fvb])
    stA.close()

    if dbg == 1:
        kb.finish([])
        return nc
    wq_s, wq_b = kb.sb("wq_s", [128, 8, 512], BF16)
    wgt_s, wgt_b = kb.sb("wgt_s", [128, 8, 24], BF16)
    wqv = wq.rearrange("(k p) n -> p k n", p=128)
    wkvv = wkv.rearrange("(k p) n -> p k n", p=128)
    for k in range(8):
        kb.dma("pool", wq_s[:, k, :], wqv[:, k, :], reads=[wqb_], writes=[wq_b])
    kb.dma("pool", wgt_s, wgt.rearrange("(k p) n -> p k n", p=128), reads=[wgtb_], writes=[wgt_b])
    tqsl = [kb.sb("tqs%d" % i, [128, 128], F32) for i in range(2)]
    curs, cursb = kb.sb("curs", [128, NS], F32)
    kb.dma("sp", curs, curd, reads=[curdb], writes=[cursb])
    kcm, kcmb = kb.sb("kcm", [128, NCP], BF16)
    vcm, vcmb = kb.sb("vcm", [128, NNT, 2, 65], BF16)
    kb.op("pool", lambda e: e.memset(vcm, 1.0), writes=[vcmb])
    kb.op("pool", lambda e: e.memset(kcm, 0.0), writes=[kcmb])
    hqt, hqtb = kb.sb("hq0", [128, 8, 128], BF16)
    QT, QTb = kb.sb("QT", [128, 2, 4, 128], BF16)
    kb.op("pool", lambda e: e.memset(QT, 0.0), writes=[QTb])
    gsb, gsbb = kb.sb("gsb", [128, 24], F32)
    Pt = [kb.sb("Pt%d" % i, [128, 4, 128], BF16) for i in range(2)]
    mk = [kb.sb("mk%d" % i, [128, 128], BF16) for i in range(2)]
    mk2 = [kb.sb("mk2%d" % i, [128, 128], F32) for i in range(2)]
    OT, OTb = kb.sb("OT", [128, 512], F32)
    oacc, oaccb = kb.sb("oacc", [128, 512], F32)
    rec, recb = kb.sb("rec", [128, 4], F32)
    coef, coefb = kb.sb("coef", [128, 4], F32)
    sel, selb = kb.sb("sel", [128, NBLK], F32)
    sc, scb = kb.sb("sc", [128, NBLK], F32)
    sc2, sc2b = kb.sb("sc2", [128, NBLK], F32)
    nf, nfb = kb.sb("nf", [128, NBLK], F32)
    frc, frcb = kb.sb("frc", [128, NBLK], F32)
    m8, m8b = kb.sb("m8", [128, 16], F32)
    bm, bmb = kb.sb("bm", [128, BT * 128], F32)
    nbT4, nbT4b = kb.sb("nbT4", [128, BT, 4, 128], BF16)
    kb.op("pool", lambda e: e.memset(nbT4, 0.0), writes=[nbT4b])
    kb.op("pool", lambda e: e.memset(bm, 0.0), writes=[bmb])
    pp = [kb.ps("pp%d" % i, [128, 512], F32) for i in range(8)]
    hTv = hT.rearrange("(k p) t -> p k t", p=128)
    GW = 512
    NG = seq // GW

    stB = ExitStack()
    wkv_s, wkv_b = kb.sbs(stB, "wkv_s", [128, 8, 256], BF16)
    for k in range(8):
        kb.dma("pool", wkv_s[:, k, :], wkvv[:, k, 0:256], reads=[wkvb_], writes=[wkv_b])
    hb, hbb = kb.sbs(stB, "hb", [128, 8, GW], BF16)
    w1s, w1sb = kb.sbs(stB, "w1s", [128, 2, 32, 64], BF16)
    kb.dma("pool", w1s[:, 0], w1d[:, 0], reads=[w1db], writes=[w1sb])
    kb.dma("pool", w1s[:, 1], w1d[:, 1], reads=[w1db], writes=[w1sb])
    w2s, w2sb = kb.sbs(stB, "w2s", [128, 2, 64], BF16)
    kb.dma("pool", w2s, w2d, reads=[w2db], writes=[w2sb])
    pes, pesb = kb.sbs(stB, "pes", [128, 2, 34], BF16)
    kb.op("pool", lambda e: e.memset(pes, 0.0), writes=[pesb])
    kb.dma("pool", pes[:, :, 0:32], ped, reads=[pedb], writes=[pesb])
    kcT, kcTb = kb.sbs(stB, "kcT", [128, 2, seq], BF16)
    gl, glb = kb.sbs(stB, "gl", [128, NCP], F32)
    gx, gxb = kb.sbs(stB, "gx", [128, NCP], F32)
    gbf, gbfb = kb.sbs(stB, "gbf", [128, NCP], BF16)
    bcol, bcolb = kb.sbs(stB, "bcol", [128, 2], F32)
    w2z, w2zb = kb.sbs(stB, "w2z", [128, 2, 64], BF16)
    if dbg == 11:
        kb.finish([])
        return nc
    for gi in range(NG):
        kb.dma("pool", hb, hTv[:, :, gi * GW:(gi + 1) * GW], reads=[hTb], writes=[hbb])
        for kv in range(2):
            p, pb = pp[(gi * 2 + kv) % 2]
            for k in range(8):
                mm(kb, p, wkv_s[:, k, kv * 128:(kv + 1) * 128], hb[:, k, :], k == 0, k == 7, [wkv_b, hbb], [pb])
            actf(kb, kcT[:, kv, gi * GW:(gi + 1) * GW], p, AF.Copy, [pb], [kcTb])
    if dbg == 12:
        kb.finish([])
        return nc
    kb.op("pool", lambda e: e.memset(gbf, 0.0), writes=[gbfb])
    NV = NCH - 1
    for kv in range(2):
        pbias, pbiasb = pp[2]
        for g in range(2):
            gs = slice(64 * g, 64 * g + 64)
            for l in range(32):
                mm(kb, pbias[gs, 0:2], w1s[gs, kv, l, :], pes[gs, kv, l:l + 2], l == 0, l == 31, [w1sb, pesb],
                   [pbiasb])
        kb.op("dve", lambda e: e.tensor_copy(out=bcol, in_=pbias[:, 0:2]), reads=[pbiasb], writes=[bcolb])
        if dbg == 13:
            kb.finish([])
            return nc
        n0 = 0
        ci = 0
        while n0 < NV:
            nn = min(512, NV - n0)
            pc, pcb = pp[3 + ci % 2]
            ci += 1
            for g in range(2):
                gs = slice(64 * g, 64 * g + 64)
                for l in range(32):
                    src = kcT[gs, kv, 16 * n0 + l:16 * n0 + l + 16 * (nn - 1) + 1:16]
                    mm(kb, pc[gs, 0:nn], w1s[gs, kv, l, :], src, l == 0, l == 31, [w1sb, kcTb], [pcb])
            actf(kb, gx[:, n0:n0 + nn], pc[:, 0:nn], AF.Identity, [pcb, bcolb], [gxb], bias=bcol[:, 0:1], scale=1.0)
            n0 += nn
        if dbg == 14:
            kb.finish([])
            return nc
        tt(kb, gl[:, 0:NV], gx[:, 0:NV], gx[:, 0:NV], ALU.mult, [gxb], [glb])
        ts(kb, gl[:, 0:NV], gl[:, 0:NV], 0.044715, ALU.mult, [glb], [glb], s2=1.0, op1=ALU.add)
        tt(kb, gl[:, 0:NV], gl[:, 0:NV], gx[:, 0:NV], ALU.mult, [glb, gxb], [glb])
        actf(kb, gl[:, 0:NV], gl[:, 0:NV], AF.Sigmoid, [glb], [glb], scale=1.5957691216057308)
        tt(kb, gbf[:, 0:NV], gl[:, 0:NV], gx[:, 0:NV], ALU.mult, [glb, gxb], [gbfb])
        if dbg == 15 or (dbg == 17 and kv == 1):
            kb.finish([])
            return nc
        if kv == 0:
            n0 = 0
            while n0 < NCP:
                nn = min(512, NCP - n0)
                pc, pcb = pp[5]
                for g in range(2):
                    gs = slice(64 * g, 64 * g + 64)
                    mm(kb, pc[gs, 0:nn], w2s[gs, 0, :], gbf[gs, n0:n0 + nn], True, True, [w2sb, gbfb], [pcb])
                actf(kb, kcm[:, n0:n0 + nn], pc[:, 0:nn], AF.Copy, [pcb], [kcmb])
                n0 += nn
            if dbg == 16:
                kb.finish([])
                return nc
        else:
            kb.op("dve", lambda e: e.memset(w2z, 0.0), writes=[w2zb])
            for g in range(2):
                gs = slice(64 * g, 64 * g + 64)
                kb.op("dve", lambda e, g=g, gs=gs: e.tensor_copy(out=w2z[gs, g, :], in_=w2s[gs, 1, :]),
                      reads=[w2sb], writes=[w2zb])
            for nt in range(NNT):
                pc, pcb = pp[5]
                for g in range(2):
                    mm(kb, pc[:, g * 64:(g + 1) * 64], gbf[:, nt * 128:(nt + 1) * 128], w2z[:, g, :], True, True,
                       [w2zb, gbfb], [pcb])
                actf(kb, vcm[:, nt, :, 0:64], pc[:, 0:128].rearrange("p (g d) -> p g d", g=2), AF.Copy,
                     [pcb], [vcmb])
    kb.release([wkv_b, hbb, w1sb, w2sb, pesb, kcTb, glb, gxb, gbfb, bcolb, w2zb])
    stB.close()

    if dbg == 2:
        kb.finish([])
        return nc
    ksT, ksTb = kb.sb("ksT", [128, seq], BF16)
    kwT, kwTb = kb.sb("kwT", [128, seq], BF16)
    vsA, vsAb = kb.sb("vsA", [128, NK, 2, 65], BF16)
    vwA, vwAb = kb.sb("vwA", [128, NK, 2, 65], BF16)
    kb.op("pool", lambda e: e.memset(vsA, 1.0), writes=[vsAb])
    kb.op("pool", lambda e: e.memset(vwA, 1.0), writes=[vwAb])
    stD = ExitStack()
    wk2, wk2b = kb.sbs(stD, "wk2", [128, 8, 512], BF16)
    for k in range(8):
        kb.dma("pool", wk2[:, k, :], wkvv[:, k, 256:768], reads=[wkvb_], writes=[wk2b])
    hb, hbb = kb.sbs(stD, "hb2", [128, 8, GW], BF16)
    for gi in range(NG):
        kb.dma("pool", hb, hTv[:, :, gi * GW:(gi + 1) * GW], reads=[hTb], writes=[hbb])
        for wi, dst, dstb in ((0, ksT, ksTb), (2, kwT, kwTb)):
            p, pb = pp[wi // 2]
            for k in range(8):
                mm(kb, p, wk2[:, k, wi * 128:(wi + 1) * 128], hb[:, k, :], k == 0, k == 7, [wk2b, hbb], [pb])
            actf(kb, dst[:, gi * GW:(gi + 1) * GW], p, AF.Copy, [pb], [dstb])
        for wi, dst, dstb in ((1, vsA, vsAb), (3, vwA, vwAb)):
            p, pb = pp[2 + wi // 2]
            for i4 in range(4):
                for k in range(8):
                    mm(kb, p[:, i4 * 128:(i4 + 1) * 128], hb[:, k, i4 * 128:(i4 + 1) * 128],
                       wk2[:, k, wi * 128:(wi + 1) * 128], k == 0, k == 7, [wk2b, hbb], [pb])
            kb.op("dve", lambda e, p=p, dst=dst, gi=gi: e.tensor_copy(
                out=dst[:, gi * 4:(gi + 1) * 4, :, 0:64],
                in_=p.rearrange("p (i g d) -> p i g d", i=4, g=2)), reads=[pb], writes=[dstb])
    kb.release([wk2b, hbb])
    stD.close()

    if dbg == 3:
        kb.finish([])
        return nc
    pS = [pp[0], pp[1]]
    pO, pOb = pp[2]
    pSel4 = [pp[3], pp[4], pp[5], pp[7]]
    pM, pMb = pp[5]
    pTk, pTkb = pp[6]
    pQ, pQb = pp[7]
    hqv = hq.rearrange("(k p) t -> p k t", p=128)
    oTv = oT.rearrange("(c p) t -> p c t", p=128)
    cnt = [0]

    def attend(g, tiles, kT, kTb_, vA, vAb_, br, first_branch, mask_fn=None, bias_fn=None, extra=None):
        nt_ = len(tiles)
        c0 = cnt[0]
        cnt[0] += nt_
        hasb = bias_fn is not None

        def issue_scores(idx):
            i = tiles[idx]
            S, Sb = pS[(c0 + idx) % 2]
            mm(kb, S, kT[:, i * 128:(i + 1) * 128], QT[:, g, :, :].rearrange("p r q -> p (r q)"), True, not hasb,
               [kTb_, QTb], [Sb])
            if hasb:
                bias_fn(i, S, Sb)
        issue_scores(0)
        for idx, i in enumerate(tiles):
            c = c0 + idx
            S, Sb = pS[c % 2]
            P, Pb = Pt[c % 2]
            if idx + 1 < nt_:
                issue_scores(idx + 1)
            actf(kb, P.rearrange("p r q -> p (r q)"), S, AF.Exp, [Sb], [Pb])
            mres = mask_fn(i, c) if mask_fn is not None else None
            if mres is not None:
                m_ap, m_b = mres
                tt(kb, P, P, m_ap.rearrange("p (o q) -> p o q", o=1).to_broadcast([128, 4, 128]), ALU.mult,
                   [Pb, m_b], [Pb])
            mm(kb, pO[0:65, :], vA[:, i, g, :], P.rearrange("p r q -> p (r q)"), idx == 0, idx == nt_ - 1,
               [vAb_, Pb], [pOb])
            if extra is not None:
                extra(i, idx, P, Pb, nt_)
        actf(kb, OT[0:65, :], pO[0:65, :], AF.Copy, [pOb], [OTb])
        for r in range(4):
            transp(kb, pTk[:, r * 65:(r + 1) * 65], OT[0:65, r * 128:(r + 1) * 128], identf[0:65, 0:65],
                   [OTb, identfb], [pTkb])
        pv = pTk[:, 0:260].rearrange("p (r e) -> p r e", r=4)
        ts(kb, rec, pv[:, :, 64], TINY, ALU.max, [pTkb], [recb])
        kb.op("dve", lambda e: e.reciprocal(out=rec, in_=rec), reads=[recb], writes=[recb])
        gv = gsb.rearrange("p (g r t) -> p g r t", g=2, r=4)
        tt(kb, coef, rec, gv[:, g, :, br], ALU.mult, [recb, gsbb], [coefb])
        for r in range(4):
            dst = oacc[:, (g * 4 + r) * 64:(g * 4 + r + 1) * 64]
            if first_branch:
                ts(kb, dst, pv[:, r, 0:64], coef[:, r:r + 1], ALU.mult, [pTkb, coefb], [oaccb])
            else:
                stt(kb, dst, pv[:, r, 0:64], coef[:, r:r + 1], dst, ALU.mult, ALU.add, [pTkb, coefb, oaccb], [oaccb])

    for m in range(NS):
        kb.dma("pool", hqt, hqv[:, :, m * 128:(m + 1) * 128], reads=[hqb_], writes=[hqtb])
        for r in range(4):
            for k in range(8):
                mm(kb, pQ[:, r * 128:(r + 1) * 128], wq_s[:, k, r * 128:(r + 1) * 128], hqt[:, k, :], k == 0, k == 7,
                   [wq_b, hqtb], [pQb])
        for g in range(2):
            gs = slice(64 * g, 64 * g + 64)
            actf(kb, QT[gs, g, :, :].rearrange("p r q -> p (r q)"), pQ[gs, :], AF.Copy, [pQb], [QTb], scale=0.125)
        for k in range(8):
            mm(kb, pM[:, 0:24], hqt[:, k, :], wgt_s[:, k, :], k == 0, k == 7, [hqtb, wgt_b], [pMb])
        actf(kb, gsb, pM[:, 0:24], AF.Sigmoid, [pMb], [gsbb])
        tqs, tqsb = tqsl[m % 2]
        kb.dma("sp", tqs, tqd[:, m * 128:(m + 1) * 128], reads=[tqdb], writes=[tqsb])
        curc = curs[:, m:m + 1]
        ts(kb, nf, jrow, curc, ALU.is_le, [jrowb, cursb], [nfb])
        ts(kb, frc, jrow, curc, ALU.is_equal, [jrowb, cursb], [frcb])
        stt(kb, frc, jrow, 0.0, frc, ALU.is_equal, ALU.add, [jrowb, frcb], [frcb])
        ts(kb, sc2, jrow, curc, ALU.subtract, [jrowb, cursb], [sc2b], s2=-1.0, op1=ALU.is_equal)
        tt(kb, frc, frc, sc2, ALU.add, [frcb, sc2b], [frcb])
        ts(kb, frc, frc, 1.0, ALU.min, [frcb], [frcb])
        for g in range(2):
            def cmp_mask(i, c):
                mt, mtb = mk[c % 2]
                ts(kb, mt, tqs, cend[:, i:i + 1], ALU.is_ge, [tqsb, cendb], [mtb])
                return mt, mtb

            def cmp_extra(i, idx, P, Pb, n):
                for r in range(4):
                    ps_, psb = pSel4[r]
                    mm(kb, ps_[:, 0:NBLK], P[:, r, :], E_[:, i, :], idx == 0, idx == n - 1, [Pb, E_b], [psb])
            attend(g, list(range(NNT)), kcm, kcmb, vcm, vcmb, 0, True, mask_fn=cmp_mask, extra=cmp_extra)
            for r in range(4):
                ps_, psb = pSel4[r]
                src = ps_[:, 0:NBLK]
                if r == 0:
                    ts(kb, sel, src, rec[:, 0:1], ALU.mult, [psb, recb], [selb])
                else:
                    stt(kb, sel, src, rec[:, r:r + 1], sel, ALU.mult, ALU.add, [psb, recb, selb], [selb])
            stt(kb, sc, frc, 1.0e4, sel, ALU.mult, ALU.add, [frcb, selb], [scb])
            ts(kb, sc, sc, 1.0, ALU.add, [scb], [scb])
            tt(kb, sc, sc, nf, ALU.mult, [scb, nfb], [scb])
            ts(kb, sc, sc, -1.0, ALU.add, [scb], [scb])
            kb.op("dve", lambda e: e.max(out=m8[:, 0:8], in_=sc), reads=[scb], writes=[m8b])
            kb.op("dve", lambda e: e.match_replace(out=sc2, in_to_replace=m8[:, 0:8], in_values=sc, imm_value=-1e30),
                  reads=[scb, m8b], writes=[sc2b])
            kb.op("dve", lambda e: e.max(out=m8[:, 8:16], in_=sc2), reads=[sc2b], writes=[m8b])
            ts(kb, sc2, sc, m8[:, 15:16], ALU.is_ge, [scb, m8b], [sc2b])
            tt(kb, bm[:, 0:NBLK], sc2, nf, ALU.mult, [sc2b, nfb], [bmb])
            for t2 in range(BT):
                transp(kb, pM[0:BR, t2 * 128:(t2 + 1) * 128], bm[:, t2 * 128:t2 * 128 + BR], identf,
                       [bmb, identfb], [pMb])
            for t2 in range(BT):
                src = pM[0:BR, t2 * 128:(t2 + 1) * 128]
                ts(kb, nbT4[0:BR, t2, :, :], src.rearrange("p (o q) -> p o q", o=1).to_broadcast([BR, 4, 128]),
                   -1.0, ALU.add, [pMb], [nbT4b], s2=30000.0, op1=ALU.mult)

            def sel_bias(i, S, Sb):
                mm(kb, S, F_[:, i % 64, :], nbT4[:, i // 64, :, :].rearrange("p r q -> p (r q)"), False, True,
                   [F_b, nbT4b], [Sb])

            def sel_mask(i, c, m=m):
                if i < 4 * m:
                    return None
                mt, mtb = mk[c % 2]
                ts(kb, mt, tqs, kpos[:, i:i + 1], ALU.is_ge, [tqsb, kposb], [mtb])
                return mt, mtb
            attend(g, list(range(4 * m + 4)), ksT, ksTb, vsA, vsAb, 1, False, mask_fn=sel_mask, bias_fn=sel_bias)

            def win_mask(i, c):
                mt, mtb = mk[c % 2]
                m2, m2b = mk2[c % 2]
                ts(kb, m2, tqs, kpos[:, i:i + 1], ALU.subtract, [tqsb, kposb], [m2b])
                ts(kb, mt, m2, 0.0, ALU.is_ge, [m2b], [mtb])
                stt(kb, mt, m2, 512.0, mt, ALU.is_lt, ALU.mult, [m2b, mtb], [mtb])
                return mt, mtb
            attend(g, list(range(max(0, 4 * m - 4), 4 * m + 4)), kwT, kwTb, vwA, vwAb, 2, False, mask_fn=win_mask)
        for t4 in range(4):
            transp(kb, pQ[:, t4 * 128:(t4 + 1) * 128], oacc[:, t4 * 128:(t4 + 1) * 128], identf, [oaccb, identfb],
                   [pQb])
        actf(kb, OT, pQ, AF.Copy, [pQb], [OTb])
        kb.dma("sp", oTv[:, :, m * 128:(m + 1) * 128], OT.rearrange("p (c q) -> p c q", c=4), reads=[OTb],
               writes=[oTb])
    kb.finish([oTb])
    return nc


def nsa_inputs(P, l, j, hT_full, hT_q, seq=None):
    seq = seq or SEQ
    ns = seq // 4 // 128
    o = 512
    qcols = np.array([o + (g * 4 + r) * 64 + d for r in range(4) for g in range(2) for d in range(64)])
    w1 = np.stack([P["cmp_w1_k"][l].reshape(32, 64, 64), P["cmp_w1_v"][l].reshape(32, 64, 64)], 0)
    w1 = np.ascontiguousarray(np.tile(w1.transpose(2, 0, 1, 3), (2, 1, 1, 1)))
    w2 = np.stack([P["cmp_w2_k"][l], P["cmp_w2_v"][l]], 1)
    w2 = np.ascontiguousarray(np.tile(w2, (2, 1, 1)))
    pe = np.stack([P["cmp_pe_k"][l].T, P["cmp_pe_v"][l].T], 1)
    pe = np.ascontiguousarray(np.tile(pe, (2, 1, 1)))
    tq = np.concatenate([np.arange(128) + 128 * (4 * m + j) for m in range(ns)]).astype(np.float32)
    cur = np.stack([(np.arange(128) + 128 * (4 * m + j)) // 64 for m in range(ns)], 1).astype(np.float32)
    return {"hT": hT_full, "hTq": hT_q,
            "w_q": np.ascontiguousarray(P["w_in"][l][:, qcols]),
            "w_kv": np.ascontiguousarray(P["w_in"][l][:, o + 512:o + 512 + 768]),
            "w_gt": np.ascontiguousarray(P["w_in"][l][:, o + 1280:o + 1304]),
            "w1": w1.astype(np.float32), "w2": w2.astype(np.float32), "peT": pe.astype(np.float32),
            "tq_bc": np.ascontiguousarray(np.broadcast_to(tq[None, :], (128, seq // 4))),
            "curcol": np.ascontiguousarray(cur)}


_PROGS = {}


def _prog(name, fn):
    if name not in _PROGS:
        _PROGS[name] = fn()
    return _PROGS[name]


def _run(nc, maps):
    return run_bass_kernel_spmd(nc, maps, core_ids=list(range(NCORES))).results


def kernel(**P):
    P = {k: np.asarray(v) for k, v in P.items()}
    x = P["x"]
    xT = [np.ascontiguousarray(x[c // 4, (c % 4) * TQ:(c % 4 + 1) * TQ].T) for c in range(NCORES)]
    res = _run(_prog("a", build_stage_a), [{"xT": xT[c], "g": _cols(P["norm_mix"][0], 8)} for c in range(NCORES)])
    hT = [r["hT"] for r in res]
    out = None
    for l in range(DEPTH):
        hfull = [np.ascontiguousarray(np.concatenate(hT[b * 4:(b + 1) * 4], axis=1)) for b in range(NB)]
        maps = []
        for c in range(NCORES):
            b, j = c // 4, c % 4
            hh = np.zeros((D, HALO + TQ), np.float32)
            hh[:, HALO:] = hT[c]
            if j > 0:
                hh[:, :HALO] = hT[c - 1][:, TQ - HALO:]
            maps.append(conv_inputs(P, l, hh))
        rconv = _run(_prog("conv", build_stage_conv), maps)
        rrw = _run(_prog("rwkv", build_stage_rwkv),
                   [rwkv_inputs(P, l, c % 4, hfull[c // 4]) for c in range(NCORES)])
        maps = []
        for c in range(NCORES):
            b, j = c // 4, c % 4
            hq = hfull[b].reshape(D, NSLOT, 4, 128)[:, :, j, :].reshape(D, TQ)
            maps.append(nsa_inputs(P, l, j, hfull[b], np.ascontiguousarray(hq)))
        rnsa = _run(_prog("nsa", build_stage_nsa), maps)
        mixfull = []
        for b in range(NB):
            mf = np.empty((D, SEQ), np.float32)
            for j in range(4):
                c = b * 4 + j
                mf[0:256, j * TQ:(j + 1) * TQ] = rconv[c]["convT"]
                mf[256:768].reshape(512, NSLOT, 4, 128)[:, :, j, :] = rnsa[c]["nsaT"].reshape(512, NSLOT, 128)
                mf[768 + j * 64:768 + (j + 1) * 64, :] = rrw[c]["yT"]
            mixfull.append(mf)
        maps = []
        for c in range(NCORES):
            b, j = c // 4, c % 4
            mixT = np.ascontiguousarray(mixfull[b][:, j * TQ:(j + 1) * TQ])
            if j > 0:
                xh = np.ascontiguousarray(xT[c - 1][:, TQ - 2:])
                mh = np.ascontiguousarray(mixfull[b][:, j * TQ - 2:j * TQ])
            else:
                xh = np.zeros((D, 2), np.float32)
                mh = np.zeros((D, 2), np.float32)
            gnext = P["norm_mix"][l + 1] if l + 1 < DEPTH else P["norm_final"]
            maps.append({"xT": xT[c], "mixT": mixT, "xh": xh, "mixh": mh,
                         "w_out": np.ascontiguousarray(P["w_out"][l]), "w_gate": np.ascontiguousarray(P["ffn_w_gate"][l]),
                         "w_up": np.ascontiguousarray(P["ffn_w_up"][l]), "w_down": np.ascontiguousarray(P["ffn_w_down"][l]),
                         "g_ffn": _cols(P["norm_ffn"][l], 8), "g_next": _cols(gnext, 8),
                         "conv_w": np.ascontiguousarray(P["ffn_conv_w"][l].reshape(3, DFF // 128, 128).transpose(2, 1, 0)),
                         "conv_b": _cols(P["ffn_conv_b"][l], DFF // 128)})
        rc = _run(_prog("c", build_stage_c), maps)
        xT = [r["xoT"] for r in rc]
        hT = [r["hoT"] for r in rc]
    out = np.empty((NB, SEQ, D), np.float32)
    for c in range(NCORES):
        out[c // 4, (c % 4) * TQ:(c % 4 + 1) * TQ] = hT[c].T
    return out
```

```python
import numpy as np
from contextlib import ExitStack
import concourse.bass as bass
import concourse.mybir as mybir
from concourse.bass_utils import run_bass_kernel_spmd

F32 = mybir.dt.float32
BF16 = mybir.dt.bfloat16
AF = mybir.ActivationFunctionType
ALU = mybir.AluOpType
AX = mybir.AxisListType

D = 1024
SEQ = 16384
NB = 2
DEPTH = 4
DFF = 2816
NCORES = 8
TQ = SEQ // 4
RMS_EPS = 1e-6


class Buf:
    __slots__ = ("w", "r", "name")

    def __init__(self, name=""):
        self.w = None
        self.r = {}
        self.name = name


class KB:
    NDMA = 6

    def __init__(self):
        self.nc = bass.Bass("TRN2", target_bir_lowering=False)
        nc = self.nc
        self.E = {"pe": nc.tensor, "act": nc.scalar, "dve": nc.vector,
                  "pool": nc.gpsimd, "sp": nc.sync}
        self.sems = {}
        self.cnt = {}
        for e in ("pe", "act", "dve", "pool"):
            self.sems[e] = nc.alloc_semaphore("c_" + e)
            self.cnt[e] = 0
        self.dq = {}
        for q in ("sp", "pool", "act"):
            ring = []
            for i in range(self.NDMA):
                key = "d_%s_%d" % (q, i)
                self.sems[key] = nc.alloc_semaphore(key)
                self.cnt[key] = 0
                ring.append(key)
            self.dq[q] = [ring, 0]
        self.known = {e: {} for e in self.E}
        self.nbuf = 0
        self.out_tokens = []

    def sb(self, name, shape, dt):
        t = self.nc.alloc_sbuf_tensor(name, list(shape), dt)
        return t.ap(), Buf(name)

    def sbs(self, stack, name, shape, dt):
        t = stack.enter_context(self.nc.sbuf_tensor(name, list(shape), dt))
        return t.ap(), Buf(name)

    def release(self, bufs):
        for eng in self.E:
            self._waits(eng, [], bufs, skip_own=(eng == "pe"))

    def ps(self, name, shape, dt=F32):
        t = self.nc.alloc_psum_tensor(name, list(shape), dt)
        return t.ap(), Buf(name)

    def dram(self, name, shape, dt, kind):
        return self.nc.dram_tensor(name, list(shape), dt, kind=kind).ap(), Buf(name)

    def _waits(self, eng, reads, writes, skip_own=False):
        deps = {}
        for b in reads:
            if b.w is not None:
                k, v = b.w
                if deps.get(k, 0) < v:
                    deps[k] = v
        for b in writes:
            if b.w is not None:
                k, v = b.w
                if deps.get(k, 0) < v:
                    deps[k] = v
            for k, v in b.r.items():
                if deps.get(k, 0) < v:
                    deps[k] = v
        kn = self.known[eng]
        for k, v in deps.items():
            if skip_own and k == eng:
                continue
            if kn.get(k, 0) < v:
                self.E[eng].wait_ge(self.sems[k], v)
                kn[k] = v

    def _record(self, tok, reads, writes):
        k, v = tok
        for b in writes:
            b.w = tok
            b.r = {}
        for b in reads:
            if b.r.get(k, 0) < v:
                b.r[k] = v

    def op(self, eng, fn, reads=(), writes=()):
        self._waits(eng, reads, writes, skip_own=(eng == "pe"))
        inst = fn(self.E[eng])
        self.cnt[eng] += 1
        inst.then_inc(self.sems[eng], 1)
        tok = (eng, self.cnt[eng])
        self._record(tok, reads, writes)
        return tok

    def dma(self, q, out, in_, reads=(), writes=(), **kw):
        ring, i = self.dq[q]
        key = ring[i % self.NDMA]
        self.dq[q][1] = i + 1
        kn = self.known[q]
        if kn.get(key, 0) < self.cnt[key]:
            self.E[q].wait_ge(self.sems[key], self.cnt[key])
            kn[key] = self.cnt[key]
        self._waits(q, reads, writes)
        inst = self.E[q].dma_start(out=out, in_=in_, **kw)
        self.cnt[key] += 16
        inst.then_inc(self.sems[key], 16)
        tok = (key, self.cnt[key])
        self._record(tok, reads, writes)
        return tok

    def finish(self, out_bufs):
        deps = {}
        for b in out_bufs:
            if b.w is not None:
                k, v = b.w
                deps[k] = max(deps.get(k, 0), v)
        for q in self.dq:
            for key in self.dq[q][0]:
                if self.cnt[key] > 0:
                    deps[key] = max(deps.get(key, 0), self.cnt[key])
        for k, v in deps.items():
            self.nc.sync.wait_ge(self.sems[k], v)


def emit_rmsnorm(kb, xt, xb, gcol, ones, W, sq, sqb, ssum, ssb, pst, pstb, rstd, rsb, outs, outb,
                 out_dt_scale=None):
    nc = kb.nc
    for k in range(8):
        if k == 0:
            kb.op("act", lambda e: e.activation(out=ssum[:, :W], in_=xt[:, 0, :W], func=AF.Square),
                  reads=[xb], writes=[ssb])
        else:
            kb.op("act", lambda e, k=k: e.activation(out=sq[:, :W], in_=xt[:, k, :W], func=AF.Square),
                  reads=[xb], writes=[sqb])
            kb.op("dve", lambda e: e.tensor_tensor(out=ssum[:, :W], in0=ssum[:, :W], in1=sq[:, :W], op=ALU.add),
                  reads=[sqb, ssb], writes=[ssb])
    kb.op("pe", lambda e: e.matmul(pst[:, :W], lhsT=ones, rhs=ssum[:, :W], start=True, stop=True),
          reads=[ssb], writes=[pstb])
    kb.op("act", lambda e: e.activation(out=rstd[:, :W], in_=pst[:, :W], func=AF.Sqrt,
                                        bias=kb.eps_col, scale=1.0 / D),
          reads=[pstb], writes=[rsb])
    kb.op("dve", lambda e: e.reciprocal(out=rstd[:, :W], in_=rstd[:, :W]), reads=[rsb], writes=[rsb])
    for k in range(8):
        kb.op("dve", lambda e, k=k: e.scalar_tensor_tensor(out=outs[:, k, :W], in0=xt[:, k, :W],
                                                           scalar=gcol[:, k:k + 1], in1=rstd[:, :W],
                                                           op0=ALU.mult, op1=ALU.mult),
              reads=[xb, rsb], writes=[outb])


def build_stage_a():
    kb = KB()
    nc = kb.nc
    W = 512
    xT, xTb = kb.dram("xT", [D, TQ], F32, "ExternalInput")
    g, gb = kb.dram("g", [128, 8], F32, "ExternalInput")
    hT, hTb = kb.dram("hT", [D, TQ], F32, "ExternalOutput")
    gcol, gcb = kb.sb("gcol", [128, 8], F32)
    ones, onb = kb.sb("ones", [128, 128], F32)
    eps, epb = kb.sb("eps", [128, 1], F32)
    kb.eps_col = eps
    kb.op("dve", lambda e: e.memset(ones, 1.0), writes=[onb])
    kb.op("dve", lambda e: e.memset(eps, RMS_EPS), writes=[epb])
    kb.dma("sp", gcol, g, reads=[gb], writes=[gcb])
    xs = [kb.sb("x%d" % i, [128, 8, W], F32) for i in range(2)]
    os_ = [kb.sb("o%d" % i, [128, 8, W], F32) for i in range(2)]
    sq, sqb = kb.sb("sq", [128, W], F32)
    ssum, ssb = kb.sb("ssum", [128, W], F32)
    rstd, rsb = kb.sb("rstd", [128, W], F32)
    pst, pstb = kb.ps("pst", [128, W], F32)
    xTv = xT.rearrange("(k p) t -> p k t", p=128)
    hTv = hT.rearrange("(k p) t -> p k t", p=128)
    for c in range(TQ // W):
        xt, xb = xs[c % 2]
        ot, ob = os_[c % 2]
        kb.dma("sp", xt, xTv[:, :, c * W:(c + 1) * W], reads=[xTb], writes=[xb])
        emit_rmsnorm(kb, xt, xb, gcol, ones, W, sq, sqb, ssum, ssb, pst, pstb, rstd, rsb, ot, ob)
        kb.dma("sp", hTv[:, :, c * W:(c + 1) * W], ot, reads=[ob], writes=[hTb])
    kb.finish([hTb])
    return nc


def build_stage_c():
    kb = KB()
    nc = kb.nc
    W = 256
    NT = TQ
    NF = DFF // 128
    xT, xTb = kb.dram("xT", [D, NT], F32, "ExternalInput")
    mixT, mixTb = kb.dram("mixT", [D, NT], F32, "ExternalInput")
    xh, xhb = kb.dram("xh", [D, 2], F32, "ExternalInput")
    mixh, mixhb = kb.dram("mixh", [D, 2], F32, "ExternalInput")
    w_out, wob = kb.dram("w_out", [D, D], F32, "ExternalInput")
    w_gate, wgb_ = kb.dram("w_gate", [D, DFF], F32, "ExternalInput")
    w_up, wub_ = kb.dram("w_up", [D, DFF], F32, "ExternalInput")
    w_down, wdb_ = kb.dram("w_down", [DFF, D], F32, "ExternalInput")
    gf, gfb = kb.dram("g_ffn", [128, 8], F32, "ExternalInput")
    gn, gnb = kb.dram("g_next", [128, 8], F32, "ExternalInput")
    cw, cwb = kb.dram("conv_w", [128, NF, 3], F32, "ExternalInput")
    cb, cbb = kb.dram("conv_b", [128, NF], F32, "ExternalInput")
    xo, xob = kb.dram("xoT", [D, NT], F32, "ExternalOutput")
    ho, hob = kb.dram("hoT", [D, NT], F32, "ExternalOutput")

    wo_s, wo_b = kb.sb("wo_s", [128, 8, D], BF16)
    wg_s, wg_b = kb.sb("wg_s", [128, 8, DFF], BF16)
    wu_s, wu_b = kb.sb("wu_s", [128, 8, DFF], BF16)
    wd_s, wd_b = kb.sb("wd_s", [128, NF, D], BF16)
    wov = w_out.rearrange("(k p) n -> p k n", p=128)
    wgv = w_gate.rearrange("(k p) n -> p k n", p=128)
    wuv = w_up.rearrange("(k p) n -> p k n", p=128)
    wdv = w_down.rearrange("(k p) n -> p k n", p=128)
    for k in range(8):
        kb.dma("pool", wo_s[:, k, :], wov[:, k, :], reads=[wob], writes=[wo_b])
    for k in range(8):
        for hh in range(2):
            sl = slice(hh * (DFF // 2), (hh + 1) * (DFF // 2))
            kb.dma("pool", wg_s[:, k, sl], wgv[:, k, sl], reads=[wgb_], writes=[wg_b])
            kb.dma("pool", wu_s[:, k, sl], wuv[:, k, sl], reads=[wub_], writes=[wu_b])
    for f in range(NF):
        kb.dma("pool", wd_s[:, f, :], wdv[:, f, :], reads=[wdb_], writes=[wd_b])

    gfc, gfcb = kb.sb("gfc", [128, 8], F32)
    gnc, gncb = kb.sb("gnc", [128, 8], F32)
    cws, cwsb = kb.sb("cws", [128, NF, 3], F32)
    cbs, cbsb = kb.sb("cbs", [128, NF], F32)
    kb.dma("sp", gfc, gf, reads=[gfb], writes=[gfcb])
    kb.dma("sp", gnc, gn, reads=[gnb], writes=[gncb])
    kb.dma("sp", cws, cw, reads=[cwb], writes=[cwsb])
    kb.dma("sp", cbs, cb, reads=[cbb], writes=[cbsb])
    ones, onb = kb.sb("ones", [128, 128], F32)
    eps, epb = kb.sb("eps", [128, 1], F32)
    kb.eps_col = eps
    kb.op("dve", lambda e: e.memset(ones, 1.0), writes=[onb])
    kb.op("dve", lambda e: e.memset(eps, RMS_EPS), writes=[epb])

    xs = [kb.sb("x%d" % i, [128, 8, W], F32) for i in range(2)]
    ms = [kb.sb("m%d" % i, [128, 8, W], BF16) for i in range(2)]
    xmid, xmb = kb.sb("xmid", [128, 8, W], F32)
    h2, h2b = kb.sb("h2", [128, 8, W], BF16)
    actb, actbb = kb.sb("actb", [128, NF, W], BF16)
    xnew, xnb = xmid, xmb
    carry, carb = kb.sb("carry", [128, NF, 2], F32)
    aext = [kb.sb("aext%d" % i, [128, W + 2], F32) for i in range(2)]
    cs = [kb.sb("cs%d" % i, [128, W], F32) for i in range(2)]
    sg = [kb.sb("sg%d" % i, [128, W], F32) for i in range(2)]
    sq, sqb = kb.sb("sq", [128, W], F32)
    ssum, ssb = kb.sb("ssum", [128, W], F32)
    rstd, rsb = kb.sb("rstd", [128, W], F32)
    pa = [kb.ps("pa%d" % i, [128, 512], F32) for i in range(2)]
    pu = [kb.ps("pu%d" % i, [128, 512], F32) for i in range(2)]
    pacc = [kb.ps("pacc%d" % i, [128, 512], F32) for i in range(2)]
    pst, pstb = kb.ps("pst", [128, 512], F32)

    xTv = xT.rearrange("(k p) t -> p k t", p=128)
    mTv = mixT.rearrange("(k p) t -> p k t", p=128)
    xhv = xh.rearrange("(k p) t -> p k t", p=128)
    mhv = mixh.rearrange("(k p) t -> p k t", p=128)
    xov = xo.rearrange("(k p) t -> p k t", p=128)
    hov = ho.rearrange("(k p) t -> p k t", p=128)

    nchunks = NT // W
    acc_i = 0
    for c in range(-1, nchunks):
        halo = c < 0
        Wc = 2 if halo else W
        xt, xb = xs[c % 2]
        mt, mb = ms[c % 2]
        if halo:
            kb.dma("sp", xt[:, :, :Wc], xhv, reads=[xhb], writes=[xb])
            kb.dma("pool", mt[:, :, :Wc], mhv, reads=[mixhb], writes=[mb])
        else:
            kb.dma("sp", xt, xTv[:, :, c * W:(c + 1) * W], reads=[xTb], writes=[xb])
            kb.dma("pool", mt, mTv[:, :, c * W:(c + 1) * W], reads=[mixTb], writes=[mb])
        for m in range(8):
            p, pb = pacc[acc_i % 2]
            acc_i += 1
            for k in range(8):
                kb.op("pe", lambda e, k=k, m=m, p=p: e.matmul(p[:, :Wc], lhsT=wo_s[:, k, m * 128:(m + 1) * 128],
                                                           rhs=mt[:, k, :Wc], start=(k == 0), stop=(k == 7)),
                      reads=[wo_b, mb], writes=[pb])
            kb.op("dve", lambda e, m=m, p=p: e.tensor_tensor(out=xmid[:, m, :Wc], in0=p[:, :Wc], in1=xt[:, m, :Wc],
                                                          op=ALU.add),
                  reads=[pb, xb], writes=[xmb])
        emit_rmsnorm(kb, xmid, xmb, gfc, ones, Wc, sq, sqb, ssum, ssb, pst, pstb, rstd, rsb, h2, h2b)
        for f in range(NF):
            a, ab = pa[f % 2]
            u, ub = pu[f % 2]
            ae, aeb = aext[f % 2]
            ct, ctb = cs[f % 2]
            st, stb = sg[f % 2]
            for k in range(8):
                kb.op("pe", lambda e, k=k, f=f, a=a: e.matmul(a[:, :Wc], lhsT=wg_s[:, k, f * 128:(f + 1) * 128],
                                                           rhs=h2[:, k, :Wc], start=(k == 0), stop=(k == 7)),
                      reads=[wg_b, h2b], writes=[ab])
            if halo:
                kb.op("act", lambda e, f=f, a=a: e.activation(out=carry[:, f, :], in_=a[:, :2], func=AF.Copy),
                      reads=[ab], writes=[carb])
                continue
            for k in range(8):
                kb.op("pe", lambda e, k=k, f=f, u=u: e.matmul(u[:, :Wc], lhsT=wu_s[:, k, f * 128:(f + 1) * 128],
                                                           rhs=h2[:, k, :Wc], start=(k == 0), stop=(k == 7)),
                      reads=[wu_b, h2b], writes=[ub])
            kb.op("pool", lambda e, f=f, ae=ae: e.tensor_copy(out=ae[:, 0:2], in_=carry[:, f, :]),
                  reads=[carb], writes=[aeb])
            kb.op("act", lambda e, a=a, ae=ae: e.activation(out=ae[:, 2:2 + W], in_=a[:, :W], func=AF.Copy),
                  reads=[ab], writes=[aeb])
            kb.op("act", lambda e, f=f, a=a, ct=ct: e.activation(out=ct, in_=a[:, :W], func=AF.Identity,
                                                              scale=cws[:, f, 2:3], bias=cbs[:, f:f + 1]),
                  reads=[ab, cwsb, cbsb], writes=[ctb])
            kb.op("pool", lambda e, f=f, ae=ae: e.tensor_copy(out=carry[:, f, :], in_=ae[:, W:W + 2]),
                  reads=[aeb], writes=[carb])
            kb.op("dve", lambda e, f=f, ae=ae, ct=ct: e.scalar_tensor_tensor(out=ct, in0=ae[:, 1:1 + W],
                                                                          scalar=cws[:, f, 1:2], in1=ct,
                                                                          op0=ALU.mult, op1=ALU.add),
                  reads=[aeb, ctb, cwsb], writes=[ctb])
            kb.op("dve", lambda e, f=f, ae=ae, ct=ct: e.scalar_tensor_tensor(out=ct, in0=ae[:, 0:W],
                                                                          scalar=cws[:, f, 0:1], in1=ct,
                                                                          op0=ALU.mult, op1=ALU.add),
                  reads=[aeb, ctb, cwsb], writes=[ctb])
            kb.op("act", lambda e, ct=ct, st=st: e.activation(out=st, in_=ct, func=AF.Silu),
                  reads=[ctb], writes=[stb])
            kb.op("dve", lambda e, f=f, st=st, u=u: e.tensor_tensor(out=actb[:, f, :], in0=u[:, :W], in1=st,
                                                                 op=ALU.mult),
                  reads=[ub, stb], writes=[actbb])
        if halo:
            continue
        for m in range(8):
            p, pb = pacc[acc_i % 2]
            acc_i += 1
            for f in range(NF):
                kb.op("pe", lambda e, f=f, m=m, p=p: e.matmul(p[:, :W], lhsT=wd_s[:, f, m * 128:(m + 1) * 128],
                                                           rhs=actb[:, f, :], start=(f == 0), stop=(f == NF - 1)),
                      reads=[wd_b, actbb], writes=[pb])
            kb.op("dve", lambda e, m=m, p=p: e.tensor_tensor(out=xnew[:, m, :], in0=p[:, :W], in1=xmid[:, m, :],
                                                          op=ALU.add),
                  reads=[pb, xmb], writes=[xnb])
        kb.dma("sp", xov[:, :, c * W:(c + 1) * W], xnew, reads=[xnb], writes=[xob])
        hout, houtb = xt, xb
        emit_rmsnorm(kb, xnew, xnb, gnc, ones, W, sq, sqb, ssum, ssb, pst, pstb, rstd, rsb, hout, houtb)
        kb.dma("sp", hov[:, :, c * W:(c + 1) * W], hout, reads=[houtb], writes=[hob])
    kb.finish([xob, hob])
    return nc


def mm(kb, out, lhsT, rhs, start, stop, reads, writes):
    return kb.op("pe", lambda e: e.matmul(out, lhsT=lhsT, rhs=rhs, start=start, stop=stop),
                 reads=reads, writes=writes)


def actf(kb, out, in_, func, reads, writes, **kw):
    return kb.op("act", lambda e: e.activation(out=out, in_=in_, func=func, **kw), reads=reads, writes=writes)


def tt(kb, out, in0, in1, op, reads, writes, eng="dve"):
    return kb.op(eng, lambda e: e.tensor_tensor(out=out, in0=in0, in1=in1, op=op), reads=reads, writes=writes)


def ts(kb, out, in0, s1, op0, reads, writes, s2=None, op1=None, eng="dve"):
    if op1 is None:
        return kb.op(eng, lambda e: e.tensor_scalar(out=out, in0=in0, scalar1=s1, scalar2=None, op0=op0),
                     reads=reads, writes=writes)
    return kb.op(eng, lambda e: e.tensor_scalar(out=out, in0=in0, scalar1=s1, scalar2=s2, op0=op0, op1=op1),
                 reads=reads, writes=writes)


def stt(kb, out, in0, scalar, in1, op0, op1, reads, writes):
    return kb.op("dve", lambda e: e.scalar_tensor_tensor(out=out, in0=in0, scalar=scalar, in1=in1,
                                                        op0=op0, op1=op1), reads=reads, writes=writes)


def transp(kb, out, in_, ident, reads, writes):
    return kb.op("pe", lambda e: e.transpose(out, in_, ident), reads=reads, writes=writes)


CONV_K = 31
HALO = 32
LN_EPS = 1e-5


def build_stage_conv():
    kb = KB()
    nc = kb.nc
    W = 512
    NT = TQ
    hT, hTb = kb.dram("hT", [D, HALO + NT], F32, "ExternalInput")
    wc, wcb = kb.dram("w_conv", [D, 512], F32, "ExternalInput")
    dww, dwwb = kb.dram("dw_w", [128, 2, CONV_K], F32, "ExternalInput")
    dwb, dwbb = kb.dram("dw_b", [128, 2], F32, "ExternalInput")
    lng, lngb = kb.dram("ln_g", [128, 2], F32, "ExternalInput")
    lnb, lnbb = kb.dram("ln_b", [128, 2], F32, "ExternalInput")
    oT, oTb = kb.dram("convT", [256, NT], F32, "ExternalOutput")

    wcs, wcsb = kb.sb("wcs", [128, 8, 512], BF16)
    wcv = wc.rearrange("(k p) n -> p k n", p=128)
    for k in range(8):
        kb.dma("pool", wcs[:, k, :], wcv[:, k, :], reads=[wcb], writes=[wcsb])
    dws, dwsb = kb.sb("dws", [128, 2, CONV_K], F32)
    dbs, dbsb = kb.sb("dbs", [128, 2], F32)
    lgs, lgsb = kb.sb("lgs", [128, 2], F32)
    lbs, lbsb = kb.sb("lbs", [128, 2], F32)
    kb.dma("sp", dws, dww, reads=[dwwb], writes=[dwsb])
    kb.dma("sp", dbs, dwb, reads=[dwbb], writes=[dbsb])
    kb.dma("sp", lgs, lng, reads=[lngb], writes=[lgsb])
    kb.dma("sp", lbs, lnb, reads=[lnbb], writes=[lbsb])
    ones, onb = kb.sb("ones", [128, 128], F32)
    eps, epb = kb.sb("eps", [128, 1], F32)
    kb.op("dve", lambda e: e.memset(ones, 1.0), writes=[onb])
    kb.op("dve", lambda e: e.memset(eps, LN_EPS), writes=[epb])

    Y, Yb = kb.sb("Y", [128, 2, HALO + NT], F32)
    acc, accb = kb.sb("acc", [128, 2, NT], F32)
    hbs = [kb.sb("hb%d" % i, [128, 8, W], BF16) for i in range(2)]
    sgs = [kb.sb("sg%d" % i, [128, W], F32) for i in range(2)]
    pu = [kb.ps("pu%d" % i, [128, 512], F32) for i in range(2)]
    pg = [kb.ps("pg%d" % i, [128, 512], F32) for i in range(2)]
    hTv = hT.rearrange("(k p) t -> p k t", p=128)
    chunks = [(0, HALO)] + [(HALO + i * W, W) for i in range(NT // W)]
    it = 0
    for ci, (c0, Wc) in enumerate(chunks):
        hb, hbb = hbs[ci % 2]
        kb.dma("pool", hb[:, :, :Wc], hTv[:, :, c0:c0 + Wc], reads=[hTb], writes=[hbb])
        for ct in range(2):
            u, ub = pu[it % 2]
            g, gb = pg[it % 2]
            sg, sgb = sgs[it % 2]
            it += 1
            for k in range(8):
                mm(kb, u[:, :Wc], wcs[:, k, ct * 128:(ct + 1) * 128], hb[:, k, :Wc], k == 0, k == 7,
                   [wcsb, hbb], [ub])
            for k in range(8):
                mm(kb, g[:, :Wc], wcs[:, k, 256 + ct * 128:256 + (ct + 1) * 128], hb[:, k, :Wc], k == 0, k == 7,
                   [wcsb, hbb], [gb])
            actf(kb, sg[:, :Wc], g[:, :Wc], AF.Sigmoid, [gb], [sgb])
            tt(kb, Y[:, ct, c0:c0 + Wc], u[:, :Wc], sg[:, :Wc], ALU.mult, [ub, sgb], [Yb])
    HW = NT // 2
    accbs = [[Buf("acc%d%d" % (ct, h)) for h in range(2)] for ct in range(2)]
    for ct in range(2):
        for h in range(2):
            o0 = h * HW
            ab = accbs[ct][h]
            actf(kb, acc[:, ct, o0:o0 + HW], Y[:, ct, HALO + o0:HALO + o0 + HW], AF.Identity, [Yb, dwsb, dbsb], [ab],
                 scale=dws[:, ct, CONV_K - 1:CONV_K], bias=dbs[:, ct:ct + 1])
            for k in range(CONV_K - 1):
                s0 = HALO - (CONV_K - 1) + k + o0
                stt(kb, acc[:, ct, o0:o0 + HW], Y[:, ct, s0:s0 + HW], dws[:, ct, k:k + 1], acc[:, ct, o0:o0 + HW],
                    ALU.mult, ALU.add, [Yb, dwsb, ab], [ab])
    sq = [kb.sb("sq%d" % i, [128, W], F32) for i in range(2)]
    mean, meanb = kb.sb("mean", [128, W], F32)
    msq, msqb = kb.sb("msq", [128, W], F32)
    var, varb = kb.sb("var", [128, W], F32)
    z = [kb.sb("z%d" % i, [128, W], F32) for i in range(2)]
    ot = [kb.sb("ot%d" % i, [128, W], F32) for i in range(2)]
    p1, p1b = pu[0]
    p2, p2b = pu[1]
    oTv = oT.rearrange("(c p) t -> p c t", p=128)
    for n in range(NT // W):
        sl = slice(n * W, (n + 1) * W)
        ab = [accbs[0][n * W // HW], accbs[1][n * W // HW]]
        for ct in range(2):
            mm(kb, p1, ones, acc[:, ct, sl], ct == 0, ct == 1, [onb, ab[ct]], [p1b])
        for ct in range(2):
            s, sb_ = sq[ct]
            actf(kb, s, acc[:, ct, sl], AF.Square, [ab[ct]], [sb_])
            mm(kb, p2, ones, s, ct == 0, ct == 1, [onb, sb_], [p2b])
        ts(kb, mean, p1, 1.0 / 256, ALU.mult, [p1b], [meanb])
        tt(kb, msq, mean, mean, ALU.mult, [meanb], [msqb])
        stt(kb, var, p2, 1.0 / 256, msq, ALU.mult, ALU.subtract, [p2b, msqb], [varb])
        actf(kb, var, var, AF.Sqrt, [varb, epb], [varb], bias=eps, scale=1.0)
        kb.op("dve", lambda e: e.reciprocal(out=var, in_=var), reads=[varb], writes=[varb])
        for ct in range(2):
            zt, zb = z[ct]
            o, ob = ot[ct]
            tt(kb, zt, acc[:, ct, sl], mean, ALU.subtract, [ab[ct], meanb], [zb])
            tt(kb, zt, zt, var, ALU.mult, [zb, varb], [zb])
            actf(kb, o, zt, AF.Silu, [zb, lgsb, lbsb], [ob], scale=lgs[:, ct:ct + 1], bias=lbs[:, ct:ct + 1])
            kb.dma("sp", oTv[:, ct, sl], o, reads=[ob], writes=[oTb])
    kb.finish([oTb])
    return nc


def _cols(v, n):
    return np.ascontiguousarray(np.asarray(v, np.float32).reshape(n, 128).T)


def conv_inputs(P, l, hT):
    return {"hT": hT,
            "w_conv": np.ascontiguousarray(P["w_in"][l][:, 0:512]),
            "dw_w": np.ascontiguousarray(P["conv_dw_w"][l].reshape(CONV_K, 2, 128).transpose(2, 1, 0)),
            "dw_b": _cols(P["conv_dw_b"][l], 2), "ln_g": _cols(P["conv_ln_g"][l], 2),
            "ln_b": _cols(P["conv_ln_b"][l], 2)}


RW_COLS = 320
GN_EPS = 64e-5
CH = 128
RW_NTOK = SEQ


def build_stage_rwkv(ntok=None, dbg=0):
    ntok = ntok or RW_NTOK
    kb = KB()
    nc = kb.nc
    GW = 512
    hT, hTb = kb.dram("hT", [D, ntok], F32, "ExternalInput")
    wr, wrb = kb.dram("w_r", [D, RW_COLS], F32, "ExternalInput")
    mu, mub = kb.dram("mu_bc", [128, RW_COLS], F32, "ExternalInput")
    vecs, vecsb = kb.dram("vecs", [64, 8], F32, "ExternalInput")
    w2d, w2db = kb.dram("w2", [32, 64], F32, "ExternalInput")
    a2d, a2db = kb.dram("a2", [32, 64], F32, "ExternalInput")
    g2d, g2db = kb.dram("g2", [64, 64], F32, "ExternalInput")
    lgd, lgdb = kb.dram("lng_bc", [128, 64], F32, "ExternalInput")
    lbd, lbdb = kb.dram("lnb_bc", [128, 64], F32, "ExternalInput")
    yT, yTb = kb.dram("yT", [64, ntok], F32, "ExternalOutput")

    dif, difb = kb.sb("dif", [128, 128], F32)
    kb.op("pool", lambda e: e.iota(dif, pattern=[[1, 128]], base=0, channel_multiplier=-1, allow_small_or_imprecise_dtypes=True), writes=[difb])
    ident, identb = kb.sb("ident", [128, 128], F32)
    mU2, mU2b = kb.sb("mU2", [128, 256], F32)
    mL, mLb = kb.sb("mL", [128, 128], F32)
    ts(kb, ident, dif, 0.0, ALU.is_equal, [difb], [identb])
    ts(kb, mU2[:, 0:128], dif, 0.0, ALU.is_gt, [difb], [mU2b])
    ts(kb, mU2[:, 128:256], dif, 0.0, ALU.is_ge, [difb], [mU2b])
    ts(kb, mL, dif, 0.0, ALU.is_lt, [difb], [mLb])
    ones64, o64b = kb.sb("ones64", [64, 64], F32)
    kb.op("dve", lambda e: e.memset(ones64, 1.0), writes=[o64b])
    rmask, rmb = kb.sb("rmask", [64, GW], F32)
    kb.op("dve", lambda e: e.memset(rmask, 1.0), writes=[rmb])
    for i in range(GW // CH):
        kb.op("dve", lambda e, i=i: e.memset(rmask[:, i * CH:i * CH + 1], 0.0), writes=[rmb])
    geps, gepsb = kb.sb("geps", [128, 1], F32)
    kb.op("dve", lambda e: e.memset(geps, GN_EPS), writes=[gepsb])

    wf, wfb = kb.sb("wf", [128, 8, RW_COLS], F32)
    mus, musb = kb.sb("mus", [128, RW_COLS], F32)
    wA, wAb = kb.sb("wA", [128, 8, RW_COLS], BF16)
    wB, wBb = kb.sb("wB", [128, 8, RW_COLS], BF16)
    kb.dma("sp", wf, wr.rearrange("(k p) n -> p k n", p=128), reads=[wrb], writes=[wfb])
    kb.dma("sp", mus, mu, reads=[mub], writes=[musb])
    tmpw, tmpwb = kb.sb("tmpw", [128, RW_COLS], F32)
    for k in range(8):
        tt(kb, tmpw, wf[:, k, :], mus, ALU.mult, [wfb, musb], [tmpwb])
        kb.op("dve", lambda e, k=k: e.tensor_copy(out=wB[:, k, :], in_=tmpw), reads=[tmpwb], writes=[wBb])
        tt(kb, wA[:, k, :], wf[:, k, :], tmpw, ALU.subtract, [wfb, tmpwb], [wAb])
    vs_, vsb = kb.sb("vecs_s", [64, 8], F32)
    w2s, w2sb = kb.sb("w2s", [32, 64], F32)
    a2s, a2sb = kb.sb("a2s", [32, 64], F32)
    g2s, g2sb = kb.sb("g2s", [64, 64], F32)
    lgs, lgsb = kb.sb("lgs", [128, 64], F32)
    lbs, lbsb = kb.sb("lbs", [128, 64], F32)
    kb.dma("sp", vs_, vecs, reads=[vecsb], writes=[vsb])
    kb.dma("sp", w2s, w2d, reads=[w2db], writes=[w2sb])
    kb.dma("sp", a2s, a2d, reads=[a2db], writes=[a2sb])
    kb.dma("sp", g2s, g2d, reads=[g2db], writes=[g2sb])
    kb.dma("sp", lgs, lgd, reads=[lgdb], writes=[lgsb])
    kb.dma("sp", lbs, lbd, reads=[lbdb], writes=[lbsb])
    W0, A0, KK_, KA_, RK_ = (vs_[:, i:i + 1] for i in range(5))

    hbs = [kb.sb("hb%d" % i, [128, 8, GW + 1], BF16) for i in range(2)]

    def t64(name, p=64, w=GW):
        return kb.sb(name, [p, w], F32)
    r_s, r_sb = t64("r_s")
    k_s, k_sb = t64("k_s")
    th, thb = t64("th", 32)
    xa_s, xa_sb = t64("xa_s", 32)
    sgx, sgxb = t64("sgx")
    ld, ldb = t64("ld")
    a_, a_b = t64("a_")
    kkr, kkrb = t64("kkr")
    sqk, sqkb = t64("sqk")
    rn, rnb = t64("rn")
    kk, kkb = t64("kk")
    t1, t1b = t64("t1")
    kp, kpb = t64("kp")
    b_, b_b = t64("b_")
    L_, L_b = t64("L_")
    Lex, Lexb = t64("Lex")
    eL, eLb = t64("eL")
    eLex, eLexb = t64("eLex")
    enL, enLb = t64("enL")
    KR, KRb = kb.sb("KR", [64, GW // CH, 2, CH], F32)
    bt, btb = t64("bt")
    kt, ktb = t64("kt")
    rkp, rkpb = t64("rkp")
    yout, youtb = t64("yout")
    MABs = [kb.sb("MAB%d" % i, [128, 256], F32) for i in range(2)]
    MAKs = [kb.sb("MAK%d" % i, [128, 256], F32) for i in range(2)]
    Nns = [kb.sb("Nn%d" % i, [128, 128], F32) for i in range(2)]
    MPs = [[kb.sb("MP%d_%d" % (j, i), [128, 256], F32) for i in range(3)] for j in range(2)]
    M64s = [kb.sb("M16_%d" % i, [128, 128], F32) for i in range(2)]
    BKs = [kb.sb("BK%d" % i, [128, 128], F32) for i in range(2)]
    Vs = [kb.sb("V_%d" % i, [128, 64], F32) for i in range(2)]
    Xs = [kb.sb("X%d" % i, [128, 64], F32) for i in range(2)]
    U_, U_b = kb.sb("U_", [128, 64], F32)
    Ss = [kb.sb("S%d" % i, [64, 64], F32) for i in range(2)]
    st6, st6b = kb.sb("st6", [128, 6], F32)
    mv, mvb = kb.sb("mv", [128, 2], F32)
    rs_, rs_b = kb.sb("rs_", [128, 1], F32)
    yn, ynb = kb.sb("yn", [128, 64], F32)
    bs_, bs_b = kb.sb("bs_", [128, 2], F32)
    yo, yob = kb.sb("yo", [128, 64], F32)

    pj = [kb.ps("pj%d" % i, [128, 512], F32) for i in range(2)]
    pA, pAb = kb.ps("pA", [128, 512], F32)
    pB, pBb = kb.ps("pB", [128, 512], F32)
    pW, pWb = kb.ps("pW", [128, 512], F32)
    pX, pXb = kb.ps("pX", [128, 512], F32)
    pY, pYb = kb.ps("pY", [128, 512], F32)
    pT, pTb = kb.ps("pT", [128, 512], F32)

    kb.op("dve", lambda e: e.memset(Ss[0][0], 0.0), writes=[Ss[0][1]])
    s_i = 0
    hTv = hT.rearrange("(k p) t -> p k t", p=128)
    pji = 0
    for g in range(ntok // GW):
        t0 = g * GW
        hb, hbb = hbs[g % 2]
        if g == 0:
            kb.op("pool", lambda e: e.memset(hb[:, :, 0:1], 0.0), writes=[hbb])
            kb.dma("pool", hb[:, :, 1:GW + 1], hTv[:, :, 0:GW], reads=[hTb], writes=[hbb])
        else:
            kb.dma("pool", hb, hTv[:, :, t0 - 1:t0 + GW], reads=[hTb], writes=[hbb])

        def proj(c0, m):
            nonlocal pji
            p, pb = pj[pji % 2]
            pji += 1
            for k in range(8):
                mm(kb, p[:m, :], wA[:, k, c0:c0 + m], hb[:, k, 1:GW + 1], k == 0, False, [wAb, hbb], [pb])
                mm(kb, p[:m, :], wB[:, k, c0:c0 + m], hb[:, k, 0:GW], False, k == 7, [wBb, hbb], [pb])
            return p, pb
        p, pb = proj(0, 64)
        actf(kb, r_s, p[:64, :], AF.Copy, [pb], [r_sb])
        p, pb = proj(64, 64)
        actf(kb, k_s, p[:64, :], AF.Copy, [pb], [k_sb])
        p, pb = proj(192, 32)
        actf(kb, th, p[:32, :], AF.Tanh, [pb], [thb])
        p, pb = proj(224, 32)
        actf(kb, xa_s, p[:32, :], AF.Copy, [pb], [xa_sb])
        p, pb = proj(256, 64)
        actf(kb, sgx, p[:64, :], AF.Sigmoid, [pb], [sgxb])
        mm(kb, pT[:64, :], w2s, th, True, True, [w2sb, thb], [pTb])
        actf(kb, ld, pT[:64, :], AF.Sigmoid, [pTb, vsb], [ldb], bias=W0)
        ts(kb, ld, ld, -0.6065306597126334, ALU.mult, [ldb], [ldb])
        mm(kb, pT[:64, :], a2s, xa_s, True, True, [a2sb, xa_sb], [pTb])
        actf(kb, a_, pT[:64, :], AF.Sigmoid, [pTb, vsb], [a_b], bias=A0)
        ts(kb, kkr, k_s, KK_, ALU.mult, [k_sb, vsb], [kkrb])
        actf(kb, sqk, kkr, AF.Square, [kkrb], [sqkb])
        mm(kb, pT[:64, :], ones64, sqk, True, True, [o64b, sqkb], [pTb])
        actf(kb, rn, pT[:64, :], AF.Sqrt, [pTb], [rnb])
        ts(kb, rn, rn, 1e-12, ALU.max, [rnb], [rnb])
        kb.op("dve", lambda e: e.reciprocal(out=rn, in_=rn), reads=[rnb], writes=[rnb])
        tt(kb, kk, kkr, rn, ALU.mult, [kkrb, rnb], [kkb])
        ts(kb, t1, a_, -1.0, ALU.add, [a_b, vsb], [t1b], s2=KA_, op1=ALU.mult)
        stt(kb, kp, t1, 1.0, k_s, ALU.add, ALU.mult, [t1b, k_sb], [kpb])
        tt(kb, b_, kk, a_, ALU.mult, [kkb, a_b], [b_b])
        kb.op("dve", lambda e: e.tensor_tensor_scan(out=L_, data0=rmask, data1=ld, initial=0.0,
                                                    op0=ALU.mult, op1=ALU.add),
              reads=[rmb, ldb], writes=[L_b])
        tt(kb, Lex, L_, ld, ALU.subtract, [L_b, ldb], [Lexb])
        actf(kb, eL, L_, AF.Exp, [L_b], [eLb])
        actf(kb, eLex, Lex, AF.Exp, [Lexb], [eLexb])
        actf(kb, enL, L_, AF.Exp, [L_b], [enLb], scale=-1.0)
        c4 = "p (c t) -> p c t"
        tt(kb, KR[:, :, 0, :], kk.rearrange(c4, t=CH), eLex.rearrange(c4, t=CH), ALU.mult, [kkb, eLexb], [KRb])
        tt(kb, KR[:, :, 1, :], r_s.rearrange(c4, t=CH), eL.rearrange(c4, t=CH), ALU.mult, [r_sb, eLb], [KRb])
        tt(kb, bt, b_, enL, ALU.mult, [b_b, enLb], [btb])
        tt(kb, kt, kp, enL, ALU.mult, [kpb, enLb], [ktb])
        stt(kb, rkp, r_s, RK_, kp, ALU.mult, ALU.mult, [r_sb, vsb, kpb], [rkpb])

        if dbg == 1:
            kb.dma("sp", yT[:, t0:t0 + GW], kt, reads=[ktb], writes=[yTb])
            continue
        def pre(i, q):
            cs = slice(i * CH, (i + 1) * CH)
            MAB, MABb = MABs[q]
            MAK, MAKb = MAKs[q]
            Nn, Nnb = Nns[q]
            MPq = MPs[q]
            M64, M64b = M64s[q]
            BK, BKb = BKs[q]
            V_, V_b = Vs[q]
            KRi = KR[:, i, :, :].rearrange("p a t -> p (a t)")
            mm(kb, pA[:, 0:256], bt[:, cs], KRi, True, True, [btb, KRb], [pAb])
            mm(kb, pB[:, 0:256], kt[:, cs], KRi, True, True, [ktb, KRb], [pBb])
            mm(kb, pB[:, 256:384], KR[:, i, 0, :], bt[:, cs], True, True, [KRb, btb], [pBb])
            yield
            tt(kb, MAB, pA[:, 0:256], mU2, ALU.mult, [pAb, mU2b], [MABb])
            tt(kb, MAK, pB[:, 0:256], mU2, ALU.mult, [pBb, mU2b], [MAKb])
            tt(kb, Nn, pB[:, 256:384], mL, ALU.mult, [pBb, mLb], [Nnb])
            yield
            cM, cMb, cN, cNb = MAB[:, 0:128], MABb, Nn, Nnb
            for pi in range(3):
                mm(kb, pW[:, 0:128], cN, cM, True, True, [cNb, cMb], [pWb])
                mm(kb, pW[:, 128:256], cM, cN, True, True, [cNb, cMb], [pWb])
                yield
                mp, mpb = MPq[pi]
                actf(kb, mp, pW[:, 0:256], AF.Copy, [pWb], [mpb])
                cM, cMb, cN, cNb = mp[:, 0:128], mpb, mp[:, 128:256], mpb
                yield
            mm(kb, pW[:, 0:128], cN, cM, True, True, [cNb, cMb], [pWb])
            transp(kb, pT[:, 0:64], bt[:, cs], ident[0:64, 0:64], [btb, identb], [pTb])
            transp(kb, pT[:, 64:128], kt[:, cs], ident[0:64, 0:64], [ktb, identb], [pTb])
            yield
            actf(kb, M64, pW[:, 0:128], AF.Copy, [pWb], [M64b])
            kb.op("dve", lambda e: e.tensor_copy(out=BK, in_=pT[:, 0:128]), reads=[pTb], writes=[BKb])
            for k in range(8):
                mm(kb, pT[:, 128:192], hb[:, k, 1 + i * CH:1 + (i + 1) * CH], wA[:, k, 128:192], k == 0, False,
                   [hbb, wAb], [pTb])
                mm(kb, pT[:, 128:192], hb[:, k, i * CH:(i + 1) * CH], wB[:, k, 128:192], False, k == 7,
                   [hbb, wBb], [pTb])
            yield
            kb.op("dve", lambda e: e.tensor_copy(out=V_, in_=pT[:, 128:192]), reads=[pTb], writes=[V_b])
            yield

        def dep(i, q):
            nonlocal s_i
            cs = slice(i * CH, (i + 1) * CH)
            MAB, MABb = MABs[q]
            MAK, MAKb = MAKs[q]
            MPq = MPs[q]
            M64, M64b = M64s[q]
            BK, BKb = BKs[q]
            V_, V_b = Vs[q]
            S0, S0b = Ss[s_i % 2]
            S1, S1b = Ss[(s_i + 1) % 2]
            s_i += 1
            mm(kb, pX[:, 0:64], KR[:, i, 0, :], S0, True, False, [KRb, S0b], [pXb])
            mm(kb, pX[:, 0:64], MAK[:, 0:128], V_, False, True, [MAKb, V_b], [pXb])
            yield
            xi = 0
            X, Xb = Xs[xi]
            ts(kb, X, pX[:, 0:64], -1.0, ALU.mult, [pXb], [Xb])
            yield
            for (Mp_, Mpb_) in [(M64, M64b)] + [(MPq[pi][0][:, 0:128], MPq[pi][1]) for pi in (2, 1, 0)]:
                mm(kb, pX[:, 0:64], ident, X, True, False, [identb, Xb], [pXb])
                mm(kb, pX[:, 0:64], Mp_, X, False, True, [Mpb_, Xb], [pXb])
                yield
                xi += 1
                X, Xb = Xs[xi % 2]
                kb.op("dve", lambda e, X=X: e.tensor_copy(out=X, in_=pX[:, 0:64]), reads=[pXb], writes=[Xb])
                yield
            mm(kb, pX[:, 0:64], MAB[:, 0:128], X, True, True, [MABb, Xb], [pXb])
            yield
            tt(kb, U_, X, pX[:, 0:64], ALU.subtract, [Xb, pXb], [U_b])
            yield
            mm(kb, pY[:64, 64:128], ident[0:64, 0:64], S0, True, False, [identb, S0b], [pYb])
            mm(kb, pY[:64, 64:128], BK[:, 0:64], U_, False, False, [BKb, U_b], [pYb])
            mm(kb, pY[:64, 64:128], BK[:, 64:128], V_, False, True, [BKb, V_b], [pYb])
            mm(kb, pY[:, 0:64], KR[:, i, 1, :], S0, True, False, [KRb, S0b], [pYb])
            mm(kb, pY[:, 0:64], MAB[:, 128:256], U_, False, False, [MABb, U_b], [pYb])
            mm(kb, pY[:, 0:64], MAK[:, 128:256], V_, False, True, [MAKb, V_b], [pYb])
            yield
            ts(kb, S1, pY[:64, 64:128], eL[:, i * CH + CH - 1:i * CH + CH], ALU.mult, [pYb, eLb], [S1b])
            kb.op("dve", lambda e: e.bn_stats(out=st6, in_=pY[:, 0:64]), reads=[pYb], writes=[st6b])
            kb.op("dve", lambda e: e.bn_aggr(out=mv, in_=st6), reads=[st6b], writes=[mvb])
            p0, p0b = pj[0]
            p1, p1b = pj[1]
            mm(kb, p0[:, 0:2], rkp[:, cs], ones64[:, 0:2], True, True, [rkpb, o64b], [p0b])
            mm(kb, p0[:, 64:128], sgx[:, cs], g2s, True, True, [sgxb, g2sb], [p0b])
            yield
            actf(kb, rs_, mv[:, 1:2], AF.Sqrt, [mvb, gepsb], [rs_b], bias=geps, scale=1.0)
            kb.op("dve", lambda e: e.tensor_copy(out=bs_, in_=p0[:, 0:2]), reads=[p0b], writes=[bs_b])
            yield
            kb.op("dve", lambda e: e.reciprocal(out=rs_, in_=rs_), reads=[rs_b], writes=[rs_b])
            ts(kb, yn, pY[:, 0:64], mv[:, 0:1], ALU.subtract, [pYb, mvb, rs_b], [ynb], s2=rs_, op1=ALU.mult)
            tt(kb, yn, yn, lgs, ALU.mult, [ynb, lgsb], [ynb])
            tt(kb, yn, yn, lbs, ALU.add, [ynb, lbsb], [ynb])
            stt(kb, yn, V_, bs_[:, 0:1], yn, ALU.mult, ALU.add, [V_b, bs_b, ynb], [ynb])
            tt(kb, yo, yn, p0[:, 64:128], ALU.mult, [ynb, p0b], [yob])
            yield
            transp(kb, p1[:64, 0:128], yo, ident, [yob, identb], [p1b])
            yield
            actf(kb, yout[:, cs], p1[:64, 0:128], AF.Copy, [p1b], [youtb])
            yield

        def zip_run(gens):
            gens = [g_ for g_ in gens if g_ is not None]
            while gens:
                for g_ in list(gens):
                    try:
                        next(g_)
                    except StopIteration:
                        gens.remove(g_)

        nchunk = GW // CH
        pending = None
        for i in range(nchunk):
            q = (g * nchunk + i) % 2
            zip_run([pre(i, q), pending])
            pending = dep(i, q)
        zip_run([pending])
        kb.dma("sp", yT[:, t0:t0 + GW], yout, reads=[youtb], writes=[yTb])
    kb.finish([yTb])
    return nc


RW_OFF = 512 + 1304


def rwkv_inputs(P, l, h, hT):
    o = RW_OFF
    cols = np.concatenate([np.arange(o + h * 64, o + h * 64 + 64), np.arange(o + 256 + h * 64, o + 256 + h * 64 + 64),
                           np.arange(o + 512 + h * 64, o + 512 + h * 64 + 64), np.arange(o + 768, o + 896)])
    hs = slice(h * 64, (h + 1) * 64)
    vecs = np.zeros((64, 8), np.float32)
    vecs[:, 0] = P["rwkv_w0"][l][hs]
    vecs[:, 1] = P["rwkv_a0"][l][hs]
    vecs[:, 2] = P["rwkv_k_k"][l][hs]
    vecs[:, 3] = P["rwkv_k_a"][l][hs]
    vecs[:, 4] = P["rwkv_r_k"][l][h]
    return {"hT": hT,
            "w_r": np.ascontiguousarray(P["w_in"][l][:, cols]),
            "mu_bc": np.ascontiguousarray(np.broadcast_to(P["rwkv_mu"][l][cols - o][None, :], (128, RW_COLS))),
            "vecs": vecs,
            "w2": np.ascontiguousarray(P["rwkv_w2"][l][:, hs]), "a2": np.ascontiguousarray(P["rwkv_a2"][l][:, hs]),
            "g2": np.ascontiguousarray(P["rwkv_g2"][l][:, hs]),
            "lng_bc": np.ascontiguousarray(np.broadcast_to(P["rwkv_ln_g"][l][hs][None, :], (128, 64))),
            "lnb_bc": np.ascontiguousarray(np.broadcast_to(P["rwkv_ln_b"][l][hs][None, :], (128, 64)))}


NSLOT = 32
NKT = SEQ // 128
TINY = 1e-30


def build_stage_nsa(seq=None, dbg=0):
    seq = seq or SEQ
    TQn = seq // 4
    NS = TQn // 128
    NK = seq // 128
    NBLK = seq // 64
    NCH = seq // 16
    NNT = max(1, NCH // 128)
    NCP = NNT * 128
    BT = (NBLK + 127) // 128
    BR = min(128, NBLK)
    kb = KB()
    nc = kb.nc
    hT, hTb = kb.dram("hT", [D, seq], F32, "ExternalInput")
    hq, hqb_ = kb.dram("hTq", [D, TQn], F32, "ExternalInput")
    wq, wqb_ = kb.dram("w_q", [D, 512], F32, "ExternalInput")
    wkv, wkvb_ = kb.dram("w_kv", [D, 768], F32, "ExternalInput")
    wgt, wgtb_ = kb.dram("w_gt", [D, 24], F32, "ExternalInput")
    w1d, w1db = kb.dram("w1", [128, 2, 32, 64], F32, "ExternalInput")
    w2d, w2db = kb.dram("w2", [128, 2, 64], F32, "ExternalInput")
    ped, pedb = kb.dram("peT", [128, 2, 32], F32, "ExternalInput")
    tqd, tqdb = kb.dram("tq_bc", [128, TQn], F32, "ExternalInput")
    curd, curdb = kb.dram("curcol", [128, NS], F32, "ExternalInput")
    oT, oTb = kb.dram("nsaT", [512, TQn], F32, "ExternalOutput")

    identf, identfb = kb.sb("identf", [128, 128], F32)
    kpos, kposb = kb.sb("kpos", [128, NK], F32)
    cend, cendb = kb.sb("cend", [128, NNT], F32)
    jrow, jrowb = kb.sb("jrow", [128, NBLK], F32)
    E_, E_b = kb.sb("E_", [128, NNT, NBLK], BF16)
    NF = min(64, NK)
    F_, F_b = kb.sb("F_", [128, NF, 128], BF16)
    stA = ExitStack()
    dif, difb = kb.sbs(stA, "dif", [128, 128], F32)
    kb.op("pool", lambda e: e.iota(dif, pattern=[[1, 128]], base=0, channel_multiplier=-1,
                                   allow_small_or_imprecise_dtypes=True), writes=[difb])
    ts(kb, identf, dif, 0.0, ALU.is_equal, [difb], [identfb])
    kb.op("pool", lambda e: e.iota(kpos, pattern=[[128, NK]], base=0, channel_multiplier=1,
                                   allow_small_or_imprecise_dtypes=True), writes=[kposb])
    kb.op("pool", lambda e: e.iota(cend, pattern=[[2048, NNT]], base=31, channel_multiplier=16,
                                   allow_small_or_imprecise_dtypes=True), writes=[cendb])
    kb.op("pool", lambda e: e.iota(jrow, pattern=[[1, NBLK]], base=0, channel_multiplier=0,
                                   allow_small_or_imprecise_dtypes=True), writes=[jrowb])
    ev, evb = kb.sbs(stA, "ev", [128, NNT, NBLK], F32)
    kb.op("pool", lambda e: e.iota(ev, pattern=[[128, NNT], [-4, NBLK]], base=0, channel_multiplier=1,
                                   allow_small_or_imprecise_dtypes=True), writes=[evb])
    ev2, ev2b = kb.sbs(stA, "ev2", [128, NNT, NBLK], F32)
    ts(kb, ev2, ev, -1.0, ALU.is_ge, [evb], [ev2b])
    stt(kb, E_, ev, 3.0, ev2, ALU.is_le, ALU.mult, [evb, ev2b], [E_b])
    fv, fvb = kb.sbs(stA, "fv", [128, NF, 2, 64], F32)
    kb.op("pool", lambda e: e.iota(fv, pattern=[[-2, NF], [-1, 2], [0, 64]], base=0, channel_multiplier=1,
                                   allow_small_or_imprecise_dtypes=True), writes=[fvb])
    ts(kb, F_, fv.rearrange("p a b c -> p a (b c)"), 0.0, ALU.is_equal, [fvb], [F_b])
    kb.release([difb, evb, ev2b, fvb])
    stA.close()

    if dbg == 1:
        kb.finish([])
        return nc
    wq_s, wq_b = kb.sb("wq_s", [128, 8, 512], BF16)
    wgt_s, wgt_b = kb.sb("wgt_s", [128, 8, 24], BF16)
    wqv = wq.rearrange("(k p) n -> p k n", p=128)
    wkvv = wkv.rearrange("(k p) n -> p k n", p=128)
    for k in range(8):
        kb.dma("pool", wq_s[:, k, :], wqv[:, k, :], reads=[wqb_], writes=[wq_b])
    kb.dma("pool", wgt_s, wgt.rearrange("(k p) n -> p k n", p=128), reads=[wgtb_], writes=[wgt_b])
    tqsl = [kb.sb("tqs%d" % i, [128, 128], F32) for i in range(2)]
    curs, cursb = kb.sb("curs", [128, NS], F32)
    kb.dma("sp", curs, curd, reads=[curdb], writes=[cursb])
    kcm, kcmb = kb.sb("kcm", [128, NCP], BF16)
    vcm, vcmb = kb.sb("vcm", [128, NNT, 2, 65], BF16)
    kb.op("pool", lambda e: e.memset(vcm, 1.0), writes=[vcmb])
    kb.op("pool", lambda e: e.memset(kcm, 0.0), writes=[kcmb])
    hqt, hqtb = kb.sb("hq0", [128, 8, 128], BF16)
    QT, QTb = kb.sb("QT", [128, 2, 4, 128], BF16)
    kb.op("pool", lambda e: e.memset(QT, 0.0), writes=[QTb])
    gsb, gsbb = kb.sb("gsb", [128, 24], F32)
    Pt = [kb.sb("Pt%d" % i, [128, 4, 128], BF16) for i in range(2)]
    mk = [kb.sb("mk%d" % i, [128, 128], BF16) for i in range(2)]
    mk2 = [kb.sb("mk2%d" % i, [128, 128], F32) for i in range(2)]
    OT, OTb = kb.sb("OT", [128, 512], F32)
    oacc, oaccb = kb.sb("oacc", [128, 512], F32)
    rec, recb = kb.sb("rec", [128, 4], F32)
    coef, coefb = kb.sb("coef", [128, 4], F32)
    sel, selb = kb.sb("sel", [128, NBLK], F32)
    sc, scb = kb.sb("sc", [128, NBLK], F32)
    sc2, sc2b = kb.sb("sc2", [128, NBLK], F32)
    nf, nfb = kb.sb("nf", [128, NBLK], F32)
    frc, frcb = kb.sb("frc", [128, NBLK], F32)
    m8, m8b = kb.sb("m8", [128, 16], F32)
    bm, bmb = kb.sb("bm", [128, BT * 128], F32)
    nbT4, nbT4b = kb.sb("nbT4", [128, BT, 4, 128], BF16)
    kb.op("pool", lambda e: e.memset(nbT4, 0.0), writes=[nbT4b])
    kb.op("pool", lambda e: e.memset(bm, 0.0), writes=[bmb])
    pp = [kb.ps("pp%d" % i, [128, 512], F32) for i in range(8)]
    hTv = hT.rearrange("(k p) t -> p k t", p=128)
    GW = 512
    NG = seq // GW

    stB = ExitStack()
    wkv_s, wkv_b = kb.sbs(stB, "wkv_s", [128, 8, 256], BF16)
    for k in range(8):
        kb.dma("pool", wkv_s[:, k, :], wkvv[:, k, 0:256], reads=[wkvb_], writes=[wkv_b])
    hb, hbb = kb.sbs(stB, "hb", [128, 8, GW], BF16)
    w1s, w1sb = kb.sbs(stB, "w1s", [128, 2, 32, 64], BF16)
    kb.dma("pool", w1s[:, 0], w1d[:, 0], reads=[w1db], writes=[w1sb])
    kb.dma("pool", w1s[:, 1], w1d[:, 1], reads=[w1db], writes=[w1sb])
    w2s, w2sb = kb.sbs(stB, "w2s", [128, 2, 64], BF16)
    kb.dma("pool", w2s, w2d, reads=[w2db], writes=[w2sb])
    pes, pesb = kb.sbs(stB, "pes", [128, 2, 34], BF16)
    kb.op("pool", lambda e: e.memset(pes, 0.0), writes=[pesb])
    kb.dma("pool", pes[:, :, 0:32], ped, reads=[pedb], writes=[pesb])
    kcT, kcTb = kb.sbs(stB, "kcT", [128, 2, seq], BF16)
    gl, glb = kb.sbs(stB, "gl", [128, NCP], F32)
    gx, gxb = kb.sbs(stB, "gx", [128, NCP], F32)
    gbf, gbfb = kb.sbs(stB, "gbf", [128, NCP], BF16)
    bcol, bcolb = kb.sbs(stB, "bcol", [128, 2], F32)
    w2z, w2zb = kb.sbs(stB, "w2z", [128, 2, 64], BF16)
    if dbg == 11:
        kb.finish([])
        return nc
    for gi in range(NG):
        kb.dma("pool", hb, hTv[:, :, gi * GW:(gi + 1) * GW], reads=[hTb], writes=[hbb])
        for kv in range(2):
            p, pb = pp[(gi * 2 + kv) % 2]
            for k in range(8):
                mm(kb, p, wkv_s[:, k, kv * 128:(kv + 1) * 128], hb[:, k, :], k == 0, k == 7, [wkv_b, hbb], [pb])
            actf(kb, kcT[:, kv, gi * GW:(gi + 1) * GW], p, AF.Copy, [pb], [kcTb])
    if dbg == 12:
        kb.finish([])
        return nc
    kb.op("pool", lambda e: e.memset(gbf, 0.0), writes=[gbfb])
    NV = NCH - 1
    for kv in range(2):
        pbias, pbiasb = pp[2]
        for g in range(2):
            gs = slice(64 * g, 64 * g + 64)
            for l in range(32):
                mm(kb, pbias[gs, 0:2], w1s[gs, kv, l, :], pes[gs, kv, l:l + 2], l == 0, l == 31, [w1sb, pesb],
                   [pbiasb])
        kb.op("dve", lambda e: e.tensor_copy(out=bcol, in_=pbias[:, 0:2]), reads=[pbiasb], writes=[bcolb])
        if dbg == 13:
            kb.finish([])
            return nc
        n0 = 0
        ci = 0
        while n0 < NV:
            nn = min(512, NV - n0)
            pc, pcb = pp[3 + ci % 2]
            ci += 1
            for g in range(2):
                gs = slice(64 * g, 64 * g + 64)
                for l in range(32):
                    src = kcT[gs, kv, 16 * n0 + l:16 * n0 + l + 16 * (nn - 1) + 1:16]
                    mm(kb, pc[gs, 0:nn], w1s[gs, kv, l, :], src, l == 0, l == 31, [w1sb, kcTb], [pcb])
            actf(kb, gx[:, n0:n0 + nn], pc[:, 0:nn], AF.Identity, [pcb, bcolb], [gxb], bias=bcol[:, 0:1], scale=1.0)
            n0 += nn
        if dbg == 14:
            kb.finish([])
            return nc
        tt(kb, gl[:, 0:NV], gx[:, 0:NV], gx[:, 0:NV], ALU.mult, [gxb], [glb])
        ts(kb, gl[:, 0:NV], gl[:, 0:NV], 0.044715, ALU.mult, [glb], [glb], s2=1.0, op1=ALU.add)
        tt(kb, gl[:, 0:NV], gl[:, 0:NV], gx[:, 0:NV], ALU.mult, [glb, gxb], [glb])
        actf(kb, gl[:, 0:NV], gl[:, 0:NV], AF.Sigmoid, [glb], [glb], scale=1.5957691216057308)
        tt(kb, gbf[:, 0:NV], gl[:, 0:NV], gx[:, 0:NV], ALU.mult, [glb, gxb], [gbfb])
        if dbg == 15 or (dbg == 17 and kv == 1):
            kb.finish([])
            return nc
        if kv == 0:
            n0 = 0
            while n0 < NCP:
                nn = min(512, NCP - n0)
                pc, pcb = pp[5]
                for g in range(2):
                    gs = slice(64 * g, 64 * g + 64)
                    mm(kb, pc[gs, 0:nn], w2s[gs, 0, :], gbf[gs, n0:n0 + nn], True, True, [w2sb, gbfb], [pcb])
                actf(kb, kcm[:, n0:n0 + nn], pc[:, 0:nn], AF.Copy, [pcb], [kcmb])
                n0 += nn
            if dbg == 16:
                kb.finish([])
                return nc
        else:
            kb.op("dve", lambda e: e.memset(w2z, 0.0), writes=[w2zb])
            for g in range(2):
                gs = slice(64 * g, 64 * g + 64)
                kb.op("dve", lambda e, g=g, gs=gs: e.tensor_copy(out=w2z[gs, g, :], in_=w2s[gs, 1, :]),
                      reads=[w2sb], writes=[w2zb])
            for nt in range(NNT):
                pc, pcb = pp[5]
                for g in range(2):
                    mm(kb, pc[:, g * 64:(g + 1) * 64], gbf[:, nt * 128:(nt + 1) * 128], w2z[:, g, :], True, True,
                       [w2zb, gbfb], [pcb])
                actf(kb, vcm[:, nt, :, 0:64], pc[:, 0:128].rearrange("p (g d) -> p g d", g=2), AF.Copy,
                     [pcb], [vcmb])
    kb.release([wkv_b, hbb, w1sb, w2sb, pesb, kcTb, glb, gxb, gbfb, bcolb, w2zb])
    stB.close()

    if dbg == 2:
        kb.finish([])
        return nc
    ksT, ksTb = kb.sb("ksT", [128, seq], BF16)
    kwT, kwTb = kb.sb("kwT", [128, seq], BF16)
    vsA, vsAb = kb.sb("vsA", [128, NK, 2, 65], BF16)
    vwA, vwAb = kb.sb("vwA", [128, NK, 2, 65], BF16)
    kb.op("pool", lambda e: e.memset(vsA, 1.0), writes=[vsAb])
    kb.op("pool", lambda e: e.memset(vwA, 1.0), writes=[vwAb])
    stD = ExitStack()
    wk2, wk2b = kb.sbs(stD, "wk2", [128, 8, 512], BF16)
    for k in range(8):
        kb.dma("pool", wk2[:, k, :], wkvv[:, k, 256:768], reads=[wkvb_], writes=[wk2b])
    hb, hbb = kb.sbs(stD, "hb2", [128, 8, GW], BF16)
    for gi in range(NG):
        kb.dma("pool", hb, hTv[:, :, gi * GW:(gi + 1) * GW], reads=[hTb], writes=[hbb])
        for wi, dst, dstb in ((0, ksT, ksTb), (2, kwT, kwTb)):
            p, pb = pp[wi // 2]
            for k in range(8):
                mm(kb, p, wk2[:, k, wi * 128:(wi + 1) * 128], hb[:, k, :], k == 0, k == 7, [wk2b, hbb], [pb])
            actf(kb, dst[:, gi * GW:(gi + 1) * GW], p, AF.Copy, [pb], [dstb])
        for wi, dst, dstb in ((1, vsA, vsAb), (3, vwA, vwAb)):
            p, pb = pp[2 + wi // 2]
            for i4 in range(4):
                for k in range(8):
                    mm(kb, p[:, i4 * 128:(i4 + 1) * 128], hb[:, k, i4 * 128:(i4 + 1) * 128],
                       wk2[:, k, wi * 128:(wi + 1) * 128], k == 0, k == 7, [wk2b, hbb], [pb])
            kb.op("dve", lambda e, p=p, dst=dst, gi=gi: e.tensor_copy(
                out=dst[:, gi * 4:(gi + 1) * 4, :, 0:64],
                in_=p.rearrange("p (i g d) -> p i g d", i=4, g=2)), reads=[pb], writes=[dstb])
    kb.release([wk2b, hbb])
    stD.close()

    if dbg == 3:
        kb.finish([])
        return nc
    pS = [pp[0], pp[1]]
    pS3 = [pp[0], pp[1], pp[3]]
    pO, pOb = pp[2]
    pSel4 = [pp[3], pp[4], pp[5], pp[7]]
    pM, pMb = pp[5]
    pTk, pTkb = pp[6]
    pQ, pQb = pp[7]
    hqv = hq.rearrange("(k p) t -> p k t", p=128)
    oTv = oT.rearrange("(c p) t -> p c t", p=128)
    cnt = [0]

    def attend(g, tiles, kT, kTb_, vA, vAb_, br, first_branch, mask_fn=None, bias_fn=None, extra=None):
        nt_ = len(tiles)
        c0 = cnt[0]
        cnt[0] += nt_
        hasb = bias_fn is not None

        deep = extra is None
        ring = pS3 if deep else pS
        nr = len(ring)
        ahead = nr - 1

        def issue_scores(idx):
            i = tiles[idx]
            S, Sb = ring[(c0 + idx) % nr]
            mm(kb, S, kT[:, i * 128:(i + 1) * 128], QT[:, g, :, :].rearrange("p r q -> p (r q)"), True, not hasb,
               [kTb_, QTb], [Sb])
            if hasb:
                bias_fn(i, S, Sb)
        for a_ in range(min(ahead, nt_)):
            issue_scores(a_)
        for idx, i in enumerate(tiles):
            c = c0 + idx
            S, Sb = ring[c % nr]
            P, Pb = Pt[c % 2]
            if idx + ahead < nt_:
                issue_scores(idx + ahead)
            actf(kb, P.rearrange("p r q -> p (r q)"), S, AF.Exp, [Sb], [Pb])
            mres = mask_fn(i, c) if mask_fn is not None else None
            if mres is not None:
                m_ap, m_b = mres
                tt(kb, P, P, m_ap.rearrange("p (o q) -> p o q", o=1).to_broadcast([128, 4, 128]), ALU.mult,
                   [Pb, m_b], [Pb])
            mm(kb, pO[0:65, :], vA[:, i, g, :], P.rearrange("p r q -> p (r q)"), idx == 0, idx == nt_ - 1,
               [vAb_, Pb], [pOb])
            if extra is not None:
                extra(i, idx, P, Pb, nt_)
        actf(kb, OT[0:65, :], pO[0:65, :], AF.Copy, [pOb], [OTb])
        for r in range(4):
            transp(kb, pTk[:, r * 65:(r + 1) * 65], OT[0:65, r * 128:(r + 1) * 128], identf[0:65, 0:65],
                   [OTb, identfb], [pTkb])
        pv = pTk[:, 0:260].rearrange("p (r e) -> p r e", r=4)
        ts(kb, rec, pv[:, :, 64], TINY, ALU.max, [pTkb], [recb])
        kb.op("dve", lambda e: e.reciprocal(out=rec, in_=rec), reads=[recb], writes=[recb])
        gv = gsb.rearrange("p (g r t) -> p g r t", g=2, r=4)
        tt(kb, coef, rec, gv[:, g, :, br], ALU.mult, [recb, gsbb], [coefb])
        for r in range(4):
            dst = oacc[:, (g * 4 + r) * 64:(g * 4 + r + 1) * 64]
            if first_branch:
                ts(kb, dst, pv[:, r, 0:64], coef[:, r:r + 1], ALU.mult, [pTkb, coefb], [oaccb])
            else:
                stt(kb, dst, pv[:, r, 0:64], coef[:, r:r + 1], dst, ALU.mult, ALU.add, [pTkb, coefb, oaccb], [oaccb])

    for m in range(NS):
        kb.dma("pool", hqt, hqv[:, :, m * 128:(m + 1) * 128], reads=[hqb_], writes=[hqtb])
        for r in range(4):
            for k in range(8):
                mm(kb, pQ[:, r * 128:(r + 1) * 128], wq_s[:, k, r * 128:(r + 1) * 128], hqt[:, k, :], k == 0, k == 7,
                   [wq_b, hqtb], [pQb])
        for g in range(2):
            gs = slice(64 * g, 64 * g + 64)
            actf(kb, QT[gs, g, :, :].rearrange("p r q -> p (r q)"), pQ[gs, :], AF.Copy, [pQb], [QTb], scale=0.125)
        for k in range(8):
            mm(kb, pM[:, 0:24], hqt[:, k, :], wgt_s[:, k, :], k == 0, k == 7, [hqtb, wgt_b], [pMb])
        actf(kb, gsb, pM[:, 0:24], AF.Sigmoid, [pMb], [gsbb])
        tqs, tqsb = tqsl[m % 2]
        kb.dma("sp", tqs, tqd[:, m * 128:(m + 1) * 128], reads=[tqdb], writes=[tqsb])
        curc = curs[:, m:m + 1]
        ts(kb, nf, jrow, curc, ALU.is_le, [jrowb, cursb], [nfb])
        ts(kb, frc, jrow, curc, ALU.is_equal, [jrowb, cursb], [frcb])
        stt(kb, frc, jrow, 0.0, frc, ALU.is_equal, ALU.add, [jrowb, frcb], [frcb])
        ts(kb, sc2, jrow, curc, ALU.subtract, [jrowb, cursb], [sc2b], s2=-1.0, op1=ALU.is_equal)
        tt(kb, frc, frc, sc2, ALU.add, [frcb, sc2b], [frcb])
        ts(kb, frc, frc, 1.0, ALU.min, [frcb], [frcb])
        for g in range(2):
            def cmp_mask(i, c):
                mt, mtb = mk[c % 2]
                ts(kb, mt, tqs, cend[:, i:i + 1], ALU.is_ge, [tqsb, cendb], [mtb])
                return mt, mtb

            def cmp_extra(i, idx, P, Pb, n):
                for r in range(4):
                    ps_, psb = pSel4[r]
                    mm(kb, ps_[:, 0:NBLK], P[:, r, :], E_[:, i, :], idx == 0, idx == n - 1, [Pb, E_b], [psb])
            attend(g, list(range(NNT)), kcm, kcmb, vcm, vcmb, 0, True, mask_fn=cmp_mask, extra=cmp_extra)
            for r in range(4):
                ps_, psb = pSel4[r]
                src = ps_[:, 0:NBLK]
                if r == 0:
                    ts(kb, sel, src, rec[:, 0:1], ALU.mult, [psb, recb], [selb])
                else:
                    stt(kb, sel, src, rec[:, r:r + 1], sel, ALU.mult, ALU.add, [psb, recb, selb], [selb])
            stt(kb, sc, frc, 1.0e4, sel, ALU.mult, ALU.add, [frcb, selb], [scb])
            ts(kb, sc, sc, 1.0, ALU.add, [scb], [scb])
            tt(kb, sc, sc, nf, ALU.mult, [scb, nfb], [scb])
            ts(kb, sc, sc, -1.0, ALU.add, [scb], [scb])
            kb.op("dve", lambda e: e.max(out=m8[:, 0:8], in_=sc), reads=[scb], writes=[m8b])
            kb.op("dve", lambda e: e.match_replace(out=sc2, in_to_replace=m8[:, 0:8], in_values=sc, imm_value=-1e30),
                  reads=[scb, m8b], writes=[sc2b])
            kb.op("dve", lambda e: e.max(out=m8[:, 8:16], in_=sc2), reads=[sc2b], writes=[m8b])
            ts(kb, sc2, sc, m8[:, 15:16], ALU.is_ge, [scb, m8b], [sc2b])
            tt(kb, bm[:, 0:NBLK], sc2, nf, ALU.mult, [sc2b, nfb], [bmb])
            for t2 in range(BT):
                transp(kb, pM[0:BR, t2 * 128:(t2 + 1) * 128], bm[:, t2 * 128:t2 * 128 + BR], identf,
                       [bmb, identfb], [pMb])
            for t2 in range(BT):
                src = pM[0:BR, t2 * 128:(t2 + 1) * 128]
                ts(kb, nbT4[0:BR, t2, :, :], src.rearrange("p (o q) -> p o q", o=1).to_broadcast([BR, 4, 128]),
                   -1.0, ALU.add, [pMb], [nbT4b], s2=30000.0, op1=ALU.mult)

            def sel_bias(i, S, Sb):
                mm(kb, S, F_[:, i % 64, :], nbT4[:, i // 64, :, :].rearrange("p r q -> p (r q)"), False, True,
                   [F_b, nbT4b], [Sb])

            def sel_mask(i, c, m=m):
                if i < 4 * m:
                    return None
                mt, mtb = mk[c % 2]
                ts(kb, mt, tqs, kpos[:, i:i + 1], ALU.is_ge, [tqsb, kposb], [mtb])
                return mt, mtb
            attend(g, list(range(4 * m + 4)), ksT, ksTb, vsA, vsAb, 1, False, mask_fn=sel_mask, bias_fn=sel_bias)

            def win_mask(i, c):
                mt, mtb = mk[c % 2]
                m2, m2b = mk2[c % 2]
                ts(kb, m2, tqs, kpos[:, i:i + 1], ALU.subtract, [tqsb, kposb], [m2b])
                ts(kb, mt, m2, 0.0, ALU.is_ge, [m2b], [mtb])
                stt(kb, mt, m2, 512.0, mt, ALU.is_lt, ALU.mult, [m2b, mtb], [mtb])
                return mt, mtb
            attend(g, list(range(max(0, 4 * m - 4), 4 * m + 4)), kwT, kwTb, vwA, vwAb, 2, False, mask_fn=win_mask)
        for t4 in range(4):
            transp(kb, pQ[:, t4 * 128:(t4 + 1) * 128], oacc[:, t4 * 128:(t4 + 1) * 128], identf, [oaccb, identfb],
                   [pQb])
        actf(kb, OT, pQ, AF.Copy, [pQb], [OTb])
        kb.dma("sp", oTv[:, :, m * 128:(m + 1) * 128], OT.rearrange("p (c q) -> p c q", c=4), reads=[OTb],
               writes=[oTb])
    kb.finish([oTb])
    return nc


def nsa_inputs(P, l, j, hT_full, hT_q, seq=None):
    seq = seq or SEQ
    ns = seq // 4 // 128
    o = 512
    qcols = np.array([o + (g * 4 + r) * 64 + d for r in range(4) for g in range(2) for d in range(64)])
    w1 = np.stack([P["cmp_w1_k"][l].reshape(32, 64, 64), P["cmp_w1_v"][l].reshape(32, 64, 64)], 0)
    w1 = np.ascontiguousarray(np.tile(w1.transpose(2, 0, 1, 3), (2, 1, 1, 1)))
    w2 = np.stack([P["cmp_w2_k"][l], P["cmp_w2_v"][l]], 1)
    w2 = np.ascontiguousarray(np.tile(w2, (2, 1, 1)))
    pe = np.stack([P["cmp_pe_k"][l].T, P["cmp_pe_v"][l].T], 1)
    pe = np.ascontiguousarray(np.tile(pe, (2, 1, 1)))
    tq = np.concatenate([np.arange(128) + 128 * (4 * m + j) for m in range(ns)]).astype(np.float32)
    cur = np.stack([(np.arange(128) + 128 * (4 * m + j)) // 64 for m in range(ns)], 1).astype(np.float32)
    return {"hT": hT_full, "hTq": hT_q,
            "w_q": np.ascontiguousarray(P["w_in"][l][:, qcols]),
            "w_kv": np.ascontiguousarray(P["w_in"][l][:, o + 512:o + 512 + 768]),
            "w_gt": np.ascontiguousarray(P["w_in"][l][:, o + 1280:o + 1304]),
            "w1": w1.astype(np.float32), "w2": w2.astype(np.float32), "peT": pe.astype(np.float32),
            "tq_bc": np.ascontiguousarray(np.broadcast_to(tq[None, :], (128, seq // 4))),
            "curcol": np.ascontiguousarray(cur)}


_PROGS = {}


def _prog(name, fn):
    if name not in _PROGS:
        _PROGS[name] = fn()
    return _PROGS[name]


def _run(nc, maps):
    return run_bass_kernel_spmd(nc, maps, core_ids=list(range(NCORES))).results


def kernel(**P):
    P = {k: np.asarray(v) for k, v in P.items()}
    x = P["x"]
    xT = [np.ascontiguousarray(x[c // 4, (c % 4) * TQ:(c % 4 + 1) * TQ].T) for c in range(NCORES)]
    res = _run(_prog("a", build_stage_a), [{"xT": xT[c], "g": _cols(P["norm_mix"][0], 8)} for c in range(NCORES)])
    hT = [r["hT"] for r in res]
    out = None
    for l in range(DEPTH):
        hfull = [np.ascontiguousarray(np.concatenate(hT[b * 4:(b + 1) * 4], axis=1)) for b in range(NB)]
        maps = []
        for c in range(NCORES):
            b, j = c // 4, c % 4
            hh = np.zeros((D, HALO + TQ), np.float32)
            hh[:, HALO:] = hT[c]
            if j > 0:
                hh[:, :HALO] = hT[c - 1][:, TQ - HALO:]
            maps.append(conv_inputs(P, l, hh))
        rconv = _run(_prog("conv", build_stage_conv), maps)
        rrw = _run(_prog("rwkv", build_stage_rwkv),
                   [rwkv_inputs(P, l, c % 4, hfull[c // 4]) for c in range(NCORES)])
        maps = []
        for c in range(NCORES):
            b, j = c // 4, c % 4
            hq = hfull[b].reshape(D, NSLOT, 4, 128)[:, :, j, :].reshape(D, TQ)
            maps.append(nsa_inputs(P, l, j, hfull[b], np.ascontiguousarray(hq)))
        rnsa = _run(_prog("nsa", build_stage_nsa), maps)
        mixfull = []
        for b in range(NB):
            mf = np.empty((D, SEQ), np.float32)
            for j in range(4):
                c = b * 4 + j
                mf[0:256, j * TQ:(j + 1) * TQ] = rconv[c]["convT"]
                mf[256:768].reshape(512, NSLOT, 4, 128)[:, :, j, :] = rnsa[c]["nsaT"].reshape(512, NSLOT, 128)
                mf[768 + j * 64:768 + (j + 1) * 64, :] = rrw[c]["yT"]
            mixfull.append(mf)
        maps = []
        for c in range(NCORES):
            b, j = c // 4, c % 4
            mixT = np.ascontiguousarray(mixfull[b][:, j * TQ:(j + 1) * TQ])
            if j > 0:
                xh = np.ascontiguousarray(xT[c - 1][:, TQ - 2:])
                mh = np.ascontiguousarray(mixfull[b][:, j * TQ - 2:j * TQ])
            else:
                xh = np.zeros((D, 2), np.float32)
                mh = np.zeros((D, 2), np.float32)
            gnext = P["norm_mix"][l + 1] if l + 1 < DEPTH else P["norm_final"]
            maps.append({"xT": xT[c], "mixT": mixT, "xh": xh, "mixh": mh,
                         "w_out": np.ascontiguousarray(P["w_out"][l]), "w_gate": np.ascontiguousarray(P["ffn_w_gate"][l]),
                         "w_up": np.ascontiguousarray(P["ffn_w_up"][l]), "w_down": np.ascontiguousarray(P["ffn_w_down"][l]),
                         "g_ffn": _cols(P["norm_ffn"][l], 8), "g_next": _cols(gnext, 8),
                         "conv_w": np.ascontiguousarray(P["ffn_conv_w"][l].reshape(3, DFF // 128, 128).transpose(2, 1, 0)),
                         "conv_b": _cols(P["ffn_conv_b"][l], DFF // 128)})
        rc = _run(_prog("c", build_stage_c), maps)
        xT = [r["xoT"] for r in rc]
        hT = [r["hoT"] for r in rc]
    out = np.empty((NB, SEQ, D), np.float32)
    for c in range(NCORES):
        out[c // 4, (c % 4) * TQ:(c % 4 + 1) * TQ] = hT[c].T
    return out
```
